# Optimizing a Trainium2 kernel written in Bass

```python
import jax, jax.numpy as jnp
from jax import lax
import numpy as np

D_MODEL = 2048
BATCH = 2
SEQ = 4096
DEPTH = 2

N_A_LAYERS = DEPTH // 2
N_B_LAYERS = DEPTH - N_A_LAYERS

CHUNK = 128
A_EXPAND = 2
A_WIDTH = A_EXPAND * D_MODEL
A_GROUPS = 8
A_GROUP_DIM = A_WIDTH // A_GROUPS

B_HEADS = 16
B_HEAD_DIM = D_MODEL // B_HEADS
B_WIDTH = B_HEADS * B_HEAD_DIM
Q_BLOCK = 128

DN_ALPHA = (2.0 * DEPTH) ** 0.25
DN_BETA = (8.0 * DEPTH) ** -0.25
LN_EPS = 1e-5

kernel_name = "yoco_gmlp_stickbreaking_deepnorm"


def layer_norm(x, g, b):
    xf = x.astype(jnp.float32)
    mu = jnp.mean(xf, axis=-1, keepdims=True)
    xc = xf - mu
    var = jnp.mean(xc * xc, axis=-1, keepdims=True)
    y = xc * lax.rsqrt(var + LN_EPS) * g.astype(jnp.float32) + b.astype(jnp.float32)
    return y.astype(x.dtype)


def gmlp_mixer(x, w_in, b_in, vln_g, vln_b, w_s, b_s, w_out):
    bsz, seq, _ = x.shape
    h = x @ w_in + b_in
    u, v, g = jnp.split(h, 3, axis=-1)
    u = jax.nn.gelu(u)
    v = layer_norm(jax.nn.gelu(v), vln_g, vln_b)
    n_chunks = seq // CHUNK
    v = v.reshape(bsz, n_chunks, CHUNK, A_GROUPS, A_GROUP_DIM)
    causal = jnp.tril(jnp.ones((CHUNK, CHUNK), dtype=bool))
    w = jnp.where(causal[None], w_s, jnp.zeros_like(w_s))
    mixed = jnp.einsum('gts,bcsge->bctge', w, v) + b_s.T[None, None, :, :, None]
    s = u * mixed.reshape(bsz, seq, A_WIDTH)
    return (s * jax.nn.silu(g)) @ w_out


def stick_breaking_attention(q, k, v):
    seq = q.shape[2]
    scale = B_HEAD_DIM ** -0.5
    outs = []
    for blk in range(seq // Q_BLOCK):
        q0 = blk * Q_BLOCK
        kend = q0 + Q_BLOCK
        qb = q[:, :, q0:kend].astype(jnp.float32)
        kb = k[:, :, :kend].astype(jnp.float32)
        vb = v[:, :, :kend].astype(jnp.float32)
        z = jnp.einsum('bhtd,bhsd->bhts', qb, kb) * scale
        t_idx = q0 + jnp.arange(Q_BLOCK)[:, None]
        s_idx = jnp.arange(kend)[None, :]
        past = s_idx < t_idx
        log_keep = jnp.where(past, -jax.nn.softplus(z), 0.0)
        after = lax.cumsum(log_keep, axis=log_keep.ndim - 1, reverse=True) - log_keep
        log_w = jax.nn.log_sigmoid(z) + after
        wts = jnp.where(past, jnp.exp(log_w), 0.0)
        outs.append(jnp.einsum('bhts,bhsd->bhtd', wts, vb))
    return jnp.concatenate(outs, axis=2)


def stick_breaking_mixer(x, k, v, w_in, w_out):
    bsz, seq, _ = x.shape
    h = x @ w_in
    q, g = jnp.split(h, 2, axis=-1)
    q = q.reshape(bsz, seq, B_HEADS, B_HEAD_DIM).transpose(0, 2, 1, 3)
    o = stick_breaking_attention(q, k, v).astype(x.dtype)
    o = o.transpose(0, 2, 1, 3).reshape(bsz, seq, B_WIDTH)
    return (o * jax.nn.silu(g)) @ w_out


def setup_inputs(seed: int = 0) -> dict:
    key = jax.random.key(seed)
    ks = jax.random.split(key, 16)
    f32 = jnp.float32
    x = jax.random.normal(ks[0], (BATCH, SEQ, D_MODEL), f32)
    a_w_in = jax.random.normal(ks[1], (N_A_LAYERS, D_MODEL, 3 * A_WIDTH), f32) * D_MODEL ** -0.5
    a_b_in = 0.02 * jax.random.normal(ks[2], (N_A_LAYERS, 3 * A_WIDTH), f32)
    a_vln_g = 1.0 + 0.02 * jax.random.normal(ks[3], (N_A_LAYERS, A_WIDTH), f32)
    a_vln_b = 0.02 * jax.random.normal(ks[4], (N_A_LAYERS, A_WIDTH), f32)
    a_w_s = jax.random.normal(ks[5], (N_A_LAYERS, A_GROUPS, CHUNK, CHUNK), f32) * CHUNK ** -0.5
    a_b_s = 1.0 + 0.02 * jax.random.normal(ks[6], (N_A_LAYERS, A_GROUPS, CHUNK), f32)
    a_w_out = jax.random.normal(ks[7], (N_A_LAYERS, A_WIDTH, D_MODEL), f32) * (A_WIDTH ** -0.5 * DN_BETA)
    kv_w = jax.random.normal(ks[8], (D_MODEL, 2 * B_WIDTH), f32) * D_MODEL ** -0.5
    b_w_in = jax.random.normal(ks[9], (N_B_LAYERS, D_MODEL, 2 * B_WIDTH), f32) * D_MODEL ** -0.5
    b_w_out = jax.random.normal(ks[10], (N_B_LAYERS, B_WIDTH, D_MODEL), f32) * (B_WIDTH ** -0.5 * DN_BETA)
    ln_g = 1.0 + 0.02 * jax.random.normal(ks[11], (DEPTH, D_MODEL), f32)
    ln_b = 0.02 * jax.random.normal(ks[12], (DEPTH, D_MODEL), f32)
    return {"x": x, "a_w_in": a_w_in, "a_b_in": a_b_in, "a_vln_g": a_vln_g, "a_vln_b": a_vln_b,
            "a_w_s": a_w_s, "a_b_s": a_b_s, "a_w_out": a_w_out, "kv_w": kv_w,
            "b_w_in": b_w_in, "b_w_out": b_w_out, "ln_g": ln_g, "ln_b": ln_b}


def reference(x, a_w_in, a_b_in, a_vln_g, a_vln_b, a_w_s, a_b_s, a_w_out, kv_w,
              b_w_in, b_w_out, ln_g, ln_b):
    bsz, seq, _ = x.shape
    k = None
    v = None
    for layer in range(DEPTH):
        if layer < N_A_LAYERS:
            i = layer
            y = gmlp_mixer(x, a_w_in[i], a_b_in[i], a_vln_g[i], a_vln_b[i],
                           a_w_s[i], a_b_s[i], a_w_out[i])
        else:
            if layer == N_A_LAYERS:
                kv = x @ kv_w
                k, v = jnp.split(kv, 2, axis=-1)
                k = k.reshape(bsz, seq, B_HEADS, B_HEAD_DIM).transpose(0, 2, 1, 3)
                v = v.reshape(bsz, seq, B_HEADS, B_HEAD_DIM).transpose(0, 2, 1, 3)
            j = layer - N_A_LAYERS
            y = stick_breaking_mixer(x, k, v, b_w_in[j], b_w_out[j])
        x = layer_norm(DN_ALPHA * x + y, ln_g[layer], ln_b[layer])
    return x
```

```python
from contextlib import ExitStack

import ml_dtypes
import numpy as np

import concourse.bass as bass
import concourse.mybir as mybir
from concourse.bass_utils import run_bass_kernel_spmd

F32 = mybir.dt.float32
BF16 = mybir.dt.bfloat16
AF = mybir.ActivationFunctionType
ALU = mybir.AluOpType

ALPHA = 4.0 ** 0.25
EPS = 1e-5
ENGS = ("pe", "act", "dve", "pool", "sp")


class Buf:
    __slots__ = ("name", "writers", "readers")

    def __init__(self, name):
        self.name = name
        self.writers = []
        self.readers = []


class Op:
    __slots__ = ("eng", "fn", "deps", "inc", "dma", "dma_val", "val", "step")

    def __init__(self, eng, fn, dma):
        self.eng = eng
        self.fn = fn
        self.deps = []
        self.inc = False
        self.dma = dma
        self.dma_val = 0
        self.val = 0


def alias(new_bufs, old_bufs):
    olds = []
    for b in old_bufs:
        for o in b.readers + b.writers:
            if o not in olds:
                olds.append(o)
    for nb in new_bufs:
        for o in olds:
            if o not in nb.readers:
                nb.readers.append(o)


class Prog:
    def __init__(self):
        self.ops = {e: [] for e in ENGS}
        self.dma_counts = {}
        self.last_dma = {}
        self.bar = []

    def barrier(self, skip_prefix="cc"):
        deps = []
        for e in ENGS:
            for op in reversed(self.ops[e]):
                if op.dma is None and op.fn is not None:
                    deps.append(op)
                    break
        for ch, op in self.last_dma.items():
            if not ch.startswith(skip_prefix):
                deps.append(op)
        self.bar = deps

    def add(self, eng, fn, reads=(), writes=(), war=(), dma=None, after=(), step=16):
        op = Op(eng, fn, dma)
        op.step = step
        deps = {}
        for b in reads:
            for w in b.writers:
                deps[w] = "raw"
        for b in list(writes) + list(war):
            for w in b.writers:
                deps.setdefault(w, "waw")
            for r in b.readers:
                deps.setdefault(r, "war")
        for a in list(after) + self.bar:
            if a is not None:
                deps[a] = "raw"
        for d, kind in deps.items():
            if d is op:
                continue
            if d.dma is not None:
                op.deps.append(d)
            elif d.eng == eng and dma is None and (kind == "war" or (kind == "waw" and eng == "pe")):
                continue
            else:
                d.inc = True
                op.deps.append(d)
        for b in reads:
            if dma is None:
                b.readers = [r for r in b.readers if not (r.eng == eng and r.dma is None)]
            b.readers.append(op)
        for b in writes:
            if b.readers:
                b.writers = [op]
                b.readers = []
            else:
                if dma is None:
                    b.writers = [w for w in b.writers if not (w.eng == eng and w.dma is None)]
                b.writers.append(op)
        if dma is not None:
            c = self.dma_counts.get(dma, 0) + step
            self.dma_counts[dma] = c
            op.dma_val = c
            self.last_dma[dma] = op
        self.ops[eng].append(op)
        return op

    def emit(self, nc, st):
        esem = {e: st.enter_context(nc.semaphore("s_" + e)) for e in ENGS}
        dsem = {c: st.enter_context(nc.semaphore("d_" + c)) for c in self.dma_counts}
        for e in ENGS:
            v = 0
            for op in self.ops[e]:
                if op.inc and op.dma is None:
                    v += 1
                    op.val = v
        block = st.enter_context(nc.Block())

        def run(e, eng):
            waited = {}
            for op in self.ops[e]:
                need = {}
                for d in op.deps:
                    if d.dma is not None:
                        key, val, sem = ("d", d.dma), d.dma_val, dsem[d.dma]
                    else:
                        key, val, sem = ("e", d.eng), d.val, esem[d.eng]
                    if val > need.get(key, (0, None))[0]:
                        need[key] = (val, sem)
                for key, (val, sem) in need.items():
                    if waited.get(key, 0) >= val:
                        continue
                    eng.wait_ge(sem, val)
                    waited[key] = val
                if op.fn is None:
                    continue
                ins = op.fn(eng)
                if op.dma is not None:
                    ins.then_inc(dsem[op.dma], op.step)
                elif op.inc:
                    ins.then_inc(esem[e], 1)

        @block.tensor
        def _(eng):
            run("pe", eng)

        @block.scalar
        def _(eng):
            run("act", eng)

        @block.vector
        def _(eng):
            run("dve", eng)

        @block.gpsimd
        def _(eng):
            run("pool", eng)

        @block.sync
        def _(eng):
            run("sp", eng)


class Rot:
    def __init__(self, n):
        self.n = n
        self.i = -1

    def __call__(self):
        self.i = (self.i + 1) % self.n
        return self.i


def f_mm(out, lhsT, rhs, start, stop):
    return lambda e: e.matmul(out, lhsT=lhsT, rhs=rhs, start=start, stop=stop, skip_group_check=True)


def f_tr(out, in_, ident):
    return lambda e: e.transpose(out, in_, ident)


def f_act(out, in_, func, bias=None, scale=1.0):
    if bias is None:
        return lambda e: e.activation(out=out, in_=in_, func=func, scale=scale)
    return lambda e: e.activation(out=out, in_=in_, func=func, bias=bias, scale=scale)


def f_copy(out, in_):
    return lambda e: e.tensor_copy(out=out, in_=in_)


def f_dma(out, in_):
    return lambda e: e.dma_start(out=out, in_=in_)


def f_tt(out, in0, in1, op):
    return lambda e: e.tensor_tensor(out=out, in0=in0, in1=in1, op=op)


def f_ts(out, in0, s1, s2, op0, op1):
    return lambda e: e.tensor_scalar(out=out, in0=in0, scalar1=s1, scalar2=s2, op0=op0, op1=op1)


def f_stt(out, in0, scalar, in1, op0, op1):
    return lambda e: e.scalar_tensor_tensor(out=out, in0=in0, scalar=scalar, in1=in1, op0=op0, op1=op1)


def f_memset(ap, v):
    return lambda e: e.memset(ap, v)


def f_asel(out, in_, cmp, fill, pattern, cm, base=0):
    return lambda e: e.affine_select(out=out, in_=in_, compare_op=cmp, fill=fill, base=base,
                                     pattern=pattern, channel_multiplier=cm)


class Ctx:
    def __init__(self, arena_bytes=184 * 1024):
        self.nc = bass.Bass("TRN2", target_bir_lowering=False)
        self.P = Prog()
        self.st = ExitStack()
        self.arena = self.st.enter_context(self.nc.sbuf_tensor("arena", [128, arena_bytes // 4], F32))
        self.nsmall = 0
        self.prefix = ""
        self.PT = None
        self.io = {}

    def view(self, off, nbytes, dt, pat=None, **kw):
        a = self.arena[:, off // 4:(off + nbytes) // 4]
        if dt is BF16:
            a = a.bitcast(BF16)
        if pat is not None:
            a = a.rearrange(pat, **kw)
        return a

    def sb(self, shape, dt, name=None):
        self.nsmall += 1
        return self.st.enter_context(self.nc.sbuf_tensor(self.prefix + (name or f"sm{self.nsmall}"), shape, dt))

    def psum4(self):
        if self.PT is None:
            self.PT = [self.ps([128, 1024], f"pt{i}") for i in range(4)]
        return self.PT

    def ps(self, shape, name):
        return self.st.enter_context(self.nc.psum_tensor(name, shape, F32))

    def din(self, name, shape, dt=F32):
        return self.nc.dram_tensor(name, shape, dt, kind="ExternalInput").ap()

    def dout(self, name, shape, dt=F32):
        return self.nc.dram_tensor(name, shape, dt, kind="ExternalOutput").ap()


K = 1024


def emit_consts(C):
    P = C.P
    if "ident" in C.io:
        return C.io["ident"]
    ident = C.sb([128, 128], F32, "ident")
    Bident = Buf("ident")
    P.add("pool", f_memset(ident[:], 1.0), writes=[Bident])
    P.add("pool", f_asel(ident[:], ident[:], ALU.is_equal, 0.0, [[-1, 128]], 1), reads=[Bident], writes=[Bident])
    C.io["ident"] = (ident, Bident)
    return ident, Bident


def emit_load_T(C, src, xs, Bxs, xsch, dstT, BdstT, ident, Bident, banks, Bbanks, rot):
    P = C.P
    for tb in range(8):
        P.add("sp", f_dma(xs[:], src[tb * 128:(tb + 1) * 128, :]), writes=[Bxs], dma=xsch)
        emit_transpose_block(C, xs, Bxs, dstT, BdstT[tb], tb, ident, Bident, banks, Bbanks, rot)


def emit_transpose_block(C, xs, Bxs, dstT, Bdst, tb, ident, Bident, banks, Bbanks, rot):
    P = C.P
    for q in range(4):
        bi = rot()
        bank = banks[bi]
        for jj in range(4):
            kc = q * 4 + jj
            P.add("pe", f_tr(bank[:, jj * 128:(jj + 1) * 128], xs[:, kc * 128:(kc + 1) * 128], ident[:]),
                  reads=[Bxs, Bident], writes=[Bbanks[bi]] if jj == 3 else [], war=[Bbanks[bi]] if jj == 0 else [])
        dst = dstT[:, q * 4:(q + 1) * 4, tb * 128:(tb + 1) * 128]
        srcv = bank[:, :].rearrange("p (a b) -> p a b", a=4)
        if q % 2 == 0:
            P.add("act", f_act(dst, srcv, AF.Copy), reads=[Bbanks[bi]], writes=[Bdst])
        else:
            P.add("dve", f_copy(dst, srcv), reads=[Bbanks[bi]], writes=[Bdst])


def emit_ln_load(C, xs, Bxs, x_src, tb):
    C.P.add("sp", f_dma(xs[:], x_src[tb * 128:(tb + 1) * 128, :]), reads=C.io.get("Bx1d_rd", {}).get(tb, []),
            writes=[Bxs], dma="xs")


def emit_ln_block(C, zt, Bz, xs, Bxs, x_src, tb, lng, lnb, Bln, ost, Bost, small, Bsm, epsc, Beps, och, out_dst):
    P = C.P
    stats, mv, rstd, nb = small
    if tb == 0:
        emit_ln_load(C, xs, Bxs, x_src, 0)
    P.add("dve", f_stt(zt, xs[:], ALPHA, zt, ALU.mult, ALU.add), reads=[Bxs, Bz], writes=[Bz])
    if tb + 1 < 8:
        emit_ln_load(C, xs, Bxs, x_src, tb + 1)
    for c in range(4):
        P.add("dve", lambda e, c=c: e.bn_stats(out=stats[:, c, :], in_=zt[:, c * 512:(c + 1) * 512]),
              reads=[Bz], writes=[Bsm])
    P.add("dve", lambda e: e.bn_aggr(out=mv[:], in_=stats[:].rearrange("p a b -> p (a b)")), reads=[Bsm], writes=[Bsm])
    P.add("act", f_act(rstd[:], mv[:, 1:2], AF.Sqrt, bias=epsc[:], scale=1.0), reads=[Bsm, Beps], writes=[Bsm])
    P.add("dve", lambda e: e.reciprocal(out=rstd[:], in_=rstd[:]), reads=[Bsm], writes=[Bsm])
    P.add("dve", f_stt(nb[:], mv[:, 0:1], -1.0, rstd[:], ALU.mult, ALU.mult), reads=[Bsm], writes=[Bsm])
    P.add("act", f_act(zt, zt, AF.Identity, bias=nb[:], scale=rstd[:]), reads=[Bz, Bsm], writes=[Bz])
    P.add("dve", f_tt(ost, zt, lng, ALU.mult), reads=[Bz, Bln], writes=[Bost])
    P.add("pool", f_tt(ost, ost, lnb, ALU.add), reads=[Bost, Bln], writes=[Bost])
    return P.add("sp", f_dma(out_dst[tb * 128:(tb + 1) * 128, :], ost), reads=[Bost], dma=och)


def load_wtile(C, dst, Bdst, ch, src2d, nk, split=4):
    P = C.P
    step = nk // split
    for s in range(split):
        P.add("pool", f_dma(dst[:, s * step:(s + 1) * step, :],
                            src2d[s * step * 128:(s + 1) * step * 128, :].rearrange("(k p) c -> p k c", p=128)),
              writes=[Bdst], dma=ch)


class WStream:
    def __init__(self, C, wsl, Bw, tiles):
        self.C, self.wsl, self.Bw, self.tiles = C, wsl, Bw, tiles
        self.issued = 0

    def use(self, i):
        while self.issued < min(i + 3, len(self.tiles)):
            j = self.issued
            src2d, nk = self.tiles[j]
            sl = j % 3
            dst = self.wsl[sl].rearrange("p (k c) -> p k c", k=nk)
            load_wtile(self.C, dst, self.Bw[sl], f"w{sl}", src2d, nk)
            self.issued += 1
        sl = i % 3
        nk = self.tiles[i][1]
        return self.wsl[sl].rearrange("p (k c) -> p k c", k=nk), self.Bw[sl]


_uid = [0]


def uch(prefix="c"):
    _uid[0] += 1
    return f"{prefix}{_uid[0]}"


class StopBuild(Exception):
    pass


def build_A(upto=None):
    C = Ctx()
    try:
        _build_A_body(C, upto)
    except StopBuild:
        pass
    C.P.emit(C.nc, C.st)
    C.st.close()
    return C.nc


def _build_A_body(C, upto):
    def gate(name):
        if upto == name:
            raise StopBuild()

    nc, P = C.nc, C.P
    io = C.io
    x = C.din("x", [1024, 2048])
    w_in = C.din("a_w_in", [2048, 12288])
    b_in = C.din("a_b_in", [1, 12288])
    vln_g = C.din("a_vln_g", [1, 4096])
    vln_b = C.din("a_vln_b", [1, 4096])
    w_s = C.din("a_w_s", [8, 128, 128])
    b_s = C.din("a_b_s", [1, 1024])
    w_out = C.din("a_w_out", [4096, 2048])
    kv_w = C.din("kv_w", [2048, 4096])
    if "fused" in io:
        ln_g, ln_b = io["ln_g"][0:1, :], io["ln_b"][0:1, :]
        x1_o = io["x1_scratch"]
    else:
        ln_g = C.din("ln_g", [1, 2048])
        ln_b = C.din("ln_b", [1, 2048])
        x1_o = C.dout("x1", [1024, 2048])
        kT_o = C.dout("kT", [16, 128, 1024], BF16)
        v_o = C.dout("v", [1024, 2048], BF16)

    xT = C.view(0, 32 * K, BF16, "p (k t) -> p k t", k=16)
    uT = [C.view(32 * K + i * 2 * K, 2 * K, BF16) for i in range(8)]
    sgT = [C.view(48 * K + i * 2 * K, 2 * K, BF16) for i in range(8)]
    z = C.view(0, 64 * K, F32, "p (a d) -> p a d", a=8)
    slab = C.view(64 * K, 64 * K, BF16, "p (f t c) -> p f t c", f=32, t=8)
    x1T = C.view(64 * K, 32 * K, BF16, "p (k t) -> p k t", k=16)
    lng = C.view(96 * K, 8 * K, F32)
    lnb = C.view(104 * K, 8 * K, F32)
    ost = [C.view(112 * K + i * 8 * K, 8 * K, F32) for i in range(2)]
    wsl = [C.view(128 * K + i * 16 * K, 16 * K, BF16) for i in range(3)]
    xs = C.view(176 * K, 8 * K, F32)
    tmp = [C.view(176 * K + i * 2 * K, 2 * K, F32) for i in range(4)]
    kst = [C.view(176 * K + i * 2 * K, 2 * K, BF16) for i in range(4)]

    Bxs = Buf("xs")
    Btmp = [Buf(f"tmp{i}") for i in range(4)]
    Bkst = [Buf(f"kst{i}") for i in range(4)]
    BxT = [Buf(f"xT{t}") for t in range(8)]
    Bu = [Buf(f"u{i}") for i in range(8)]
    Bsg = [Buf(f"sg{i}") for i in range(8)]
    Bz = [Buf(f"z{t}") for t in range(8)]
    Bslab = [[Buf(f"slab{f}_{h}") for h in range(2)] for f in range(32)]
    Bx1T = [Buf(f"x1T{t}") for t in range(8)]
    Bln = Buf("ln")
    Bost = [Buf("ost0"), Buf("ost1")]
    Bw = [Buf(f"w{i}") for i in range(3)]
    tiles = ([(w_in[:, 4096 + cb * 512:4096 + (cb + 1) * 512], 16) for cb in range(8)]
             + [t for g in range(8) for t in ((w_in[:, g * 512:(g + 1) * 512], 16),
                                              (w_in[:, 8192 + g * 512:8192 + (g + 1) * 512], 16))]
             + [(w_out[:, cb * 256:(cb + 1) * 256], 32) for cb in range(8)]
             + [(kv_w[:, ct * 512:(ct + 1) * 512], 16) for ct in range(8)])
    tiles = tiles + io.get("extra_tiles", [])
    WS = WStream(C, wsl, Bw, tiles)

    PT = C.psum4()
    banks = [PT[i // 2][:, (i % 2) * 512:(i % 2 + 1) * 512] for i in range(8)]
    Bbanks = [Buf(f"bank{i}") for i in range(8)]
    brot = Rot(8)

    ident, Bident = emit_consts(C)

    brow_u = C.sb([32, 128], F32, "brow_u")
    brow_g = C.sb([32, 128], F32, "brow_g")
    grow = C.sb([32, 128], F32, "grow")
    vbrow = C.sb([32, 128], F32, "vbrow")
    bs8 = C.sb([8, 128], F32, "bs8")
    bcol = C.sb([128, 64], F32, "bcol")
    gcol = C.sb([128, 32], F32, "gcol")
    CG = C.sb([128, 32, 2], F32, "CG")
    RB = C.sb([128, 8, 2], F32, "RB")
    Rab = [C.sb([2, 128], F32, f"Rab{i}") for i in range(4)]
    Rg = [C.sb([2, 128], F32, f"Rg{i}") for i in range(2)]
    Rd = C.sb([1, 640], F32, "Rd")
    Wsf = C.view(160 * K, 4 * K, F32, "p (g s) -> p g s", g=8)
    WsTf = C.view(164 * K, 4 * K, F32, "p (g s) -> p g s", g=8)
    WsT = C.sb([128, 8, 128], BF16, "WsT")
    epsc = C.sb([128, 1], F32, "epsc")
    stats = C.sb([128, 8, 8, 6], F32, "stats")
    mv = C.sb([128, 8, 2], F32, "mv")
    rstd = C.sb([128, 8], F32, "rstd")
    lstats = C.sb([128, 4, 6], F32, "lstats")
    lmv = C.sb([128, 2], F32, "lmv")
    lrstd = C.sb([128, 1], F32, "lrstd")
    lnbias = C.sb([128, 1], F32, "lnbias")
    lsets = [(lstats, lmv, lrstd, lnbias),
             (C.sb([128, 4, 6], F32, "lstats2"), C.sb([128, 2], F32, "lmv2"), C.sb([128, 1], F32, "lrstd2"),
              C.sb([128, 1], F32, "lnbias2"))]
    Blsets = [Buf("lsm0"), Buf("lsm1")]
    Bbrow, Bgrow, Bbcol, Bgcol = Buf("brow"), Buf("grow"), Buf("bcol"), Buf("gcol")
    BCG, BRB, BRab, BRg = Buf("CG"), Buf("RB"), [Buf(f"Rab{i}") for i in range(4)], [Buf("Rg0"), Buf("Rg1")]
    Bones, Bbv = Buf("ones"), Buf("bv")
    BWsf, BWsTf, BWsT, Beps = Bw[2], Bw[2], Buf("WsT"), Buf("eps")
    Bstats = [Buf(f"stats{t}") for t in range(8)]
    Bmv, Brstd, Blsm = Buf("mv"), Buf("rstd"), Buf("lsm")

    P.add("pool", f_memset(epsc[:], EPS), writes=[Beps])
    P.add("pool", f_memset(Rd[0:1, 512:640], 1.0), writes=[Bones])
    b_in96 = b_in.rearrange("o (f p) -> (o f) p", p=128)
    P.add("sp", f_dma(brow_u[:], b_in96[0:32, :]), writes=[Bbrow], dma=uch())
    P.add("sp", f_dma(brow_g[:], b_in96[64:96, :]), writes=[Bbrow], dma=uch())
    P.add("sp", f_dma(grow[:], vln_g.rearrange("o (f p) -> (o f) p", p=128)), writes=[Bgrow], dma=uch())
    P.add("sp", f_dma(vbrow[:], vln_b.rearrange("o (f p) -> (o f) p", p=128)), writes=[Bgrow], dma=uch())
    P.add("sp", f_dma(bs8[:], b_s.rearrange("o (g t) -> (o g) t", t=128)), writes=[Bgrow], dma=uch())
    P.add("sp", f_dma(Wsf[:], w_s.rearrange("g t s -> t g s")), writes=[BWsf], dma=uch())
    gate('c1')
    for (src_, nrow, dst_, Bsrc, Bd) in ((brow_u, 32, bcol[:, 0:32], Bbrow, Bbcol), (brow_g, 32, bcol[:, 32:64], Bbrow, Bbcol),
                                       (grow, 32, gcol[:, :], Bgrow, Bgcol), (vbrow, 32, CG[:, :, 0], Bgrow, BCG),
                                       (bs8, 8, RB[:, :, 1], Bgrow, BRB)):
        bi = brot()
        P.add("pe", f_tr(banks[bi][:, 0:nrow], src_[:, :], ident[0:nrow, 0:nrow]), reads=[Bsrc, Bident], writes=[Bbanks[bi]])
        P.add("dve", f_copy(dst_, banks[bi][:, 0:nrow]), reads=[Bbanks[bi]], writes=[Bd])
    P.add("dve", lambda e: e.reciprocal(out=CG[:, :, 1], in_=gcol[:, :]), reads=[Bgcol], writes=[BCG])
    P.add("dve", f_tt(CG[:, :, 0], CG[:, :, 0], CG[:, :, 1], ALU.mult), reads=[BCG], writes=[BCG])
    gate('c2')
    for g in range(8):
        P.add("pool", f_asel(Wsf[:, g, :], Wsf[:, g, :], ALU.is_ge, 0.0, [[-1, 128]], 1), reads=[BWsf], writes=[BWsf])
    P.add("dve", lambda e: e.tensor_reduce(out=RB[:, :, 0], in_=Wsf[:, :, :], axis=mybir.AxisListType.X, op=ALU.add),
          reads=[BWsf], writes=[BRB])
    gate('c3')
    for g in range(8):
        bi = brot()
        P.add("pe", f_tr(banks[bi][:, 0:128], Wsf[:, g, :], ident[:]), reads=[BWsf, Bident], writes=[Bbanks[bi]])
        P.add("dve", f_copy(WsT[:, g, :], banks[bi][:, 0:128]), reads=[Bbanks[bi]], writes=[BWsT])
    gate('c4')

    gate('consts')
    emit_load_T(C, x, xs, Bxs, "xs", xT, BxT, ident, Bident, banks, Bbanks, brot)

    gate('A0')
    alias(Btmp, [Bxs])
    it = 0
    for cb in range(8):
        wt, Bwt = WS.use(cb)
        P.add("sp", f_dma(Rd[0:1, 0:512], b_in[:, 4096 + cb * 512:4096 + (cb + 1) * 512]), writes=[Bbv], dma="bv")
        for tb in range(8):
            bi = brot()
            bank = banks[bi]
            for kc in range(16):
                P.add("pe", f_mm(bank[:, :], xT[:, kc, tb * 128:(tb + 1) * 128], wt[:, kc, :], kc == 0, False),
                      reads=[BxT[tb], Bwt], war=[Bbanks[bi]] if kc == 0 else [])
            P.add("pe", f_mm(bank[:, :], Rd[0:1, 512:640], Rd[0:1, 0:512], False, True),
                  reads=[Bones, Bbv], writes=[Bbanks[bi]])
            ti = it % 4
            it += 1
            P.add("act", f_act(tmp[ti], bank[:, :], AF.Gelu_apprx_tanh), reads=[Bbanks[bi]], writes=[Btmp[ti]])
            P.add("dve", lambda e, tb=tb, cb=cb, ti=ti: e.bn_stats(out=stats[:, tb, cb, :], in_=tmp[ti]),
                  reads=[Btmp[ti]], writes=[Bstats[tb]])
            P.add("pool", f_copy(slab[:, 4 * cb:4 * cb + 4, tb, :], tmp[ti].rearrange("p (a b) -> p a b", a=4)),
                  reads=[Btmp[ti]], writes=[Bslab[4 * cb + a][tb // 4] for a in range(4)])

    gate('A1')
    for tb in range(8):
        P.add("dve", lambda e, tb=tb: e.bn_aggr(out=mv[:, tb, :], in_=stats[:, tb, :, :].rearrange("p a b -> p (a b)")),
              reads=[Bstats[tb]], writes=[Bmv])
    gate('a2a')
    P.add("act", f_act(rstd[:, :], mv[:, :, 1], AF.Sqrt, bias=epsc[:], scale=1.0), reads=[Bmv, Beps], writes=[Brstd])
    P.add("dve", lambda e: e.reciprocal(out=rstd[:, :], in_=rstd[:, :]), reads=[Brstd], writes=[Brstd])
    gate('a2b')
    for tb in range(8):
        P.add("dve", f_ts(slab[:, :, tb, :], slab[:, :, tb, :], mv[:, tb, 0:1], rstd[:, tb:tb + 1], ALU.subtract, ALU.mult),
              reads=[Bmv, Brstd] + [Bslab[f][tb // 4] for f in range(32)],
              writes=[Bslab[f][tb // 4] for f in range(32)])

    gate('A2')
    for g in range(8):
        rg = Rg[g % 2]
        bi = brot()
        P.add("pe", f_tr(banks[bi][0:2, 0:128], RB[:, g, :], ident[:, :]), reads=[BRB, Bident], writes=[Bbanks[bi]])
        P.add("dve", f_copy(rg[:, :], banks[bi][0:2, 0:128]), reads=[Bbanks[bi]], writes=[BRg[g % 2]])
        for path in range(2):
            wt_, Bwt = WS.use(8 + 2 * g + path)
            func = AF.Gelu_apprx_tanh if path == 0 else AF.Silu
            boff = 0 if path == 0 else 32
            for fcl in range(4):
                fc = g * 4 + fcl
                s8 = fc % 8
                dstbuf, Bdst = (uT[s8], Bu[s8]) if path == 0 else (sgT[s8], Bsg[s8])
                for th in range(2):
                    bi = brot()
                    bank = banks[bi]
                    for kc in range(16):
                        P.add("pe", f_mm(bank[:, :], wt_[:, kc, fcl * 128:(fcl + 1) * 128],
                                         xT[:, kc, th * 512:(th + 1) * 512], kc == 0, kc == 15),
                              reads=[Bwt] + BxT[th * 4:(th + 1) * 4],
                              war=[Bbanks[bi]] if kc == 0 else [], writes=[Bbanks[bi]] if kc == 15 else [])
                    P.add("act", f_act(dstbuf[:, th * 512:(th + 1) * 512], bank[:, :], func,
                                       bias=bcol[:, boff + fc:boff + fc + 1]),
                          reads=[Bbanks[bi], Bbcol], writes=[Bdst])
        for fcl in range(4):
            fc = g * 4 + fcl
            s8 = fc % 8
            ra = Rab[fc % 4]
            bi = brot()
            P.add("pe", f_tr(banks[bi][0:2, 0:128], CG[:, fc, :], ident[:, :]), reads=[BCG, Bident], writes=[Bbanks[bi]])
            P.add("dve", f_copy(ra[:, :], banks[bi][0:2, 0:128]), reads=[Bbanks[bi]], writes=[BRab[fc % 4]])
            P.add("pool", f_tt(uT[s8], uT[s8], sgT[s8], ALU.mult),
                  reads=[Bu[s8], Bsg[s8]], writes=[Bu[s8]])
            for half in range(2):
                bi = brot()
                bank = banks[bi]
                for tbl in range(4):
                    tb = half * 4 + tbl
                    cols = bank[:, tbl * 128:(tbl + 1) * 128]
                    P.add("pe", f_mm(cols, slab[:, fc, tb, :], WsT[:, g, :], True, False),
                          reads=[Bslab[fc][half], BWsT], war=[Bbanks[bi]] if tbl == 0 else [])
                    P.add("pe", f_mm(cols, ra[:, :], rg[:, :], False, True),
                          reads=[BRab[fc % 4], BRg[g % 2]], writes=[Bbanks[bi]] if tbl == 3 else [])
                sview = slab[:, fc, half * 4:(half + 1) * 4, :].rearrange("p a b -> p (a b)")
                P.add("dve", f_stt(sview, bank[:, :], gcol[:, fc:fc + 1], uT[s8][:, half * 512:(half + 1) * 512],
                                   ALU.mult, ALU.mult),
                      reads=[Bbanks[bi], Bu[s8], Bgcol], writes=[Bslab[fc][half]])

    gate('A3')
    alias(Bz, BxT + Bu + Bsg)
    for cb in range(8):
        wo, Bwo = WS.use(24 + cb)
        for tb in range(8):
            bi = brot()
            bank = banks[bi]
            half = tb // 4
            for kc in range(32):
                P.add("pe", f_mm(bank[:, 0:256], slab[:, kc, tb, :], wo[:, kc, :], kc == 0, kc == 31),
                      reads=[Bslab[kc][half], Bwo],
                      war=[Bbanks[bi]] if kc == 0 else [], writes=[Bbanks[bi]] if kc == 31 else [])
            dst = z[:, tb, cb * 256:(cb + 1) * 256]
            if (cb * 8 + tb) % 2 == 0:
                P.add("act", f_act(dst, bank[:, 0:256], AF.Copy), reads=[Bbanks[bi]], writes=[Bz[tb]])
            else:
                P.add("dve", f_copy(dst, bank[:, 0:256]), reads=[Bbanks[bi]], writes=[Bz[tb]])

    gate('A4')
    allslab = [b for fb in Bslab for b in fb]
    alias([Bln] + Bost + Bx1T, allslab)
    alias([Bxs], Btmp)
    P.add("sp", f_dma(lng, ln_g.partition_broadcast(128)), writes=[Bln], dma="c1")
    P.add("sp", f_dma(lnb, ln_b.partition_broadcast(128)), writes=[Bln], dma="c1")
    outs = []
    for tb in range(8):
        o = ost[tb % 2]
        od = emit_ln_block(C, z[:, tb, :], Bz[tb], xs, Bxs, x, tb, lng, lnb, Bln, o, Bost[tb % 2],
                           lsets[tb % 2], Blsets[tb % 2], epsc, Beps, f"o{tb % 2}", x1_o)
        outs.append(od)
        if "fused" in io:
            io["Bx1d"][tb].writers = [od]
        if tb >= 1:
            emit_transpose_block(C, ost[(tb - 1) % 2], Bost[(tb - 1) % 2], x1T, Bx1T[tb - 1], tb - 1, ident, Bident,
                                 banks, Bbanks, brot)
    emit_transpose_block(C, ost[7 % 2], Bost[7 % 2], x1T, Bx1T[7], 7, ident, Bident, banks, Bbanks, brot)

    gate('A5')
    alias(Bkst, [Bxs])
    ki = 0
    for ct in range(8):
        wt, Bwt = WS.use(32 + ct)
        if ct < 4:
            for hh in range(4):
                h = ct * 4 + hh
                si = ki % 4
                ki += 1
                for th in range(2):
                    bi = brot()
                    bank = banks[bi]
                    for kc in range(16):
                        P.add("pe", f_mm(bank[:, :], wt[:, kc, hh * 128:(hh + 1) * 128], x1T[:, kc, th * 512:(th + 1) * 512],
                                         kc == 0, kc == 15),
                              reads=[Bwt] + Bx1T[th * 4:(th + 1) * 4],
                              war=[Bbanks[bi]] if kc == 0 else [], writes=[Bbanks[bi]] if kc == 15 else [])
                    if th == 0:
                        P.add("act", f_act(kst[si][:, 0:512], bank[:, :], AF.Copy), reads=[Bbanks[bi]], writes=[Bkst[si]])
                    else:
                        P.add("dve", f_copy(kst[si][:, 512:1024], bank[:, :]), reads=[Bbanks[bi]], writes=[Bkst[si]])
                if "fused" in io:
                    kd = P.add("sp", f_dma(io["kloc"][ct][hh * 128:(hh + 1) * 128, :], kst[si]), reads=[Bkst[si]], dma=f"k{si}")
                    io["Bkloc"][ct].writers.append(kd)
                else:
                    outs.append(P.add("sp", f_dma(kT_o[h], kst[si]), reads=[Bkst[si]], dma=f"k{si}"))
        else:
            for tbp in range(4):
                si = ki % 4
                ki += 1
                for t2 in range(2):
                    tb = tbp * 2 + t2
                    bi = brot()
                    bank = banks[bi]
                    for kc in range(16):
                        P.add("pe", f_mm(bank[:, :], x1T[:, kc, tb * 128:(tb + 1) * 128], wt[:, kc, :], kc == 0, kc == 15),
                              reads=[Bwt, Bx1T[tb]],
                              war=[Bbanks[bi]] if kc == 0 else [], writes=[Bbanks[bi]] if kc == 15 else [])
                    if t2 == 0:
                        P.add("act", f_act(kst[si][:, 0:512], bank[:, :], AF.Copy), reads=[Bbanks[bi]], writes=[Bkst[si]])
                    else:
                        P.add("dve", f_copy(kst[si][:, 512:1024], bank[:, :]), reads=[Bbanks[bi]], writes=[Bkst[si]])
                c0 = (ct - 4) * 512
                if "fused" in io:
                    dst = io["vloc"][ct - 4][tbp * 256:(tbp + 1) * 256, :].rearrange("(a p) c -> p a c", p=128)
                    vd = P.add("sp", f_dma(dst, kst[si].rearrange("p (a c) -> p a c", a=2)), reads=[Bkst[si]], dma=f"k{si}")
                    io["Bvloc"][ct - 4].writers.append(vd)
                else:
                    dst = v_o[tbp * 256:(tbp + 1) * 256, c0:c0 + 512].rearrange("(a p) c -> p a c", p=128)
                    outs.append(P.add("sp", f_dma(dst, kst[si].rearrange("p (a c) -> p a c", a=2)), reads=[Bkst[si]], dma=f"k{si}"))
        if "fused" in io and ct in (0, 4):
            io["gather"](ct)
    if "fused" not in io:
        P.add("sp", None, after=outs)
    else:
        io.update(A_Bz=Bz, A_Bx1T=Bx1T, A_x1T=x1T, A_Bln=Bln, A_Bost=Bost, A_Bkst=Bkst, A_Bxs=Bxs,
                  banks=banks, Bbanks=Bbanks, brot=brot, WS=WS)


def blocks_of(r):
    out = []
    for m in range(4):
        out += [8 * m + r, 8 * m + 7 - r]
    return out


def shard_tokens(x):
    res = []
    for c in range(8):
        b, r = divmod(c, 4)
        xb = x[b].reshape(32, 128, -1)
        res.append(np.ascontiguousarray(xb[blocks_of(r)].reshape(1024, -1)))
    return res


def common_A(inputs):
    return {
        "a_w_in": np.ascontiguousarray(inputs["a_w_in"][0]),
        "a_b_in": np.ascontiguousarray(inputs["a_b_in"][0].reshape(1, 12288)),
        "a_vln_g": np.ascontiguousarray(inputs["a_vln_g"][0].reshape(1, 4096)),
        "a_vln_b": np.ascontiguousarray(inputs["a_vln_b"][0].reshape(1, 4096)),
        "a_w_s": np.ascontiguousarray(inputs["a_w_s"][0]),
        "a_b_s": np.ascontiguousarray(inputs["a_b_s"][0].reshape(1, 1024)),
        "a_w_out": np.ascontiguousarray(inputs["a_w_out"][0]),
        "kv_w": np.ascontiguousarray(inputs["kv_w"]),
        "ln_g": np.ascontiguousarray(inputs["ln_g"][0].reshape(1, 2048)),
        "ln_b": np.ascontiguousarray(inputs["ln_b"][0].reshape(1, 2048)),
    }


def run_A(inputs, upto=None):
    nc = build_A(upto)
    xs = shard_tokens(np.asarray(inputs["x"], dtype=np.float32))
    common = common_A(inputs)
    in_maps = [dict(common, x=xs[c]) for c in range(8)]
    res = run_bass_kernel_spmd(nc, in_maps, core_ids=list(range(8)))
    return res.results


def build_B(upto=None):
    C = Ctx()
    try:
        _build_B_body(C, upto)
    except StopBuild:
        pass
    C.P.emit(C.nc, C.st)
    C.st.close()
    return C.nc


def _build_B_body(C, upto):
    def gate(name):
        if upto == name:
            raise StopBuild()

    nc, P = C.nc, C.P
    io = C.io
    fused = "fused" in io
    masks_d = C.din("masks", [128, 32 * 128], BF16)
    if fused:
        w_in, w_out = io["b_w_in"], io["b_w_out"]
    else:
        w_in = C.din("b_w_in", [2048, 4096])
        w_out = C.din("b_w_out", [2048, 2048])
    out_o = C.dout("out", [1024, 2048])
    if fused:
        x1 = io["x1_scratch"]
        ln_g, ln_b = io["ln_g"][1:2, :], io["ln_b"][1:2, :]
    else:
        x1 = C.din("x1", [1024, 2048])
        kTf = C.din("kTf", [16, 128, 4096], BF16)
        vf = C.din("vf", [4096, 2048], BF16)
        ln_g = C.din("ln_g", [1, 2048])
        ln_b = C.din("ln_b", [1, 2048])

    def pos(kb):
        if not fused:
            return kb
        m, o = divmod(kb, 8)
        return (o * 8 + 2 * m) if o < 4 else ((7 - o) * 8 + 2 * m + 1)

    if fused:
        oX, oQ, oZ, oSG, oKV = 64 * K, 32 * K, 32 * K, 0, 96 * K
    else:
        oX, oQ, oZ, oSG, oKV = 0, 32 * K, 0, 64 * K, 144 * K
    x1T = C.view(oX, 32 * K, BF16, "p (k t) -> p k t", k=16)
    ebuf = [C.view(oX + i * 4 * K, 4 * K, F32) for i in range(2)]
    spb = [C.view(oX + 8 * K + i * 2 * K, 2 * K, BF16) for i in range(2)]
    Sb = C.view(oX + 12 * K, 2 * K, BF16)
    wb = [C.view(oX + 14 * K + i * 2 * K, 2 * K, BF16) for i in range(2)]
    qT = C.view(oQ, 32 * K, BF16, "p (h t) -> p h t", h=16)
    z = C.view(oZ, 64 * K, F32, "p (a d) -> p a d", a=8)
    sgT = C.view(oSG, 32 * K, BF16, "p (h t) -> p h t", h=16)
    wsl = [C.view(96 * K + i * 16 * K, 16 * K, BF16) for i in range(3)]
    KTs = [C.view(oKV + i * 8 * K, 8 * K, BF16) for i in range(2)]
    Vs = [C.view(oKV + 16 * K + i * 8 * K, 8 * K, BF16, "p (b d) -> p b d", b=32) for i in range(2)]
    lng = C.view(oKV, 8 * K, F32)
    lnb = C.view(oKV + 8 * K, 8 * K, F32)
    ost = [C.view(oKV + 16 * K + i * 8 * K, 8 * K, F32) for i in range(2)]
    xs = C.view(176 * K, 8 * K, F32)

    Bxs = Buf("xs")
    Bx1T = [Buf(f"x1T{t}") for t in range(8)]
    Bq = [Buf(f"q{h}") for h in range(16)]
    Bsg = [Buf(f"sg{h}") for h in range(16)]
    Bz = [Buf(f"z{t}") for t in range(8)]
    Bw = [Buf(f"w{i}") for i in range(3)]
    BKT = [Buf("KT0"), Buf("KT1")]
    BV = [Buf("V0"), Buf("V1")]
    Be = [Buf("e0"), Buf("e1")]
    Bsp = [Buf("sp0"), Buf("sp1")]
    BS = Buf("S")
    Bwb = [Buf("wb0"), Buf("wb1")]
    Bln = Buf("ln")
    Bost = [Buf("ost0"), Buf("ost1")]

    PT = C.psum4()
    if fused:
        banks, Bbanks, brot = io["banks"], io["Bbanks"], io["brot"]
        Bx1T = io["A_Bx1T"]
        alias(Bsg + Bq, io["A_Bz"])
        alias(BKT + BV, [io["A_Bln"]] + io["A_Bost"])
        alias([Bxs], io["A_Bkst"] + [io["A_Bxs"]])
    else:
        banks = [PT[i // 2][:, (i % 2) * 512:(i % 2 + 1) * 512] for i in range(8)]
        Bbanks = [Buf(f"bank{i}") for i in range(8)]
        brot = Rot(8)

    ident, Bident = emit_consts(C)
    masks = C.sb([128, 32, 128], BF16, "masks_sb")
    negU = C.sb([128, 128], BF16, "negU")
    negO = C.sb([128, 128], BF16, "negO")
    epsc = C.sb([128, 1], F32, "epsc")
    lstats = C.sb([128, 4, 6], F32, "lstats")
    lmv = C.sb([128, 2], F32, "lmv")
    lrstd = C.sb([128, 1], F32, "lrstd")
    lnbias = C.sb([128, 1], F32, "lnbias")
    lsets = [(lstats, lmv, lrstd, lnbias),
             (C.sb([128, 4, 6], F32, "lstats2"), C.sb([128, 2], F32, "lmv2"), C.sb([128, 1], F32, "lrstd2"),
              C.sb([128, 1], F32, "lnbias2"))]
    Blsets = [Buf("lsm0"), Buf("lsm1")]
    Bmasks, BnegU, BnegO, Beps, Blsm = Buf("masks"), Buf("negU"), Buf("negO"), Buf("eps"), Buf("lsm")
    P.add("sp", f_dma(masks[:].rearrange("p a b -> p (a b)"), masks_d), writes=[Bmasks], dma=uch())
    P.add("pool", f_memset(epsc[:], EPS), writes=[Beps])
    P.add("pool", f_memset(negO[:], -1.0), writes=[BnegO])
    P.add("pool", f_memset(negU[:], -1.0), writes=[BnegU])
    P.add("pool", f_asel(negU[:], negU[:], ALU.is_ge, 0.0, [[-1, 128]], 1), reads=[BnegU], writes=[BnegU])

    if fused:
        WS, wbase = io["WS"], 40
    else:
        tiles = ([(w_in[:, ct * 512:(ct + 1) * 512], 16) for ct in range(8)]
                 + [(w_out[:, cb * 512:(cb + 1) * 512], 16) for cb in range(4)])
        WS, wbase = WStream(C, wsl, Bw, tiles), 0
        emit_load_T(C, x1, xs, Bxs, "xs", x1T, Bx1T, ident, Bident, banks, Bbanks, brot)
    gate('B0')

    for ct in range(8):
        wt, Bwt = WS.use(wbase + ct)
        if fused and ct >= 2:
            io["gather"]([0, 4, 1, 5, 2, 6, 3, 7][ct])
        for hh in range(4):
            h = (ct % 4) * 4 + hh
            for th in range(2):
                bi = brot()
                bank = banks[bi]
                for kc in range(16):
                    P.add("pe", f_mm(bank, wt[:, kc, hh * 128:(hh + 1) * 128], x1T[:, kc, th * 512:(th + 1) * 512],
                                     kc == 0, kc == 15),
                          reads=[Bwt] + Bx1T[th * 4:(th + 1) * 4],
                          war=[Bbanks[bi]] if kc == 0 else [], writes=[Bbanks[bi]] if kc == 15 else [])
                if ct < 4:
                    P.add("act", f_act(qT[:, h, th * 512:(th + 1) * 512], bank, AF.Copy, scale=128.0 ** -0.5),
                          reads=[Bbanks[bi]], writes=[Bq[h]])
                else:
                    P.add("act", f_act(sgT[:, h, th * 512:(th + 1) * 512], bank, AF.Silu),
                          reads=[Bbanks[bi]], writes=[Bsg[h]])
    gate('B1')

    Z = PT[0:3]
    BZ = [Buf("Z0"), Buf("Z1"), Buf("Z2")]
    OT = PT[3]
    BOT = Buf("OT")
    alias(BZ + [BOT], Bbanks)
    alias(Be + Bsp + [BS] + Bwb, Bx1T)
    its = [(h, kb) for h in range(16) for kb in range(31, -1, -1)]
    n = len(its)

    def segs(c0, c1):
        out = []
        if c0 < 512:
            out.append((c0, min(c1, 512)))
        if c1 > 512:
            out.append((max(c0, 512), c1))
        return [(a, b) for a, b in out if b > a]

    def load_head(h):
        hs = h % 2
        if fused:
            ct, hh = divmod(h, 4)
            ksrc = io["kall"][ct].rearrange("(r q) t -> q r t", q=512)[hh * 128:(hh + 1) * 128, :, :]
            P.add("sp", f_dma(KTs[hs].rearrange("p (r t) -> p r t", r=4), ksrc), reads=[io["Bkall"][ct]],
                  writes=[BKT[hs]], dma=f"kt{hs}")
            src = io["vall"][ct].rearrange("(b p) c -> p b c", p=128)[:, :, hh * 128:(hh + 1) * 128]
            rd = [io["Bvall"][ct]]
        else:
            P.add("sp", f_dma(KTs[hs], kTf[h]), writes=[BKT[hs]], dma=f"kt{hs}")
            src = vf[:, h * 128:(h + 1) * 128].rearrange("(b p) d -> p b d", p=128)
            rd = []
        for q4 in range(4):
            P.add("sp", f_dma(Vs[hs][:, q4 * 8:(q4 + 1) * 8, :], src[:, q4 * 8:(q4 + 1) * 8, :]), reads=rd,
                  writes=[BV[hs]], dma=f"v{hs}")

    def QK(i):
        h, kb = its[i]
        if kb == 31 and h == 0:
            load_head(0)
            load_head(1)
        c0 = (kb // 4) * 128
        s = i % 3
        sg_ = segs(c0, 1024)
        for k, (a, b) in enumerate(sg_):
            P.add("pe", f_mm(Z[s][:, a:b], KTs[h % 2][:, pos(kb) * 128:(pos(kb) + 1) * 128], qT[:, h, a:b], True, False),
                  reads=[BKT[h % 2], Bq[h]], war=[BZ[s]] if k == 0 else [], writes=[BZ[s]] if k == len(sg_) - 1 else [])

    def E(i):
        h, kb = its[i]
        c0 = (kb // 4) * 128
        P.add("act", f_act(ebuf[i % 2][:, c0:1024], Z[i % 3][:, c0:1024], AF.Exp), reads=[BZ[i % 3]], writes=[Be[i % 2]])

    def L(i):
        h, kb = its[i]
        j0, mi = kb // 4, kb % 4
        c0 = j0 * 128
        P.add("act", f_act(spb[i % 2][:, c0:1024], ebuf[i % 2][:, c0:1024], AF.Ln, bias=1.0), reads=[Be[i % 2]], writes=[Bsp[i % 2]])
        P.add("dve", f_tt(spb[i % 2][:, c0:c0 + 128], spb[i % 2][:, c0:c0 + 128], masks[:, j0 * 4 + mi, :], ALU.mult),
              reads=[Bsp[i % 2], Bmasks], writes=[Bsp[i % 2]])

    def US(i):
        h, kb = its[i]
        j0, mi = kb // 4, kb % 4
        c0 = j0 * 128
        s = i % 3
        ops = [(negU, BnegU, Bsp[i % 2], spb[i % 2], a, b) for a, b in segs(c0, 1024)]
        cS = c0 + 128 if mi == 3 else c0
        ops += [(negO, BnegO, BS, Sb, a, b) for a, b in segs(cS, 1024)]
        for k, (lh, Blh, Brh, rh, a, b) in enumerate(ops):
            P.add("pe", f_mm(Z[s][:, a:b], lh[:, :], rh[:, a:b], False, True),
                  reads=[Blh, Brh], war=[BZ[s]] if k == 0 else [], writes=[BZ[s]] if k == len(ops) - 1 else [])
        if mi == 3:
            P.add("dve", f_copy(Sb[:, c0:c0 + 128], spb[i % 2][:, c0:c0 + 128]), reads=[Bsp[i % 2]], writes=[BS])
        if cS < 1024:
            P.add("dve", f_tt(Sb[:, cS:1024], Sb[:, cS:1024], spb[i % 2][:, cS:1024], ALU.add), reads=[Bsp[i % 2], BS], writes=[BS])

    def W(i):
        h, kb = its[i]
        j0, mi = kb // 4, kb % 4
        c0 = j0 * 128
        P.add("act", f_act(wb[i % 2][:, c0:1024], Z[i % 3][:, c0:1024], AF.Exp), reads=[BZ[i % 3]], writes=[Bwb[i % 2]])
        P.add("dve", f_tt(wb[i % 2][:, c0:c0 + 128], wb[i % 2][:, c0:c0 + 128], masks[:, j0 * 4 + mi, :], ALU.mult),
              reads=[Bwb[i % 2], Bmasks], writes=[Bwb[i % 2]])

    def PV(i):
        h, kb = its[i]
        c0 = (kb // 4) * 128
        if kb == 31:
            if 1 <= h < 15:
                load_head(h + 1)
            P.add("dve", f_memset(OT[:, :], 0.0), writes=[BOT])
        sg_ = segs(c0, 1024)
        for k, (a, b) in enumerate(sg_):
            P.add("pe", f_mm(OT[:, a:b], Vs[h % 2][:, pos(kb), :], wb[i % 2][:, a:b], False, True),
                  reads=[BV[h % 2], Bwb[i % 2]], war=[BOT] if k == 0 else [], writes=[BOT] if k == len(sg_) - 1 else [])
        if kb == 0:
            for hf in range(2):
                P.add("dve", f_tt(sgT[:, h, hf * 512:(hf + 1) * 512], OT[:, hf * 512:(hf + 1) * 512],
                                  sgT[:, h, hf * 512:(hf + 1) * 512], ALU.mult),
                      reads=[BOT, Bsg[h]], writes=[Bsg[h], BOT])

    nit = n
    QK(0)
    E(0)
    L(0)
    for i in range(nit):
        if i + 1 < nit:
            QK(i + 1)
            E(i + 1)
        if i - 1 >= 0:
            W(i - 1)
        if i + 1 < nit:
            L(i + 1)
        US(i)
        if i - 1 >= 0:
            PV(i - 1)
    W(nit - 1)
    PV(nit - 1)
    gate('B2')

    alias(Bz, Bq + Be + Bsp + [BS] + Bwb + Bx1T)
    alias(Bbanks, BZ + [BOT])
    for cb in range(4):
        wo, Bwo = WS.use(wbase + 8 + cb)
        for tb in range(8):
            bi = brot()
            bank = banks[bi]
            for kc in range(16):
                P.add("pe", f_mm(bank, sgT[:, kc, tb * 128:(tb + 1) * 128], wo[:, kc, :], kc == 0, kc == 15),
                      reads=[Bsg[kc], Bwo], war=[Bbanks[bi]] if kc == 0 else [], writes=[Bbanks[bi]] if kc == 15 else [])
            dst = z[:, tb, cb * 512:(cb + 1) * 512]
            if (cb * 8 + tb) % 2 == 0:
                P.add("act", f_act(dst, bank, AF.Copy), reads=[Bbanks[bi]], writes=[Bz[tb]])
            else:
                P.add("dve", f_copy(dst, bank), reads=[Bbanks[bi]], writes=[Bz[tb]])
    gate('B4')

    alias([Bln] + Bost, BKT + BV)
    P.add("sp", f_dma(lng, ln_g.partition_broadcast(128)), writes=[Bln], dma="c1")
    P.add("sp", f_dma(lnb, ln_b.partition_broadcast(128)), writes=[Bln], dma="c1")
    outs = []
    for tb in range(8):
        o = ost[tb % 2]
        outs.append(emit_ln_block(C, z[:, tb, :], Bz[tb], xs, Bxs, x1, tb, lng, lnb, Bln, o, Bost[tb % 2],
                                  lsets[tb % 2], Blsets[tb % 2], epsc, Beps, f"o{tb % 2}", out_o))
    P.add("sp", None, after=outs)


def make_masks(r):
    qb = blocks_of(r)
    m = np.zeros((128, 32, 128), np.float32)
    tri = (np.arange(128)[:, None] < np.arange(128)[None, :]).astype(np.float32)
    for j in range(8):
        for mi in range(4):
            kb = 4 * j + mi
            if kb < qb[j]:
                m[:, j * 4 + mi, :] = 1.0
            elif kb == qb[j]:
                m[:, j * 4 + mi, :] = tri
    return np.ascontiguousarray(m.reshape(128, 32 * 128)).astype(ml_dtypes.bfloat16)


def assemble_kv(resA):
    kTf = [np.zeros((16, 128, 4096), ml_dtypes.bfloat16) for _ in range(2)]
    vf = [np.zeros((4096, 2048), ml_dtypes.bfloat16) for _ in range(2)]
    for c in range(8):
        b, r = divmod(c, 4)
        for j, qb in enumerate(blocks_of(r)):
            kTf[b][:, :, qb * 128:(qb + 1) * 128] = resA[c]["kT"][:, :, j * 128:(j + 1) * 128]
            vf[b][qb * 128:(qb + 1) * 128, :] = resA[c]["v"][j * 128:(j + 1) * 128, :]
    return kTf, vf


def common_B(inputs):
    return {
        "b_w_in": np.ascontiguousarray(inputs["b_w_in"][0]),
        "b_w_out": np.ascontiguousarray(inputs["b_w_out"][0]),
        "ln_g": np.ascontiguousarray(inputs["ln_g"][1].reshape(1, 2048)),
        "ln_b": np.ascontiguousarray(inputs["ln_b"][1].reshape(1, 2048)),
    }


def unshard_tokens(outs):
    res = np.zeros((2, 32, 128, 2048), np.float32)
    for c in range(8):
        b, r = divmod(c, 4)
        res[b, blocks_of(r)] = outs[c].reshape(8, 128, 2048)
    return res.reshape(2, 4096, 2048)


def build_fused():
    C = Ctx()
    nc, P = C.nc, C.P
    io = C.io
    io["fused"] = True
    io["ln_g"] = C.din("ln_g", [2, 2048])
    io["ln_b"] = C.din("ln_b", [2, 2048])
    io["x1_scratch"] = nc.dram_tensor("x1_scratch", [1024, 2048], F32).ap()
    kloc = [nc.dram_tensor(f"kloc{i}", [512, 1024], BF16) for i in range(4)]
    vloc = [nc.dram_tensor(f"vloc{i}", [1024, 512], BF16) for i in range(4)]
    kall = [nc.dram_tensor(f"kall{i}", [2048, 1024], BF16) for i in range(4)]
    vall = [nc.dram_tensor(f"vall{i}", [4096, 512], BF16) for i in range(4)]
    io["kloc"] = [t.ap() for t in kloc]
    io["vloc"] = [t.ap() for t in vloc]
    io["kall"] = [t.ap() for t in kall]
    io["vall"] = [t.ap() for t in vall]
    io["Bkloc"] = [Buf(f"kloc{i}") for i in range(4)]
    io["Bvloc"] = [Buf(f"vloc{i}") for i in range(4)]
    io["Bkall"] = [Buf(f"kall{i}") for i in range(4)]
    io["Bvall"] = [Buf(f"vall{i}") for i in range(4)]
    io["Bx1d"] = [Buf(f"x1d{t}") for t in range(8)]
    groups = [[0, 1, 2, 3], [4, 5, 6, 7]]

    def gather(ct):
        if ct < 4:
            src, dst, Bs, Bd = kloc[ct], kall[ct], io["Bkloc"][ct], io["Bkall"][ct]
        else:
            src, dst, Bs, Bd = vloc[ct - 4], vall[ct - 4], io["Bvloc"][ct - 4], io["Bvall"][ct - 4]
        P.add("pool", lambda e: e.collective_compute("AllGather", ALU.bypass, replica_groups=groups,
                                                     ins=[src.ap().opt()], outs=[dst.ap().opt()]),
              reads=[Bs], writes=[Bd], dma=f"cc{ct}", step=1)

    io["gather"] = gather
    io["b_w_in"] = C.din("b_w_in", [2048, 4096])
    io["b_w_out"] = C.din("b_w_out", [2048, 2048])
    io["extra_tiles"] = ([(io["b_w_in"][:, ct * 512:(ct + 1) * 512], 16) for ct in range(8)]
                         + [(io["b_w_out"][:, cb * 512:(cb + 1) * 512], 16) for cb in range(4)])
    C.prefix = "A_"
    _build_A_body(C, None)
    io["Bx1d_rd"] = {tb: [io["Bx1d"][tb]] for tb in range(8)}
    C.prefix = "B_"
    _build_B_body(C, None)
    P.emit(nc, C.st)
    C.st.close()
    return nc


def kernel(x, a_w_in, a_b_in, a_vln_g, a_vln_b, a_w_s, a_b_s, a_w_out, kv_w, b_w_in, b_w_out, ln_g, ln_b):
    inputs = dict(x=np.asarray(x), a_w_in=np.asarray(a_w_in), a_b_in=np.asarray(a_b_in), a_vln_g=np.asarray(a_vln_g),
                  a_vln_b=np.asarray(a_vln_b), a_w_s=np.asarray(a_w_s), a_b_s=np.asarray(a_b_s),
                  a_w_out=np.asarray(a_w_out), kv_w=np.asarray(kv_w), b_w_in=np.asarray(b_w_in),
                  b_w_out=np.asarray(b_w_out), ln_g=np.asarray(ln_g), ln_b=np.asarray(ln_b))
    nc = build_fused()
    in_maps = fused_in_maps(inputs)
    res = run_bass_kernel_spmd(nc, in_maps, core_ids=list(range(8))).results
    return unshard_tokens([res[c]["out"] for c in range(8)])


def fused_in_maps(inputs):
    xs = shard_tokens(np.asarray(inputs["x"], dtype=np.float32))
    common = common_A(inputs)
    common.update(common_B(inputs))
    common["ln_g"] = np.ascontiguousarray(inputs["ln_g"], dtype=np.float32)
    common["ln_b"] = np.ascontiguousarray(inputs["ln_b"], dtype=np.float32)
    return [dict(common, x=xs[c], masks=make_masks(c % 4)) for c in range(8)]
```

```python
from contextlib import ExitStack

import ml_dtypes
import numpy as np

import concourse.bass as bass
import concourse.mybir as mybir
from concourse.bass_utils import run_bass_kernel_spmd

F32 = mybir.dt.float32
BF16 = mybir.dt.bfloat16
AF = mybir.ActivationFunctionType
ALU = mybir.AluOpType

ALPHA = 4.0 ** 0.25
EPS = 1e-5
ENGS = ("pe", "act", "dve", "pool", "sp")


class Buf:
    __slots__ = ("name", "writers", "readers")

    def __init__(self, name):
        self.name = name
        self.writers = []
        self.readers = []


class Op:
    __slots__ = ("eng", "fn", "deps", "inc", "dma", "dma_val", "val", "step")

    def __init__(self, eng, fn, dma):
        self.eng = eng
        self.fn = fn
        self.deps = []
        self.inc = False
        self.dma = dma
        self.dma_val = 0
        self.val = 0


def alias(new_bufs, old_bufs):
    olds = []
    for b in old_bufs:
        for o in b.readers + b.writers:
            if o not in olds:
                olds.append(o)
    for nb in new_bufs:
        for o in olds:
            if o not in nb.readers:
                nb.readers.append(o)


class Prog:
    def __init__(self):
        self.ops = {e: [] for e in ENGS}
        self.dma_counts = {}
        self.last_dma = {}
        self.bar = []

    def barrier(self, skip_prefix="cc"):
        deps = []
        for e in ENGS:
            for op in reversed(self.ops[e]):
                if op.dma is None and op.fn is not None:
                    deps.append(op)
                    break
        for ch, op in self.last_dma.items():
            if not ch.startswith(skip_prefix):
                deps.append(op)
        self.bar = deps

    def add(self, eng, fn, reads=(), writes=(), war=(), dma=None, after=(), step=16):
        op = Op(eng, fn, dma)
        op.step = step
        deps = {}
        for b in reads:
            for w in b.writers:
                deps[w] = "raw"
        for b in list(writes) + list(war):
            for w in b.writers:
                deps.setdefault(w, "waw")
            for r in b.readers:
                deps.setdefault(r, "war")
        for a in list(after) + self.bar:
            if a is not None:
                deps[a] = "raw"
        for d, kind in deps.items():
            if d is op:
                continue
            if d.dma is not None:
                op.deps.append(d)
            elif d.eng == eng and dma is None and (kind == "war" or (kind == "waw" and eng == "pe")):
                continue
            else:
                d.inc = True
                op.deps.append(d)
        for b in reads:
            if dma is None:
                b.readers = [r for r in b.readers if not (r.eng == eng and r.dma is None)]
            b.readers.append(op)
        for b in writes:
            if b.readers:
                b.writers = [op]
                b.readers = []
            else:
                if dma is None:
                    b.writers = [w for w in b.writers if not (w.eng == eng and w.dma is None)]
                b.writers.append(op)
        if dma is not None:
            c = self.dma_counts.get(dma, 0) + step
            self.dma_counts[dma] = c
            op.dma_val = c
            self.last_dma[dma] = op
        self.ops[eng].append(op)
        return op

    def emit(self, nc, st):
        esem = {e: st.enter_context(nc.semaphore("s_" + e)) for e in ENGS}
        dsem = {c: st.enter_context(nc.semaphore("d_" + c)) for c in self.dma_counts}
        for e in ENGS:
            v = 0
            for op in self.ops[e]:
                if op.inc and op.dma is None:
                    v += 1
                    op.val = v
        block = st.enter_context(nc.Block())

        def run(e, eng):
            waited = {}
            for op in self.ops[e]:
                need = {}
                for d in op.deps:
                    if d.dma is not None:
                        key, val, sem = ("d", d.dma), d.dma_val, dsem[d.dma]
                    else:
                        key, val, sem = ("e", d.eng), d.val, esem[d.eng]
                    if val > need.get(key, (0, None))[0]:
                        need[key] = (val, sem)
                for key, (val, sem) in need.items():
                    if waited.get(key, 0) >= val:
                        continue
                    eng.wait_ge(sem, val)
                    waited[key] = val
                if op.fn is None:
                    continue
                ins = op.fn(eng)
                if op.dma is not None:
                    ins.then_inc(dsem[op.dma], op.step)
                elif op.inc:
                    ins.then_inc(esem[e], 1)

        @block.tensor
        def _(eng):
            run("pe", eng)

        @block.scalar
        def _(eng):
            run("act", eng)

        @block.vector
        def _(eng):
            run("dve", eng)

        @block.gpsimd
        def _(eng):
            run("pool", eng)

        @block.sync
        def _(eng):
            run("sp", eng)


class Rot:
    def __init__(self, n):
        self.n = n
        self.i = -1

    def __call__(self):
        self.i = (self.i + 1) % self.n
        return self.i


def f_mm(out, lhsT, rhs, start, stop):
    return lambda e: e.matmul(out, lhsT=lhsT, rhs=rhs, start=start, stop=stop, skip_group_check=True)


def f_tr(out, in_, ident):
    return lambda e: e.transpose(out, in_, ident)


def f_act(out, in_, func, bias=None, scale=1.0):
    if bias is None:
        return lambda e: e.activation(out=out, in_=in_, func=func, scale=scale)
    return lambda e: e.activation(out=out, in_=in_, func=func, bias=bias, scale=scale)


def f_copy(out, in_):
    return lambda e: e.tensor_copy(out=out, in_=in_)


def f_dma(out, in_):
    return lambda e: e.dma_start(out=out, in_=in_)


def f_tt(out, in0, in1, op):
    return lambda e: e.tensor_tensor(out=out, in0=in0, in1=in1, op=op)


def f_ts(out, in0, s1, s2, op0, op1):
    return lambda e: e.tensor_scalar(out=out, in0=in0, scalar1=s1, scalar2=s2, op0=op0, op1=op1)


def f_stt(out, in0, scalar, in1, op0, op1):
    return lambda e: e.scalar_tensor_tensor(out=out, in0=in0, scalar=scalar, in1=in1, op0=op0, op1=op1)


def f_memset(ap, v):
    return lambda e: e.memset(ap, v)


def f_asel(out, in_, cmp, fill, pattern, cm, base=0):
    return lambda e: e.affine_select(out=out, in_=in_, compare_op=cmp, fill=fill, base=base,
                                     pattern=pattern, channel_multiplier=cm)


class Ctx:
    def __init__(self, arena_bytes=184 * 1024):
        self.nc = bass.Bass("TRN2", target_bir_lowering=False)
        self.P = Prog()
        self.st = ExitStack()
        self.arena = self.st.enter_context(self.nc.sbuf_tensor("arena", [128, arena_bytes // 4], F32))
        self.nsmall = 0
        self.prefix = ""
        self.PT = None
        self.io = {}

    def view(self, off, nbytes, dt, pat=None, **kw):
        a = self.arena[:, off // 4:(off + nbytes) // 4]
        if dt is BF16:
            a = a.bitcast(BF16)
        if pat is not None:
            a = a.rearrange(pat, **kw)
        return a

    def sb(self, shape, dt, name=None):
        self.nsmall += 1
        return self.st.enter_context(self.nc.sbuf_tensor(self.prefix + (name or f"sm{self.nsmall}"), shape, dt))

    def psum4(self):
        if self.PT is None:
            self.PT = [self.ps([128, 1024], f"pt{i}") for i in range(4)]
        return self.PT

    def ps(self, shape, name):
        return self.st.enter_context(self.nc.psum_tensor(name, shape, F32))

    def din(self, name, shape, dt=F32):
        return self.nc.dram_tensor(name, shape, dt, kind="ExternalInput").ap()

    def dout(self, name, shape, dt=F32):
        return self.nc.dram_tensor(name, shape, dt, kind="ExternalOutput").ap()


K = 1024


def emit_consts(C):
    P = C.P
    if "ident" in C.io:
        return C.io["ident"]
    ident = C.sb([128, 128], F32, "ident")
    Bident = Buf("ident")
    P.add("pool", f_memset(ident[:], 1.0), writes=[Bident])
    P.add("pool", f_asel(ident[:], ident[:], ALU.is_equal, 0.0, [[-1, 128]], 1), reads=[Bident], writes=[Bident])
    C.io["ident"] = (ident, Bident)
    return ident, Bident


def emit_load_T(C, src, xs, Bxs, xsch, dstT, BdstT, ident, Bident, banks, Bbanks, rot):
    P = C.P
    for tb in range(8):
        P.add("sp", f_dma(xs[:], src[tb * 128:(tb + 1) * 128, :]), writes=[Bxs], dma=xsch)
        emit_transpose_block(C, xs, Bxs, dstT, BdstT[tb], tb, ident, Bident, banks, Bbanks, rot)


def emit_transpose_block(C, xs, Bxs, dstT, Bdst, tb, ident, Bident, banks, Bbanks, rot):
    P = C.P
    for q in range(4):
        bi = rot()
        bank = banks[bi]
        for jj in range(4):
            kc = q * 4 + jj
            P.add("pe", f_tr(bank[:, jj * 128:(jj + 1) * 128], xs[:, kc * 128:(kc + 1) * 128], ident[:]),
                  reads=[Bxs, Bident], writes=[Bbanks[bi]] if jj == 3 else [], war=[Bbanks[bi]] if jj == 0 else [])
        dst = dstT[:, q * 4:(q + 1) * 4, tb * 128:(tb + 1) * 128]
        srcv = bank[:, :].rearrange("p (a b) -> p a b", a=4)
        if q % 2 == 0:
            P.add("act", f_act(dst, srcv, AF.Copy), reads=[Bbanks[bi]], writes=[Bdst])
        else:
            P.add("dve", f_copy(dst, srcv), reads=[Bbanks[bi]], writes=[Bdst])


def emit_ln_load(C, xs, Bxs, x_src, tb):
    C.P.add("sp", f_dma(xs[:], x_src[tb * 128:(tb + 1) * 128, :]), reads=C.io.get("Bx1d_rd", {}).get(tb, []),
            writes=[Bxs], dma="xs")


def emit_ln_block(C, zt, Bz, xs, Bxs, x_src, tb, lng, lnb, Bln, ost, Bost, small, Bsm, epsc, Beps, och, out_dst):
    P = C.P
    stats, mv, rstd, nb = small
    if tb == 0:
        emit_ln_load(C, xs, Bxs, x_src, 0)
    P.add("dve", f_stt(zt, xs[:], ALPHA, zt, ALU.mult, ALU.add), reads=[Bxs, Bz], writes=[Bz])
    if tb + 1 < 8:
        emit_ln_load(C, xs, Bxs, x_src, tb + 1)
    for c in range(4):
        P.add("dve", lambda e, c=c: e.bn_stats(out=stats[:, c, :], in_=zt[:, c * 512:(c + 1) * 512]),
              reads=[Bz], writes=[Bsm])
    P.add("dve", lambda e: e.bn_aggr(out=mv[:], in_=stats[:].rearrange("p a b -> p (a b)")), reads=[Bsm], writes=[Bsm])
    P.add("act", f_act(rstd[:], mv[:, 1:2], AF.Sqrt, bias=epsc[:], scale=1.0), reads=[Bsm, Beps], writes=[Bsm])
    P.add("dve", lambda e: e.reciprocal(out=rstd[:], in_=rstd[:]), reads=[Bsm], writes=[Bsm])
    P.add("dve", f_stt(nb[:], mv[:, 0:1], -1.0, rstd[:], ALU.mult, ALU.mult), reads=[Bsm], writes=[Bsm])
    P.add("act", f_act(zt, zt, AF.Identity, bias=nb[:], scale=rstd[:]), reads=[Bz, Bsm], writes=[Bz])
    P.add("dve", f_tt(ost, zt, lng, ALU.mult), reads=[Bz, Bln], writes=[Bost])
    P.add("pool", f_tt(ost, ost, lnb, ALU.add), reads=[Bost, Bln], writes=[Bost])
    return P.add("sp", f_dma(out_dst[tb * 128:(tb + 1) * 128, :], ost), reads=[Bost], dma=och)


def load_wtile(C, dst, Bdst, ch, src2d, nk, split=4):
    P = C.P
    step = nk // split
    for s in range(split):
        P.add("pool", f_dma(dst[:, s * step:(s + 1) * step, :],
                            src2d[s * step * 128:(s + 1) * step * 128, :].rearrange("(k p) c -> p k c", p=128)),
              writes=[Bdst], dma=ch)


class WStream:
    def __init__(self, C, wsl, Bw, tiles):
        self.C, self.wsl, self.Bw, self.tiles = C, wsl, Bw, tiles
        self.issued = 0

    def use(self, i):
        while self.issued < min(i + 3, len(self.tiles)):
            j = self.issued
            src2d, nk = self.tiles[j]
            sl = j % 3
            dst = self.wsl[sl].rearrange("p (k c) -> p k c", k=nk)
            load_wtile(self.C, dst, self.Bw[sl], f"w{sl}", src2d, nk)
            self.issued += 1
        sl = i % 3
        nk = self.tiles[i][1]
        return self.wsl[sl].rearrange("p (k c) -> p k c", k=nk), self.Bw[sl]


_uid = [0]


def uch(prefix="c"):
    _uid[0] += 1
    return f"{prefix}{_uid[0]}"


class StopBuild(Exception):
    pass


def build_A(upto=None):
    C = Ctx()
    try:
        _build_A_body(C, upto)
    except StopBuild:
        pass
    C.P.emit(C.nc, C.st)
    C.st.close()
    return C.nc


def _build_A_body(C, upto):
    def gate(name):
        if upto == name:
            raise StopBuild()

    nc, P = C.nc, C.P
    io = C.io
    x = C.din("x", [1024, 2048])
    w_in = C.din("a_w_in", [2048, 12288])
    b_in = C.din("a_b_in", [1, 12288])
    vln_g = C.din("a_vln_g", [1, 4096])
    vln_b = C.din("a_vln_b", [1, 4096])
    w_s = C.din("a_w_s", [8, 128, 128])
    b_s = C.din("a_b_s", [1, 1024])
    w_out = C.din("a_w_out", [4096, 2048])
    kv_w = C.din("kv_w", [2048, 4096])
    if "fused" in io:
        ln_g, ln_b = io["ln_g"][0:1, :], io["ln_b"][0:1, :]
        x1_o = io["x1_scratch"]
    else:
        ln_g = C.din("ln_g", [1, 2048])
        ln_b = C.din("ln_b", [1, 2048])
        x1_o = C.dout("x1", [1024, 2048])
        kT_o = C.dout("kT", [16, 128, 1024], BF16)
        v_o = C.dout("v", [1024, 2048], BF16)

    xT = C.view(0, 32 * K, BF16, "p (k t) -> p k t", k=16)
    uT = [C.view(32 * K + i * 2 * K, 2 * K, BF16) for i in range(8)]
    sgT = [C.view(48 * K + i * 2 * K, 2 * K, BF16) for i in range(8)]
    z = C.view(0, 64 * K, F32, "p (a d) -> p a d", a=8)
    slab = C.view(64 * K, 64 * K, BF16, "p (f t c) -> p f t c", f=32, t=8)
    x1T = C.view(64 * K, 32 * K, BF16, "p (k t) -> p k t", k=16)
    lng = C.view(96 * K, 8 * K, F32)
    lnb = C.view(104 * K, 8 * K, F32)
    ost = [C.view(112 * K + i * 8 * K, 8 * K, F32) for i in range(2)]
    wsl = [C.view(128 * K + i * 16 * K, 16 * K, BF16) for i in range(3)]
    xs = C.view(176 * K, 8 * K, F32)
    tmp = [C.view(176 * K + i * 2 * K, 2 * K, F32) for i in range(4)]
    kst = [C.view(176 * K + i * 2 * K, 2 * K, BF16) for i in range(4)]

    Bxs = Buf("xs")
    Btmp = [Buf(f"tmp{i}") for i in range(4)]
    Bkst = [Buf(f"kst{i}") for i in range(4)]
    BxT = [Buf(f"xT{t}") for t in range(8)]
    Bu = [Buf(f"u{i}") for i in range(8)]
    Bsg = [Buf(f"sg{i}") for i in range(8)]
    Bz = [Buf(f"z{t}") for t in range(8)]
    Bslab = [[Buf(f"slab{f}_{h}") for h in range(2)] for f in range(32)]
    Bx1T = [Buf(f"x1T{t}") for t in range(8)]
    Bln = Buf("ln")
    Bost = [Buf("ost0"), Buf("ost1")]
    Bw = [Buf(f"w{i}") for i in range(3)]
    tiles = ([(w_in[:, 4096 + cb * 512:4096 + (cb + 1) * 512], 16) for cb in range(8)]
             + [t for g in range(8) for t in ((w_in[:, g * 512:(g + 1) * 512], 16),
                                              (w_in[:, 8192 + g * 512:8192 + (g + 1) * 512], 16))]
             + [(w_out[:, cb * 256:(cb + 1) * 256], 32) for cb in range(8)]
             + [(kv_w[:, ct * 512:(ct + 1) * 512], 16) for ct in range(8)])
    tiles = tiles + io.get("extra_tiles", [])
    WS = WStream(C, wsl, Bw, tiles)

    PT = C.psum4()
    banks = [PT[i // 2][:, (i % 2) * 512:(i % 2 + 1) * 512] for i in range(8)]
    Bbanks = [Buf(f"bank{i}") for i in range(8)]
    brot = Rot(8)

    ident, Bident = emit_consts(C)

    brow_u = C.sb([32, 128], F32, "brow_u")
    brow_g = C.sb([32, 128], F32, "brow_g")
    grow = C.sb([32, 128], F32, "grow")
    vbrow = C.sb([32, 128], F32, "vbrow")
    bs8 = C.sb([8, 128], F32, "bs8")
    bcol = C.sb([128, 64], F32, "bcol")
    gcol = C.sb([128, 32], F32, "gcol")
    CG = C.sb([128, 32, 2], F32, "CG")
    RB = C.sb([128, 8, 2], F32, "RB")
    Rab = [C.sb([2, 128], F32, f"Rab{i}") for i in range(4)]
    Rg = [C.sb([2, 128], F32, f"Rg{i}") for i in range(2)]
    Rd = C.sb([1, 640], F32, "Rd")
    Wsf = C.view(160 * K, 4 * K, F32, "p (g s) -> p g s", g=8)
    WsTf = C.view(164 * K, 4 * K, F32, "p (g s) -> p g s", g=8)
    WsT = C.sb([128, 8, 128], BF16, "WsT")
    epsc = C.sb([128, 1], F32, "epsc")
    stats = C.sb([128, 8, 8, 6], F32, "stats")
    mv = C.sb([128, 8, 2], F32, "mv")
    rstd = C.sb([128, 8], F32, "rstd")
    lstats = C.sb([128, 4, 6], F32, "lstats")
    lmv = C.sb([128, 2], F32, "lmv")
    lrstd = C.sb([128, 1], F32, "lrstd")
    lnbias = C.sb([128, 1], F32, "lnbias")
    lsets = [(lstats, lmv, lrstd, lnbias),
             (C.sb([128, 4, 6], F32, "lstats2"), C.sb([128, 2], F32, "lmv2"), C.sb([128, 1], F32, "lrstd2"),
              C.sb([128, 1], F32, "lnbias2"))]
    Blsets = [Buf("lsm0"), Buf("lsm1")]
    Bbrow, Bgrow, Bbcol, Bgcol = Buf("brow"), Buf("grow"), Buf("bcol"), Buf("gcol")
    BCG, BRB, BRab, BRg = Buf("CG"), Buf("RB"), [Buf(f"Rab{i}") for i in range(4)], [Buf("Rg0"), Buf("Rg1")]
    Bones, Bbv = Buf("ones"), Buf("bv")
    BWsf, BWsTf, BWsT, Beps = Bw[2], Bw[2], Buf("WsT"), Buf("eps")
    Bstats = [Buf(f"stats{t}") for t in range(8)]
    Bmv, Brstd, Blsm = Buf("mv"), Buf("rstd"), Buf("lsm")

    P.add("pool", f_memset(epsc[:], EPS), writes=[Beps])
    P.add("pool", f_memset(Rd[0:1, 512:640], 1.0), writes=[Bones])
    b_in96 = b_in.rearrange("o (f p) -> (o f) p", p=128)
    P.add("sp", f_dma(brow_u[:], b_in96[0:32, :]), writes=[Bbrow], dma=uch())
    P.add("sp", f_dma(brow_g[:], b_in96[64:96, :]), writes=[Bbrow], dma=uch())
    P.add("sp", f_dma(grow[:], vln_g.rearrange("o (f p) -> (o f) p", p=128)), writes=[Bgrow], dma=uch())
    P.add("sp", f_dma(vbrow[:], vln_b.rearrange("o (f p) -> (o f) p", p=128)), writes=[Bgrow], dma=uch())
    P.add("sp", f_dma(bs8[:], b_s.rearrange("o (g t) -> (o g) t", t=128)), writes=[Bgrow], dma=uch())
    P.add("sp", f_dma(Wsf[:], w_s.rearrange("g t s -> t g s")), writes=[BWsf], dma=uch())
    gate('c1')
    for (src_, nrow, dst_, Bsrc, Bd) in ((brow_u, 32, bcol[:, 0:32], Bbrow, Bbcol), (brow_g, 32, bcol[:, 32:64], Bbrow, Bbcol),
                                       (grow, 32, gcol[:, :], Bgrow, Bgcol), (vbrow, 32, CG[:, :, 0], Bgrow, BCG),
                                       (bs8, 8, RB[:, :, 1], Bgrow, BRB)):
        bi = brot()
        P.add("pe", f_tr(banks[bi][:, 0:nrow], src_[:, :], ident[0:nrow, 0:nrow]), reads=[Bsrc, Bident], writes=[Bbanks[bi]])
        P.add("dve", f_copy(dst_, banks[bi][:, 0:nrow]), reads=[Bbanks[bi]], writes=[Bd])
    P.add("dve", lambda e: e.reciprocal(out=CG[:, :, 1], in_=gcol[:, :]), reads=[Bgcol], writes=[BCG])
    P.add("dve", f_tt(CG[:, :, 0], CG[:, :, 0], CG[:, :, 1], ALU.mult), reads=[BCG], writes=[BCG])
    gate('c2')
    for g in range(8):
        P.add("pool", f_asel(Wsf[:, g, :], Wsf[:, g, :], ALU.is_ge, 0.0, [[-1, 128]], 1), reads=[BWsf], writes=[BWsf])
    P.add("dve", lambda e: e.tensor_reduce(out=RB[:, :, 0], in_=Wsf[:, :, :], axis=mybir.AxisListType.X, op=ALU.add),
          reads=[BWsf], writes=[BRB])
    gate('c3')
    for g in range(8):
        bi = brot()
        P.add("pe", f_tr(banks[bi][:, 0:128], Wsf[:, g, :], ident[:]), reads=[BWsf, Bident], writes=[Bbanks[bi]])
        P.add("dve", f_copy(WsT[:, g, :], banks[bi][:, 0:128]), reads=[Bbanks[bi]], writes=[BWsT])
    gate('c4')

    gate('consts')
    xs_b = C.view(32 * K, 8 * K, F32)
    Bxs_b = Buf("xs_b")
    for tb in range(8):
        xb, Bxb, ch = (xs, Bxs, "xs") if tb % 2 == 0 else (xs_b, Bxs_b, "xsb")
        P.add("sp", f_dma(xb[:], x[tb * 128:(tb + 1) * 128, :]), writes=[Bxb], dma=ch)
        emit_transpose_block(C, xb, Bxb, xT, BxT[tb], tb, ident, Bident, banks, Bbanks, brot)
    alias(Bu[0:4], [Bxs_b])

    gate('A0')
    alias(Btmp, [Bxs])
    it = 0
    for cb in range(8):
        wt, Bwt = WS.use(cb)
        P.add("sp", f_dma(Rd[0:1, 0:512], b_in[:, 4096 + cb * 512:4096 + (cb + 1) * 512]), writes=[Bbv], dma="bv")
        for tb in range(8):
            bi = brot()
            bank = banks[bi]
            for kc in range(16):
                P.add("pe", f_mm(bank[:, :], xT[:, kc, tb * 128:(tb + 1) * 128], wt[:, kc, :], kc == 0, False),
                      reads=[BxT[tb], Bwt], war=[Bbanks[bi]] if kc == 0 else [])
            P.add("pe", f_mm(bank[:, :], Rd[0:1, 512:640], Rd[0:1, 0:512], False, True),
                  reads=[Bones, Bbv], writes=[Bbanks[bi]])
            ti = it % 4
            it += 1
            P.add("act", f_act(tmp[ti], bank[:, :], AF.Gelu_apprx_tanh), reads=[Bbanks[bi]], writes=[Btmp[ti]])
            P.add("dve", lambda e, tb=tb, cb=cb, ti=ti: e.bn_stats(out=stats[:, tb, cb, :], in_=tmp[ti]),
                  reads=[Btmp[ti]], writes=[Bstats[tb]])
            P.add("pool", f_copy(slab[:, 4 * cb:4 * cb + 4, tb, :], tmp[ti].rearrange("p (a b) -> p a b", a=4)),
                  reads=[Btmp[ti]], writes=[Bslab[4 * cb + a][tb // 4] for a in range(4)])

    gate('A1')
    for tb in range(8):
        P.add("dve", lambda e, tb=tb: e.bn_aggr(out=mv[:, tb, :], in_=stats[:, tb, :, :].rearrange("p a b -> p (a b)")),
              reads=[Bstats[tb]], writes=[Bmv])
    gate('a2a')
    P.add("act", f_act(rstd[:, :], mv[:, :, 1], AF.Sqrt, bias=epsc[:], scale=1.0), reads=[Bmv, Beps], writes=[Brstd])
    P.add("dve", lambda e: e.reciprocal(out=rstd[:, :], in_=rstd[:, :]), reads=[Brstd], writes=[Brstd])
    gate('a2b')
    for tb in range(8):
        P.add("dve", f_ts(slab[:, :, tb, :], slab[:, :, tb, :], mv[:, tb, 0:1], rstd[:, tb:tb + 1], ALU.subtract, ALU.mult),
              reads=[Bmv, Brstd] + [Bslab[f][tb // 4] for f in range(32)],
              writes=[Bslab[f][tb // 4] for f in range(32)])

    gate('A2')
    for g in range(8):
        rg = Rg[g % 2]
        bi = brot()
        P.add("pe", f_tr(banks[bi][0:2, 0:128], RB[:, g, :], ident[:, :]), reads=[BRB, Bident], writes=[Bbanks[bi]])
        P.add("dve", f_copy(rg[:, :], banks[bi][0:2, 0:128]), reads=[Bbanks[bi]], writes=[BRg[g % 2]])
        for path in range(2):
            wt_, Bwt = WS.use(8 + 2 * g + path)
            func = AF.Gelu_apprx_tanh if path == 0 else AF.Silu
            boff = 0 if path == 0 else 32
            for fcl in range(4):
                fc = g * 4 + fcl
                s8 = fc % 8
                dstbuf, Bdst = (uT[s8], Bu[s8]) if path == 0 else (sgT[s8], Bsg[s8])
                for th in range(2):
                    bi = brot()
                    bank = banks[bi]
                    for kc in range(16):
                        P.add("pe", f_mm(bank[:, :], wt_[:, kc, fcl * 128:(fcl + 1) * 128],
                                         xT[:, kc, th * 512:(th + 1) * 512], kc == 0, kc == 15),
                              reads=[Bwt] + BxT[th * 4:(th + 1) * 4],
                              war=[Bbanks[bi]] if kc == 0 else [], writes=[Bbanks[bi]] if kc == 15 else [])
                    P.add("act", f_act(dstbuf[:, th * 512:(th + 1) * 512], bank[:, :], func,
                                       bias=bcol[:, boff + fc:boff + fc + 1]),
                          reads=[Bbanks[bi], Bbcol], writes=[Bdst])
        for fcl in range(4):
            fc = g * 4 + fcl
            s8 = fc % 8
            ra = Rab[fc % 4]
            bi = brot()
            P.add("pe", f_tr(banks[bi][0:2, 0:128], CG[:, fc, :], ident[:, :]), reads=[BCG, Bident], writes=[Bbanks[bi]])
            P.add("dve", f_copy(ra[:, :], banks[bi][0:2, 0:128]), reads=[Bbanks[bi]], writes=[BRab[fc % 4]])
            P.add("pool", f_tt(uT[s8], uT[s8], sgT[s8], ALU.mult),
                  reads=[Bu[s8], Bsg[s8]], writes=[Bu[s8]])
            for half in range(2):
                bi = brot()
                bank = banks[bi]
                for tbl in range(4):
                    tb = half * 4 + tbl
                    cols = bank[:, tbl * 128:(tbl + 1) * 128]
                    P.add("pe", f_mm(cols, slab[:, fc, tb, :], WsT[:, g, :], True, False),
                          reads=[Bslab[fc][half], BWsT], war=[Bbanks[bi]] if tbl == 0 else [])
                    P.add("pe", f_mm(cols, ra[:, :], rg[:, :], False, True),
                          reads=[BRab[fc % 4], BRg[g % 2]], writes=[Bbanks[bi]] if tbl == 3 else [])
                sview = slab[:, fc, half * 4:(half + 1) * 4, :].rearrange("p a b -> p (a b)")
                P.add("dve", f_stt(sview, bank[:, :], gcol[:, fc:fc + 1], uT[s8][:, half * 512:(half + 1) * 512],
                                   ALU.mult, ALU.mult),
                      reads=[Bbanks[bi], Bu[s8], Bgcol], writes=[Bslab[fc][half]])

    gate('A3')
    alias(Bz, BxT + Bu + Bsg)
    for cb in range(8):
        wo, Bwo = WS.use(24 + cb)
        for tb in range(8):
            bi = brot()
            bank = banks[bi]
            half = tb // 4
            for kc in range(32):
                P.add("pe", f_mm(bank[:, 0:256], slab[:, kc, tb, :], wo[:, kc, :], kc == 0, kc == 31),
                      reads=[Bslab[kc][half], Bwo],
                      war=[Bbanks[bi]] if kc == 0 else [], writes=[Bbanks[bi]] if kc == 31 else [])
            dst = z[:, tb, cb * 256:(cb + 1) * 256]
            if (cb * 8 + tb) % 2 == 0:
                P.add("act", f_act(dst, bank[:, 0:256], AF.Copy), reads=[Bbanks[bi]], writes=[Bz[tb]])
            else:
                P.add("dve", f_copy(dst, bank[:, 0:256]), reads=[Bbanks[bi]], writes=[Bz[tb]])

    gate('A4')
    allslab = [b for fb in Bslab for b in fb]
    alias([Bln] + Bost + Bx1T, allslab)
    alias([Bxs], Btmp)
    P.add("sp", f_dma(lng, ln_g.partition_broadcast(128)), writes=[Bln], dma="c1")
    P.add("sp", f_dma(lnb, ln_b.partition_broadcast(128)), writes=[Bln], dma="c1")
    outs = []
    for tb in range(8):
        o = ost[tb % 2]
        od = emit_ln_block(C, z[:, tb, :], Bz[tb], xs, Bxs, x, tb, lng, lnb, Bln, o, Bost[tb % 2],
                           lsets[tb % 2], Blsets[tb % 2], epsc, Beps, f"o{tb % 2}", x1_o)
        outs.append(od)
        if "fused" in io:
            io["Bx1d"][tb].writers = [od]
        if tb >= 1:
            emit_transpose_block(C, ost[(tb - 1) % 2], Bost[(tb - 1) % 2], x1T, Bx1T[tb - 1], tb - 1, ident, Bident,
                                 banks, Bbanks, brot)
    emit_transpose_block(C, ost[7 % 2], Bost[7 % 2], x1T, Bx1T[7], 7, ident, Bident, banks, Bbanks, brot)

    gate('A5')
    alias(Bkst, [Bxs])
    ki = 0
    for ct in range(8):
        wt, Bwt = WS.use(32 + ct)
        if ct < 4:
            for hh in range(4):
                h = ct * 4 + hh
                si = ki % 4
                ki += 1
                for th in range(2):
                    bi = brot()
                    bank = banks[bi]
                    for kc in range(16):
                        P.add("pe", f_mm(bank[:, :], wt[:, kc, hh * 128:(hh + 1) * 128], x1T[:, kc, th * 512:(th + 1) * 512],
                                         kc == 0, kc == 15),
                              reads=[Bwt] + Bx1T[th * 4:(th + 1) * 4],
                              war=[Bbanks[bi]] if kc == 0 else [], writes=[Bbanks[bi]] if kc == 15 else [])
                    if th == 0:
                        P.add("act", f_act(kst[si][:, 0:512], bank[:, :], AF.Copy), reads=[Bbanks[bi]], writes=[Bkst[si]])
                    else:
                        P.add("dve", f_copy(kst[si][:, 512:1024], bank[:, :]), reads=[Bbanks[bi]], writes=[Bkst[si]])
                if "fused" in io:
                    kd = P.add("sp", f_dma(io["kloc"][ct][hh * 128:(hh + 1) * 128, :], kst[si]), reads=[Bkst[si]], dma=f"k{si}")
                    io["Bkloc"][ct].writers.append(kd)
                else:
                    outs.append(P.add("sp", f_dma(kT_o[h], kst[si]), reads=[Bkst[si]], dma=f"k{si}"))
        else:
            for tbp in range(4):
                si = ki % 4
                ki += 1
                for t2 in range(2):
                    tb = tbp * 2 + t2
                    bi = brot()
                    bank = banks[bi]
                    for kc in range(16):
                        P.add("pe", f_mm(bank[:, :], x1T[:, kc, tb * 128:(tb + 1) * 128], wt[:, kc, :], kc == 0, kc == 15),
                              reads=[Bwt, Bx1T[tb]],
                              war=[Bbanks[bi]] if kc == 0 else [], writes=[Bbanks[bi]] if kc == 15 else [])
                    if t2 == 0:
                        P.add("act", f_act(kst[si][:, 0:512], bank[:, :], AF.Copy), reads=[Bbanks[bi]], writes=[Bkst[si]])
                    else:
                        P.add("dve", f_copy(kst[si][:, 512:1024], bank[:, :]), reads=[Bbanks[bi]], writes=[Bkst[si]])
                c0 = (ct - 4) * 512
                if "fused" in io:
                    dst = io["vloc"][ct - 4][tbp * 256:(tbp + 1) * 256, :].rearrange("(a p) c -> p a c", p=128)
                    vd = P.add("sp", f_dma(dst, kst[si].rearrange("p (a c) -> p a c", a=2)), reads=[Bkst[si]], dma=f"k{si}")
                    io["Bvloc"][ct - 4].writers.append(vd)
                else:
                    dst = v_o[tbp * 256:(tbp + 1) * 256, c0:c0 + 512].rearrange("(a p) c -> p a c", p=128)
                    outs.append(P.add("sp", f_dma(dst, kst[si].rearrange("p (a c) -> p a c", a=2)), reads=[Bkst[si]], dma=f"k{si}"))
        if "fused" in io and ct in (0, 4):
            io["gather"](ct)
    if "fused" not in io:
        P.add("sp", None, after=outs)
    else:
        io.update(A_Bz=Bz, A_Bx1T=Bx1T, A_x1T=x1T, A_Bln=Bln, A_Bost=Bost, A_Bkst=Bkst, A_Bxs=Bxs,
                  banks=banks, Bbanks=Bbanks, brot=brot, WS=WS)


def blocks_of(r):
    out = []
    for m in range(4):
        out += [8 * m + r, 8 * m + 7 - r]
    return out


def shard_tokens(x):
    res = []
    for c in range(8):
        b, r = divmod(c, 4)
        xb = x[b].reshape(32, 128, -1)
        res.append(np.ascontiguousarray(xb[blocks_of(r)].reshape(1024, -1)))
    return res


def common_A(inputs):
    return {
        "a_w_in": np.ascontiguousarray(inputs["a_w_in"][0]),
        "a_b_in": np.ascontiguousarray(inputs["a_b_in"][0].reshape(1, 12288)),
        "a_vln_g": np.ascontiguousarray(inputs["a_vln_g"][0].reshape(1, 4096)),
        "a_vln_b": np.ascontiguousarray(inputs["a_vln_b"][0].reshape(1, 4096)),
        "a_w_s": np.ascontiguousarray(inputs["a_w_s"][0]),
        "a_b_s": np.ascontiguousarray(inputs["a_b_s"][0].reshape(1, 1024)),
        "a_w_out": np.ascontiguousarray(inputs["a_w_out"][0]),
        "kv_w": np.ascontiguousarray(inputs["kv_w"]),
        "ln_g": np.ascontiguousarray(inputs["ln_g"][0].reshape(1, 2048)),
        "ln_b": np.ascontiguousarray(inputs["ln_b"][0].reshape(1, 2048)),
    }


def run_A(inputs, upto=None):
    nc = build_A(upto)
    xs = shard_tokens(np.asarray(inputs["x"], dtype=np.float32))
    common = common_A(inputs)
    in_maps = [dict(common, x=xs[c]) for c in range(8)]
    res = run_bass_kernel_spmd(nc, in_maps, core_ids=list(range(8)))
    return res.results


def build_B(upto=None):
    C = Ctx()
    try:
        _build_B_body(C, upto)
    except StopBuild:
        pass
    C.P.emit(C.nc, C.st)
    C.st.close()
    return C.nc


def _build_B_body(C, upto):
    def gate(name):
        if upto == name:
            raise StopBuild()

    nc, P = C.nc, C.P
    io = C.io
    fused = "fused" in io
    masks_d = C.din("masks", [128, 32 * 128], BF16)
    if fused:
        w_in, w_out = io["b_w_in"], io["b_w_out"]
    else:
        w_in = C.din("b_w_in", [2048, 4096])
        w_out = C.din("b_w_out", [2048, 2048])
    out_o = C.dout("out", [1024, 2048])
    if fused:
        x1 = io["x1_scratch"]
        ln_g, ln_b = io["ln_g"][1:2, :], io["ln_b"][1:2, :]
    else:
        x1 = C.din("x1", [1024, 2048])
        kTf = C.din("kTf", [16, 128, 4096], BF16)
        vf = C.din("vf", [4096, 2048], BF16)
        ln_g = C.din("ln_g", [1, 2048])
        ln_b = C.din("ln_b", [1, 2048])

    def pos(kb):
        if not fused:
            return kb
        m, o = divmod(kb, 8)
        return (o * 8 + 2 * m) if o < 4 else ((7 - o) * 8 + 2 * m + 1)

    if fused:
        oX, oQ, oZ, oSG, oKV = 64 * K, 32 * K, 32 * K, 0, 96 * K
    else:
        oX, oQ, oZ, oSG, oKV = 0, 32 * K, 0, 64 * K, 144 * K
    x1T = C.view(oX, 32 * K, BF16, "p (k t) -> p k t", k=16)
    ebuf = [C.view(oX + i * 4 * K, 4 * K, F32) for i in range(2)]
    spb = [C.view(oX + 8 * K + i * 2 * K, 2 * K, BF16) for i in range(2)]
    Sb = C.view(oX + 12 * K, 2 * K, BF16)
    wb = [C.view(oX + 14 * K + i * 2 * K, 2 * K, BF16) for i in range(2)]
    qT = C.view(oQ, 32 * K, BF16, "p (h t) -> p h t", h=16)
    z = C.view(oZ, 64 * K, F32, "p (a d) -> p a d", a=8)
    sgT = C.view(oSG, 32 * K, BF16, "p (h t) -> p h t", h=16)
    wsl = [C.view(96 * K + i * 16 * K, 16 * K, BF16) for i in range(3)]
    KTs = [C.view(oKV + i * 8 * K, 8 * K, BF16) for i in range(2)]
    Vs = [C.view(oKV + 16 * K + i * 8 * K, 8 * K, BF16, "p (b d) -> p b d", b=32) for i in range(2)]
    lng = C.view(oKV, 8 * K, F32)
    lnb = C.view(oKV + 8 * K, 8 * K, F32)
    ost = [C.view(oKV + 16 * K + i * 8 * K, 8 * K, F32) for i in range(2)]
    xs = C.view(176 * K, 8 * K, F32)

    Bxs = Buf("xs")
    Bx1T = [Buf(f"x1T{t}") for t in range(8)]
    Bq = [Buf(f"q{h}") for h in range(16)]
    Bsg = [Buf(f"sg{h}") for h in range(16)]
    Bz = [Buf(f"z{t}") for t in range(8)]
    Bw = [Buf(f"w{i}") for i in range(3)]
    BKT = [Buf("KT0"), Buf("KT1")]
    BV = [Buf("V0"), Buf("V1")]
    Be = [Buf("e0"), Buf("e1")]
    Bsp = [Buf("sp0"), Buf("sp1")]
    BS = Buf("S")
    Bwb = [Buf("wb0"), Buf("wb1")]
    Bln = Buf("ln")
    Bost = [Buf("ost0"), Buf("ost1")]

    PT = C.psum4()
    if fused:
        banks, Bbanks, brot = io["banks"], io["Bbanks"], io["brot"]
        Bx1T = io["A_Bx1T"]
        alias(Bsg + Bq, io["A_Bz"])
        alias(BKT + BV, [io["A_Bln"]] + io["A_Bost"])
        alias([Bxs], io["A_Bkst"] + [io["A_Bxs"]])
    else:
        banks = [PT[i // 2][:, (i % 2) * 512:(i % 2 + 1) * 512] for i in range(8)]
        Bbanks = [Buf(f"bank{i}") for i in range(8)]
        brot = Rot(8)

    ident, Bident = emit_consts(C)
    masks = C.sb([128, 32, 128], BF16, "masks_sb")
    negU = C.sb([128, 128], BF16, "negU")
    negO = C.sb([128, 128], BF16, "negO")
    epsc = C.sb([128, 1], F32, "epsc")
    lstats = C.sb([128, 4, 6], F32, "lstats")
    lmv = C.sb([128, 2], F32, "lmv")
    lrstd = C.sb([128, 1], F32, "lrstd")
    lnbias = C.sb([128, 1], F32, "lnbias")
    lsets = [(lstats, lmv, lrstd, lnbias),
             (C.sb([128, 4, 6], F32, "lstats2"), C.sb([128, 2], F32, "lmv2"), C.sb([128, 1], F32, "lrstd2"),
              C.sb([128, 1], F32, "lnbias2"))]
    Blsets = [Buf("lsm0"), Buf("lsm1")]
    Bmasks, BnegU, BnegO, Beps, Blsm = Buf("masks"), Buf("negU"), Buf("negO"), Buf("eps"), Buf("lsm")
    P.add("sp", f_dma(masks[:].rearrange("p a b -> p (a b)"), masks_d), writes=[Bmasks], dma=uch())
    P.add("pool", f_memset(epsc[:], EPS), writes=[Beps])
    P.add("pool", f_memset(negO[:], -1.0), writes=[BnegO])
    P.add("pool", f_memset(negU[:], -1.0), writes=[BnegU])
    P.add("pool", f_asel(negU[:], negU[:], ALU.is_ge, 0.0, [[-1, 128]], 1), reads=[BnegU], writes=[BnegU])

    if fused:
        WS, wbase = io["WS"], 40
    else:
        tiles = ([(w_in[:, ct * 512:(ct + 1) * 512], 16) for ct in range(8)]
                 + [(w_out[:, cb * 512:(cb + 1) * 512], 16) for cb in range(4)])
        WS, wbase = WStream(C, wsl, Bw, tiles), 0
        emit_load_T(C, x1, xs, Bxs, "xs", x1T, Bx1T, ident, Bident, banks, Bbanks, brot)
    gate('B0')

    for ct in range(8):
        wt, Bwt = WS.use(wbase + ct)
        if fused and ct >= 2:
            io["gather"]([0, 4, 1, 5, 2, 6, 3, 7][ct])
        for hh in range(4):
            h = (ct % 4) * 4 + hh
            for th in range(2):
                bi = brot()
                bank = banks[bi]
                for kc in range(16):
                    P.add("pe", f_mm(bank, wt[:, kc, hh * 128:(hh + 1) * 128], x1T[:, kc, th * 512:(th + 1) * 512],
                                     kc == 0, kc == 15),
                          reads=[Bwt] + Bx1T[th * 4:(th + 1) * 4],
                          war=[Bbanks[bi]] if kc == 0 else [], writes=[Bbanks[bi]] if kc == 15 else [])
                if ct < 4:
                    P.add("act", f_act(qT[:, h, th * 512:(th + 1) * 512], bank, AF.Copy, scale=128.0 ** -0.5),
                          reads=[Bbanks[bi]], writes=[Bq[h]])
                else:
                    P.add("act", f_act(sgT[:, h, th * 512:(th + 1) * 512], bank, AF.Silu),
                          reads=[Bbanks[bi]], writes=[Bsg[h]])
    gate('B1')

    Z = PT[0:3]
    BZ = [Buf("Z0"), Buf("Z1"), Buf("Z2")]
    OT = PT[3]
    BOT = Buf("OT")
    alias(BZ + [BOT], Bbanks)
    alias(Be + Bsp + [BS] + Bwb, Bx1T)
    its = [(h, kb) for h in range(16) for kb in range(31, -1, -1)]
    n = len(its)

    def segs(c0, c1):
        out = []
        if c0 < 512:
            out.append((c0, min(c1, 512)))
        if c1 > 512:
            out.append((max(c0, 512), c1))
        return [(a, b) for a, b in out if b > a]

    def load_head(h):
        hs = h % 2
        if fused:
            ct, hh = divmod(h, 4)
            ksrc = io["kall"][ct].rearrange("(r q) t -> q r t", q=512)[hh * 128:(hh + 1) * 128, :, :]
            P.add("sp", f_dma(KTs[hs].rearrange("p (r t) -> p r t", r=4), ksrc), reads=[io["Bkall"][ct]],
                  writes=[BKT[hs]], dma=f"kt{hs}")
            src = io["vall"][ct].rearrange("(b p) c -> p b c", p=128)[:, :, hh * 128:(hh + 1) * 128]
            rd = [io["Bvall"][ct]]
        else:
            P.add("sp", f_dma(KTs[hs], kTf[h]), writes=[BKT[hs]], dma=f"kt{hs}")
            src = vf[:, h * 128:(h + 1) * 128].rearrange("(b p) d -> p b d", p=128)
            rd = []
        for q4 in range(4):
            P.add("sp", f_dma(Vs[hs][:, q4 * 8:(q4 + 1) * 8, :], src[:, q4 * 8:(q4 + 1) * 8, :]), reads=rd,
                  writes=[BV[hs]], dma=f"v{hs}")

    def QK(i):
        h, kb = its[i]
        if kb == 31 and h == 0:
            load_head(0)
            load_head(1)
        c0 = (kb // 4) * 128
        s = i % 3
        sg_ = segs(c0, 1024)
        for k, (a, b) in enumerate(sg_):
            P.add("pe", f_mm(Z[s][:, a:b], KTs[h % 2][:, pos(kb) * 128:(pos(kb) + 1) * 128], qT[:, h, a:b], True, False),
                  reads=[BKT[h % 2], Bq[h]], war=[BZ[s]] if k == 0 else [], writes=[BZ[s]] if k == len(sg_) - 1 else [])

    def E(i):
        h, kb = its[i]
        c0 = (kb // 4) * 128
        P.add("act", f_act(ebuf[i % 2][:, c0:1024], Z[i % 3][:, c0:1024], AF.Exp), reads=[BZ[i % 3]], writes=[Be[i % 2]])

    def L(i):
        h, kb = its[i]
        j0, mi = kb // 4, kb % 4
        c0 = j0 * 128
        P.add("act", f_act(spb[i % 2][:, c0:1024], ebuf[i % 2][:, c0:1024], AF.Ln, bias=1.0), reads=[Be[i % 2]], writes=[Bsp[i % 2]])
        P.add("dve", f_tt(spb[i % 2][:, c0:c0 + 128], spb[i % 2][:, c0:c0 + 128], masks[:, j0 * 4 + mi, :], ALU.mult),
              reads=[Bsp[i % 2], Bmasks], writes=[Bsp[i % 2]])

    def US(i):
        h, kb = its[i]
        j0, mi = kb // 4, kb % 4
        c0 = j0 * 128
        s = i % 3
        ops = [(negU, BnegU, Bsp[i % 2], spb[i % 2], a, b) for a, b in segs(c0, 1024)]
        cS = c0 + 128 if mi == 3 else c0
        ops += [(negO, BnegO, BS, Sb, a, b) for a, b in segs(cS, 1024)]
        for k, (lh, Blh, Brh, rh, a, b) in enumerate(ops):
            P.add("pe", f_mm(Z[s][:, a:b], lh[:, :], rh[:, a:b], False, True),
                  reads=[Blh, Brh], war=[BZ[s]] if k == 0 else [], writes=[BZ[s]] if k == len(ops) - 1 else [])
        if mi == 3:
            P.add("dve", f_copy(Sb[:, c0:c0 + 128], spb[i % 2][:, c0:c0 + 128]), reads=[Bsp[i % 2]], writes=[BS])
        if cS < 1024:
            P.add("dve", f_tt(Sb[:, cS:1024], Sb[:, cS:1024], spb[i % 2][:, cS:1024], ALU.add), reads=[Bsp[i % 2], BS], writes=[BS])

    def W(i):
        h, kb = its[i]
        j0, mi = kb // 4, kb % 4
        c0 = j0 * 128
        P.add("act", f_act(wb[i % 2][:, c0:1024], Z[i % 3][:, c0:1024], AF.Exp), reads=[BZ[i % 3]], writes=[Bwb[i % 2]])
        P.add("dve", f_tt(wb[i % 2][:, c0:c0 + 128], wb[i % 2][:, c0:c0 + 128], masks[:, j0 * 4 + mi, :], ALU.mult),
              reads=[Bwb[i % 2], Bmasks], writes=[Bwb[i % 2]])

    def PV(i):
        h, kb = its[i]
        c0 = (kb // 4) * 128
        if kb == 31:
            if 1 <= h < 15:
                load_head(h + 1)
            P.add("dve", f_memset(OT[:, :], 0.0), writes=[BOT])
        sg_ = segs(c0, 1024)
        for k, (a, b) in enumerate(sg_):
            P.add("pe", f_mm(OT[:, a:b], Vs[h % 2][:, pos(kb), :], wb[i % 2][:, a:b], False, True),
                  reads=[BV[h % 2], Bwb[i % 2]], war=[BOT] if k == 0 else [], writes=[BOT] if k == len(sg_) - 1 else [])
        if kb == 0:
            for hf in range(2):
                P.add("dve", f_tt(sgT[:, h, hf * 512:(hf + 1) * 512], OT[:, hf * 512:(hf + 1) * 512],
                                  sgT[:, h, hf * 512:(hf + 1) * 512], ALU.mult),
                      reads=[BOT, Bsg[h]], writes=[Bsg[h], BOT])

    nit = n
    QK(0)
    E(0)
    L(0)
    for i in range(nit):
        if i + 1 < nit:
            QK(i + 1)
            E(i + 1)
        if i - 1 >= 0:
            W(i - 1)
        if i + 1 < nit:
            L(i + 1)
        US(i)
        if i - 1 >= 0:
            PV(i - 1)
    W(nit - 1)
    PV(nit - 1)
    gate('B2')

    alias(Bz, Bq + Be + Bsp + [BS] + Bwb + Bx1T)
    alias(Bbanks, BZ + [BOT])
    for cb in range(4):
        wo, Bwo = WS.use(wbase + 8 + cb)
        for tb in range(8):
            bi = brot()
            bank = banks[bi]
            for kc in range(16):
                P.add("pe", f_mm(bank, sgT[:, kc, tb * 128:(tb + 1) * 128], wo[:, kc, :], kc == 0, kc == 15),
                      reads=[Bsg[kc], Bwo], war=[Bbanks[bi]] if kc == 0 else [], writes=[Bbanks[bi]] if kc == 15 else [])
            dst = z[:, tb, cb * 512:(cb + 1) * 512]
            if (cb * 8 + tb) % 2 == 0:
                P.add("act", f_act(dst, bank, AF.Copy), reads=[Bbanks[bi]], writes=[Bz[tb]])
            else:
                P.add("dve", f_copy(dst, bank), reads=[Bbanks[bi]], writes=[Bz[tb]])
    gate('B4')

    alias([Bln] + Bost, BKT + BV)
    P.add("sp", f_dma(lng, ln_g.partition_broadcast(128)), writes=[Bln], dma="c1")
    P.add("sp", f_dma(lnb, ln_b.partition_broadcast(128)), writes=[Bln], dma="c1")
    outs = []
    for tb in range(8):
        o = ost[tb % 2]
        outs.append(emit_ln_block(C, z[:, tb, :], Bz[tb], xs, Bxs, x1, tb, lng, lnb, Bln, o, Bost[tb % 2],
                                  lsets[tb % 2], Blsets[tb % 2], epsc, Beps, f"o{tb % 2}", out_o))
    P.add("sp", None, after=outs)


def make_masks(r):
    qb = blocks_of(r)
    m = np.zeros((128, 32, 128), np.float32)
    tri = (np.arange(128)[:, None] < np.arange(128)[None, :]).astype(np.float32)
    for j in range(8):
        for mi in range(4):
            kb = 4 * j + mi
            if kb < qb[j]:
                m[:, j * 4 + mi, :] = 1.0
            elif kb == qb[j]:
                m[:, j * 4 + mi, :] = tri
    return np.ascontiguousarray(m.reshape(128, 32 * 128)).astype(ml_dtypes.bfloat16)


def assemble_kv(resA):
    kTf = [np.zeros((16, 128, 4096), ml_dtypes.bfloat16) for _ in range(2)]
    vf = [np.zeros((4096, 2048), ml_dtypes.bfloat16) for _ in range(2)]
    for c in range(8):
        b, r = divmod(c, 4)
        for j, qb in enumerate(blocks_of(r)):
            kTf[b][:, :, qb * 128:(qb + 1) * 128] = resA[c]["kT"][:, :, j * 128:(j + 1) * 128]
            vf[b][qb * 128:(qb + 1) * 128, :] = resA[c]["v"][j * 128:(j + 1) * 128, :]
    return kTf, vf


def common_B(inputs):
    return {
        "b_w_in": np.ascontiguousarray(inputs["b_w_in"][0]),
        "b_w_out": np.ascontiguousarray(inputs["b_w_out"][0]),
        "ln_g": np.ascontiguousarray(inputs["ln_g"][1].reshape(1, 2048)),
        "ln_b": np.ascontiguousarray(inputs["ln_b"][1].reshape(1, 2048)),
    }


def unshard_tokens(outs):
    res = np.zeros((2, 32, 128, 2048), np.float32)
    for c in range(8):
        b, r = divmod(c, 4)
        res[b, blocks_of(r)] = outs[c].reshape(8, 128, 2048)
    return res.reshape(2, 4096, 2048)


def build_fused():
    C = Ctx()
    nc, P = C.nc, C.P
    io = C.io
    io["fused"] = True
    io["ln_g"] = C.din("ln_g", [2, 2048])
    io["ln_b"] = C.din("ln_b", [2, 2048])
    io["x1_scratch"] = nc.dram_tensor("x1_scratch", [1024, 2048], F32).ap()
    kloc = [nc.dram_tensor(f"kloc{i}", [512, 1024], BF16) for i in range(4)]
    vloc = [nc.dram_tensor(f"vloc{i}", [1024, 512], BF16) for i in range(4)]
    kall = [nc.dram_tensor(f"kall{i}", [2048, 1024], BF16) for i in range(4)]
    vall = [nc.dram_tensor(f"vall{i}", [4096, 512], BF16) for i in range(4)]
    io["kloc"] = [t.ap() for t in kloc]
    io["vloc"] = [t.ap() for t in vloc]
    io["kall"] = [t.ap() for t in kall]
    io["vall"] = [t.ap() for t in vall]
    io["Bkloc"] = [Buf(f"kloc{i}") for i in range(4)]
    io["Bvloc"] = [Buf(f"vloc{i}") for i in range(4)]
    io["Bkall"] = [Buf(f"kall{i}") for i in range(4)]
    io["Bvall"] = [Buf(f"vall{i}") for i in range(4)]
    io["Bx1d"] = [Buf(f"x1d{t}") for t in range(8)]
    groups = [[0, 1, 2, 3], [4, 5, 6, 7]]

    def gather(ct):
        if ct < 4:
            src, dst, Bs, Bd = kloc[ct], kall[ct], io["Bkloc"][ct], io["Bkall"][ct]
        else:
            src, dst, Bs, Bd = vloc[ct - 4], vall[ct - 4], io["Bvloc"][ct - 4], io["Bvall"][ct - 4]
        P.add("pool", lambda e: e.collective_compute("AllGather", ALU.bypass, replica_groups=groups,
                                                     ins=[src.ap().opt()], outs=[dst.ap().opt()]),
              reads=[Bs], writes=[Bd], dma=f"cc{ct}", step=1)

    io["gather"] = gather
    io["b_w_in"] = C.din("b_w_in", [2048, 4096])
    io["b_w_out"] = C.din("b_w_out", [2048, 2048])
    io["extra_tiles"] = ([(io["b_w_in"][:, ct * 512:(ct + 1) * 512], 16) for ct in range(8)]
                         + [(io["b_w_out"][:, cb * 512:(cb + 1) * 512], 16) for cb in range(4)])
    C.prefix = "A_"
    _build_A_body(C, None)
    io["Bx1d_rd"] = {tb: [io["Bx1d"][tb]] for tb in range(8)}
    C.prefix = "B_"
    _build_B_body(C, None)
    P.emit(nc, C.st)
    C.st.close()
    return nc


def kernel(x, a_w_in, a_b_in, a_vln_g, a_vln_b, a_w_s, a_b_s, a_w_out, kv_w, b_w_in, b_w_out, ln_g, ln_b):
    inputs = dict(x=np.asarray(x), a_w_in=np.asarray(a_w_in), a_b_in=np.asarray(a_b_in), a_vln_g=np.asarray(a_vln_g),
                  a_vln_b=np.asarray(a_vln_b), a_w_s=np.asarray(a_w_s), a_b_s=np.asarray(a_b_s),
                  a_w_out=np.asarray(a_w_out), kv_w=np.asarray(kv_w), b_w_in=np.asarray(b_w_in),
                  b_w_out=np.asarray(b_w_out), ln_g=np.asarray(ln_g), ln_b=np.asarray(ln_b))
    nc = build_fused()
    in_maps = fused_in_maps(inputs)
    res = run_bass_kernel_spmd(nc, in_maps, core_ids=list(range(8))).results
    return unshard_tokens([res[c]["out"] for c in range(8)])


def fused_in_maps(inputs):
    xs = shard_tokens(np.asarray(inputs["x"], dtype=np.float32))
    common = common_A(inputs)
    common.update(common_B(inputs))
    common["ln_g"] = np.ascontiguousarray(inputs["ln_g"], dtype=np.float32)
    common["ln_b"] = np.ascontiguousarray(inputs["ln_b"], dtype=np.float32)
    return [dict(common, x=xs[c], masks=make_masks(c % 4)) for c in range(8)]
```

```python
from contextlib import ExitStack

import ml_dtypes
import numpy as np

import concourse.bass as bass
import concourse.mybir as mybir
from concourse.bass_utils import run_bass_kernel_spmd

F32 = mybir.dt.float32
BF16 = mybir.dt.bfloat16
AF = mybir.ActivationFunctionType
ALU = mybir.AluOpType

ALPHA = 4.0 ** 0.25
EPS = 1e-5
ENGS = ("pe", "act", "dve", "pool", "sp")


class Buf:
    __slots__ = ("name", "writers", "readers")

    def __init__(self, name):
        self.name = name
        self.writers = []
        self.readers = []


class Op:
    __slots__ = ("eng", "fn", "deps", "inc", "dma", "dma_val", "val", "step")

    def __init__(self, eng, fn, dma):
        self.eng = eng
        self.fn = fn
        self.deps = []
        self.inc = False
        self.dma = dma
        self.dma_val = 0
        self.val = 0


def alias(new_bufs, old_bufs):
    olds = []
    for b in old_bufs:
        for o in b.readers + b.writers:
            if o not in olds:
                olds.append(o)
    for nb in new_bufs:
        for o in olds:
            if o not in nb.readers:
                nb.readers.append(o)


class Prog:
    def __init__(self):
        self.ops = {e: [] for e in ENGS}
        self.dma_counts = {}
        self.last_dma = {}
        self.bar = []

    def barrier(self, skip_prefix="cc"):
        deps = []
        for e in ENGS:
            for op in reversed(self.ops[e]):
                if op.dma is None and op.fn is not None:
                    deps.append(op)
                    break
        for ch, op in self.last_dma.items():
            if not ch.startswith(skip_prefix):
                deps.append(op)
        self.bar = deps

    def add(self, eng, fn, reads=(), writes=(), war=(), dma=None, after=(), step=16):
        op = Op(eng, fn, dma)
        op.step = step
        deps = {}
        for b in reads:
            for w in b.writers:
                deps[w] = "raw"
        for b in list(writes) + list(war):
            for w in b.writers:
                deps.setdefault(w, "waw")
            for r in b.readers:
                deps.setdefault(r, "war")
        for a in list(after) + self.bar:
            if a is not None:
                deps[a] = "raw"
        for d, kind in deps.items():
            if d is op:
                continue
            if d.dma is not None:
                op.deps.append(d)
            elif d.eng == eng and dma is None and (kind == "war" or (kind == "waw" and eng == "pe")):
                continue
            else:
                d.inc = True
                op.deps.append(d)
        for b in reads:
            if dma is None:
                b.readers = [r for r in b.readers if not (r.eng == eng and r.dma is None)]
            b.readers.append(op)
        for b in writes:
            if b.readers:
                b.writers = [op]
                b.readers = []
            else:
                if dma is None:
                    b.writers = [w for w in b.writers if not (w.eng == eng and w.dma is None)]
                b.writers.append(op)
        if dma is not None:
            c = self.dma_counts.get(dma, 0) + step
            self.dma_counts[dma] = c
            op.dma_val = c
            self.last_dma[dma] = op
        self.ops[eng].append(op)
        return op

    def emit(self, nc, st):
        esem = {e: st.enter_context(nc.semaphore("s_" + e)) for e in ENGS}
        dsem = {c: st.enter_context(nc.semaphore("d_" + c)) for c in self.dma_counts}
        for e in ENGS:
            v = 0
            for op in self.ops[e]:
                if op.inc and op.dma is None:
                    v += 1
                    op.val = v
        block = st.enter_context(nc.Block())

        def run(e, eng):
            waited = {}
            for op in self.ops[e]:
                need = {}
                for d in op.deps:
                    if d.dma is not None:
                        key, val, sem = ("d", d.dma), d.dma_val, dsem[d.dma]
                    else:
                        key, val, sem = ("e", d.eng), d.val, esem[d.eng]
                    if val > need.get(key, (0, None))[0]:
                        need[key] = (val, sem)
                for key, (val, sem) in need.items():
                    if waited.get(key, 0) >= val:
                        continue
                    eng.wait_ge(sem, val)
                    waited[key] = val
                if op.fn is None:
                    continue
                ins = op.fn(eng)
                if op.dma is not None:
                    ins.then_inc(dsem[op.dma], op.step)
                elif op.inc:
                    ins.then_inc(esem[e], 1)

        @block.tensor
        def _(eng):
            run("pe", eng)

        @block.scalar
        def _(eng):
            run("act", eng)

        @block.vector
        def _(eng):
            run("dve", eng)

        @block.gpsimd
        def _(eng):
            run("pool", eng)

        @block.sync
        def _(eng):
            run("sp", eng)


class Rot:
    def __init__(self, n):
        self.n = n
        self.i = -1

    def __call__(self):
        self.i = (self.i + 1) % self.n
        return self.i


def f_mm(out, lhsT, rhs, start, stop):
    return lambda e: e.matmul(out, lhsT=lhsT, rhs=rhs, start=start, stop=stop, skip_group_check=True)


def f_tr(out, in_, ident):
    return lambda e: e.transpose(out, in_, ident)


def f_act(out, in_, func, bias=None, scale=1.0):
    if bias is None:
        return lambda e: e.activation(out=out, in_=in_, func=func, scale=scale)
    return lambda e: e.activation(out=out, in_=in_, func=func, bias=bias, scale=scale)


def f_copy(out, in_):
    return lambda e: e.tensor_copy(out=out, in_=in_)


def f_dma(out, in_):
    return lambda e: e.dma_start(out=out, in_=in_)


def f_tt(out, in0, in1, op):
    return lambda e: e.tensor_tensor(out=out, in0=in0, in1=in1, op=op)


def f_ts(out, in0, s1, s2, op0, op1):
    return lambda e: e.tensor_scalar(out=out, in0=in0, scalar1=s1, scalar2=s2, op0=op0, op1=op1)


def f_stt(out, in0, scalar, in1, op0, op1):
    return lambda e: e.scalar_tensor_tensor(out=out, in0=in0, scalar=scalar, in1=in1, op0=op0, op1=op1)


def f_memset(ap, v):
    return lambda e: e.memset(ap, v)


def f_asel(out, in_, cmp, fill, pattern, cm, base=0):
    return lambda e: e.affine_select(out=out, in_=in_, compare_op=cmp, fill=fill, base=base,
                                     pattern=pattern, channel_multiplier=cm)


class Ctx:
    def __init__(self, arena_bytes=184 * 1024):
        self.nc = bass.Bass("TRN2", target_bir_lowering=False)
        self.P = Prog()
        self.st = ExitStack()
        self.arena = self.st.enter_context(self.nc.sbuf_tensor("arena", [128, arena_bytes // 4], F32))
        self.nsmall = 0
        self.prefix = ""
        self.PT = None
        self.io = {}

    def view(self, off, nbytes, dt, pat=None, **kw):
        a = self.arena[:, off // 4:(off + nbytes) // 4]
        if dt is BF16:
            a = a.bitcast(BF16)
        if pat is not None:
            a = a.rearrange(pat, **kw)
        return a

    def sb(self, shape, dt, name=None):
        self.nsmall += 1
        return self.st.enter_context(self.nc.sbuf_tensor(self.prefix + (name or f"sm{self.nsmall}"), shape, dt))

    def psum4(self):
        if self.PT is None:
            self.PT = [self.ps([128, 1024], f"pt{i}") for i in range(4)]
        return self.PT

    def ps(self, shape, name):
        return self.st.enter_context(self.nc.psum_tensor(name, shape, F32))

    def din(self, name, shape, dt=F32):
        return self.nc.dram_tensor(name, shape, dt, kind="ExternalInput").ap()

    def dout(self, name, shape, dt=F32):
        return self.nc.dram_tensor(name, shape, dt, kind="ExternalOutput").ap()


K = 1024


def emit_consts(C):
    P = C.P
    if "ident" in C.io:
        return C.io["ident"]
    ident = C.sb([128, 128], F32, "ident")
    Bident = Buf("ident")
    P.add("pool", f_memset(ident[:], 1.0), writes=[Bident])
    P.add("pool", f_asel(ident[:], ident[:], ALU.is_equal, 0.0, [[-1, 128]], 1), reads=[Bident], writes=[Bident])
    C.io["ident"] = (ident, Bident)
    return ident, Bident


def emit_load_T(C, src, xs, Bxs, xsch, dstT, BdstT, ident, Bident, banks, Bbanks, rot):
    P = C.P
    for tb in range(8):
        P.add("sp", f_dma(xs[:], src[tb * 128:(tb + 1) * 128, :]), writes=[Bxs], dma=xsch)
        emit_transpose_block(C, xs, Bxs, dstT, BdstT[tb], tb, ident, Bident, banks, Bbanks, rot)


def emit_transpose_block(C, xs, Bxs, dstT, Bdst, tb, ident, Bident, banks, Bbanks, rot):
    P = C.P
    for q in range(4):
        bi = rot()
        bank = banks[bi]
        for jj in range(4):
            kc = q * 4 + jj
            P.add("pe", f_tr(bank[:, jj * 128:(jj + 1) * 128], xs[:, kc * 128:(kc + 1) * 128], ident[:]),
                  reads=[Bxs, Bident], writes=[Bbanks[bi]] if jj == 3 else [], war=[Bbanks[bi]] if jj == 0 else [])
        dst = dstT[:, q * 4:(q + 1) * 4, tb * 128:(tb + 1) * 128]
        srcv = bank[:, :].rearrange("p (a b) -> p a b", a=4)
        if q % 2 == 0:
            P.add("act", f_act(dst, srcv, AF.Copy), reads=[Bbanks[bi]], writes=[Bdst])
        else:
            P.add("dve", f_copy(dst, srcv), reads=[Bbanks[bi]], writes=[Bdst])


def emit_ln_load(C, xs, Bxs, x_src, tb):
    C.P.add("sp", f_dma(xs[:], x_src[tb * 128:(tb + 1) * 128, :]), reads=C.io.get("Bx1d_rd", {}).get(tb, []),
            writes=[Bxs], dma="xs")


def emit_ln_block(C, zt, Bz, xs, Bxs, x_src, tb, lng, lnb, Bln, ost, Bost, small, Bsm, epsc, Beps, och, out_dst):
    P = C.P
    stats, mv, rstd, nb = small
    if tb == 0:
        emit_ln_load(C, xs, Bxs, x_src, 0)
    P.add("dve", f_stt(zt, xs[:], ALPHA, zt, ALU.mult, ALU.add), reads=[Bxs, Bz], writes=[Bz])
    if tb + 1 < 8:
        emit_ln_load(C, xs, Bxs, x_src, tb + 1)
    for c in range(4):
        P.add("dve", lambda e, c=c: e.bn_stats(out=stats[:, c, :], in_=zt[:, c * 512:(c + 1) * 512]),
              reads=[Bz], writes=[Bsm])
    P.add("dve", lambda e: e.bn_aggr(out=mv[:], in_=stats[:].rearrange("p a b -> p (a b)")), reads=[Bsm], writes=[Bsm])
    P.add("act", f_act(rstd[:], mv[:, 1:2], AF.Sqrt, bias=epsc[:], scale=1.0), reads=[Bsm, Beps], writes=[Bsm])
    P.add("dve", lambda e: e.reciprocal(out=rstd[:], in_=rstd[:]), reads=[Bsm], writes=[Bsm])
    P.add("dve", f_stt(nb[:], mv[:, 0:1], -1.0, rstd[:], ALU.mult, ALU.mult), reads=[Bsm], writes=[Bsm])
    P.add("act", f_act(zt, zt, AF.Identity, bias=nb[:], scale=rstd[:]), reads=[Bz, Bsm], writes=[Bz])
    P.add("dve", f_tt(ost, zt, lng, ALU.mult), reads=[Bz, Bln], writes=[Bost])
    P.add("pool", f_tt(ost, ost, lnb, ALU.add), reads=[Bost, Bln], writes=[Bost])
    return P.add("sp", f_dma(out_dst[tb * 128:(tb + 1) * 128, :], ost), reads=[Bost], dma=och)


def load_wtile(C, dst, Bdst, ch, src2d, nk, split=4):
    P = C.P
    step = nk // split
    for s in range(split):
        P.add("pool", f_dma(dst[:, s * step:(s + 1) * step, :],
                            src2d[s * step * 128:(s + 1) * step * 128, :].rearrange("(k p) c -> p k c", p=128)),
              writes=[Bdst], dma=ch)


class WStream:
    def __init__(self, C, wsl, Bw, tiles):
        self.C, self.wsl, self.Bw, self.tiles = C, wsl, Bw, tiles
        self.issued = 0

    def use(self, i):
        while self.issued < min(i + 3, len(self.tiles)):
            j = self.issued
            src2d, nk = self.tiles[j]
            sl = j % 3
            dst = self.wsl[sl].rearrange("p (k c) -> p k c", k=nk)
            load_wtile(self.C, dst, self.Bw[sl], f"w{sl}", src2d, nk)
            self.issued += 1
        sl = i % 3
        nk = self.tiles[i][1]
        return self.wsl[sl].rearrange("p (k c) -> p k c", k=nk), self.Bw[sl]


_uid = [0]


def uch(prefix="c"):
    _uid[0] += 1
    return f"{prefix}{_uid[0]}"


class StopBuild(Exception):
    pass


def build_A(upto=None):
    C = Ctx()
    try:
        _build_A_body(C, upto)
    except StopBuild:
        pass
    C.P.emit(C.nc, C.st)
    C.st.close()
    return C.nc


def _build_A_body(C, upto):
    def gate(name):
        if upto == name:
            raise StopBuild()

    nc, P = C.nc, C.P
    io = C.io
    x = C.din("x", [1024, 2048])
    w_in = C.din("a_w_in", [2048, 12288])
    b_in = C.din("a_b_in", [1, 12288])
    vln_g = C.din("a_vln_g", [1, 4096])
    vln_b = C.din("a_vln_b", [1, 4096])
    w_s = C.din("a_w_s", [8, 128, 128])
    b_s = C.din("a_b_s", [1, 1024])
    w_out = C.din("a_w_out", [4096, 2048])
    kv_w = C.din("kv_w", [2048, 4096])
    if "fused" in io:
        ln_g, ln_b = io["ln_g"][0:1, :], io["ln_b"][0:1, :]
        x1_o = io["x1_scratch"]
    else:
        ln_g = C.din("ln_g", [1, 2048])
        ln_b = C.din("ln_b", [1, 2048])
        x1_o = C.dout("x1", [1024, 2048])
        kT_o = C.dout("kT", [16, 128, 1024], BF16)
        v_o = C.dout("v", [1024, 2048], BF16)

    xT = C.view(0, 32 * K, BF16, "p (k t) -> p k t", k=16)
    uT = [C.view(32 * K + i * 2 * K, 2 * K, BF16) for i in range(8)]
    sgT = [C.view(48 * K + i * 2 * K, 2 * K, BF16) for i in range(8)]
    z = C.view(0, 64 * K, F32, "p (a d) -> p a d", a=8)
    slab = C.view(64 * K, 64 * K, BF16, "p (f t c) -> p f t c", f=32, t=8)
    x1T = C.view(64 * K, 32 * K, BF16, "p (k t) -> p k t", k=16)
    lng = C.view(96 * K, 8 * K, F32)
    lnb = C.view(104 * K, 8 * K, F32)
    ost = [C.view(112 * K + i * 8 * K, 8 * K, F32) for i in range(2)]
    wsl = [C.view(128 * K + i * 16 * K, 16 * K, BF16) for i in range(3)]
    xs = C.view(176 * K, 8 * K, F32)
    tmp = [C.view(176 * K + i * 2 * K, 2 * K, F32) for i in range(4)]
    kst = [C.view(176 * K + i * 2 * K, 2 * K, BF16) for i in range(4)]

    Bxs = Buf("xs")
    Btmp = [Buf(f"tmp{i}") for i in range(4)]
    Bkst = [Buf(f"kst{i}") for i in range(4)]
    BxT = [Buf(f"xT{t}") for t in range(8)]
    Bu = [Buf(f"u{i}") for i in range(8)]
    Bsg = [Buf(f"sg{i}") for i in range(8)]
    Bz = [Buf(f"z{t}") for t in range(8)]
    Bslab = [[Buf(f"slab{f}_{h}") for h in range(2)] for f in range(32)]
    Bx1T = [Buf(f"x1T{t}") for t in range(8)]
    Bln = Buf("ln")
    Bost = [Buf("ost0"), Buf("ost1")]
    Bw = [Buf(f"w{i}") for i in range(3)]
    tiles = ([(w_in[:, 4096 + cb * 512:4096 + (cb + 1) * 512], 16) for cb in range(8)]
             + [t for g in range(8) for t in ((w_in[:, g * 512:(g + 1) * 512], 16),
                                              (w_in[:, 8192 + g * 512:8192 + (g + 1) * 512], 16))]
             + [(w_out[:, cb * 256:(cb + 1) * 256], 32) for cb in range(8)]
             + [(kv_w[:, ct * 512:(ct + 1) * 512], 16) for ct in range(8)])
    tiles = tiles + io.get("extra_tiles", [])
    WS = WStream(C, wsl, Bw, tiles)

    PT = C.psum4()
    banks = [PT[i // 2][:, (i % 2) * 512:(i % 2 + 1) * 512] for i in range(8)]
    Bbanks = [Buf(f"bank{i}") for i in range(8)]
    brot = Rot(8)

    ident, Bident = emit_consts(C)

    brow_u = C.sb([32, 128], F32, "brow_u")
    brow_g = C.sb([32, 128], F32, "brow_g")
    grow = C.sb([32, 128], F32, "grow")
    vbrow = C.sb([32, 128], F32, "vbrow")
    bs8 = C.sb([8, 128], F32, "bs8")
    bcol = C.sb([128, 64], F32, "bcol")
    gcol = C.sb([128, 32], F32, "gcol")
    CG = C.sb([128, 32, 2], F32, "CG")
    RB = C.sb([128, 8, 2], F32, "RB")
    Rab = [C.sb([2, 128], F32, f"Rab{i}") for i in range(4)]
    Rg = [C.sb([2, 128], F32, f"Rg{i}") for i in range(2)]
    Rd = C.sb([1, 640], F32, "Rd")
    Wsf = C.view(160 * K, 4 * K, F32, "p (g s) -> p g s", g=8)
    WsTf = C.view(164 * K, 4 * K, F32, "p (g s) -> p g s", g=8)
    WsT = C.sb([128, 8, 128], BF16, "WsT")
    epsc = C.sb([128, 1], F32, "epsc")
    stats = C.sb([128, 8, 8, 6], F32, "stats")
    mv = C.sb([128, 8, 2], F32, "mv")
    rstd = C.sb([128, 8], F32, "rstd")
    lstats = C.sb([128, 4, 6], F32, "lstats")
    lmv = C.sb([128, 2], F32, "lmv")
    lrstd = C.sb([128, 1], F32, "lrstd")
    lnbias = C.sb([128, 1], F32, "lnbias")
    lsets = [(lstats, lmv, lrstd, lnbias),
             (C.sb([128, 4, 6], F32, "lstats2"), C.sb([128, 2], F32, "lmv2"), C.sb([128, 1], F32, "lrstd2"),
              C.sb([128, 1], F32, "lnbias2"))]
    Blsets = [Buf("lsm0"), Buf("lsm1")]
    Bbrow, Bgrow, Bbcol, Bgcol = Buf("brow"), Buf("grow"), Buf("bcol"), Buf("gcol")
    BCG, BRB, BRab, BRg = Buf("CG"), Buf("RB"), [Buf(f"Rab{i}") for i in range(4)], [Buf("Rg0"), Buf("Rg1")]
    Bones, Bbv = Buf("ones"), Buf("bv")
    BWsf, BWsTf, BWsT, Beps = Bw[2], Bw[2], Buf("WsT"), Buf("eps")
    Bstats = [Buf(f"stats{t}") for t in range(8)]
    Bmv, Brstd, Blsm = Buf("mv"), Buf("rstd"), Buf("lsm")

    P.add("pool", f_memset(epsc[:], EPS), writes=[Beps])
    P.add("pool", f_memset(Rd[0:1, 512:640], 1.0), writes=[Bones])
    b_in96 = b_in.rearrange("o (f p) -> (o f) p", p=128)
    P.add("sp", f_dma(brow_u[:], b_in96[0:32, :]), writes=[Bbrow], dma=uch())
    P.add("sp", f_dma(brow_g[:], b_in96[64:96, :]), writes=[Bbrow], dma=uch())
    P.add("sp", f_dma(grow[:], vln_g.rearrange("o (f p) -> (o f) p", p=128)), writes=[Bgrow], dma=uch())
    P.add("sp", f_dma(vbrow[:], vln_b.rearrange("o (f p) -> (o f) p", p=128)), writes=[Bgrow], dma=uch())
    P.add("sp", f_dma(bs8[:], b_s.rearrange("o (g t) -> (o g) t", t=128)), writes=[Bgrow], dma=uch())
    P.add("sp", f_dma(Wsf[:], w_s.rearrange("g t s -> t g s")), writes=[BWsf], dma=uch())
    gate('c1')
    for (src_, nrow, dst_, Bsrc, Bd) in ((brow_u, 32, bcol[:, 0:32], Bbrow, Bbcol), (brow_g, 32, bcol[:, 32:64], Bbrow, Bbcol),
                                       (grow, 32, gcol[:, :], Bgrow, Bgcol), (vbrow, 32, CG[:, :, 0], Bgrow, BCG),
                                       (bs8, 8, RB[:, :, 1], Bgrow, BRB)):
        bi = brot()
        P.add("pe", f_tr(banks[bi][:, 0:nrow], src_[:, :], ident[0:nrow, 0:nrow]), reads=[Bsrc, Bident], writes=[Bbanks[bi]])
        P.add("dve", f_copy(dst_, banks[bi][:, 0:nrow]), reads=[Bbanks[bi]], writes=[Bd])
    P.add("dve", lambda e: e.reciprocal(out=CG[:, :, 1], in_=gcol[:, :]), reads=[Bgcol], writes=[BCG])
    P.add("dve", f_tt(CG[:, :, 0], CG[:, :, 0], CG[:, :, 1], ALU.mult), reads=[BCG], writes=[BCG])
    gate('c2')
    for g in range(8):
        P.add("pool", f_asel(Wsf[:, g, :], Wsf[:, g, :], ALU.is_ge, 0.0, [[-1, 128]], 1), reads=[BWsf], writes=[BWsf])
    P.add("dve", lambda e: e.tensor_reduce(out=RB[:, :, 0], in_=Wsf[:, :, :], axis=mybir.AxisListType.X, op=ALU.add),
          reads=[BWsf], writes=[BRB])
    gate('c3')
    for g in range(8):
        bi = brot()
        P.add("pe", f_tr(banks[bi][:, 0:128], Wsf[:, g, :], ident[:]), reads=[BWsf, Bident], writes=[Bbanks[bi]])
        P.add("dve", f_copy(WsT[:, g, :], banks[bi][:, 0:128]), reads=[Bbanks[bi]], writes=[BWsT])
    gate('c4')

    gate('consts')
    emit_load_T(C, x, xs, Bxs, "xs", xT, BxT, ident, Bident, banks, Bbanks, brot)

    gate('A0')
    alias(Btmp, [Bxs])
    it = 0
    for cb in range(8):
        wt, Bwt = WS.use(cb)
        P.add("sp", f_dma(Rd[0:1, 0:512], b_in[:, 4096 + cb * 512:4096 + (cb + 1) * 512]), writes=[Bbv], dma="bv")
        for tb in range(8):
            bi = brot()
            bank = banks[bi]
            for kc in range(16):
                P.add("pe", f_mm(bank[:, :], xT[:, kc, tb * 128:(tb + 1) * 128], wt[:, kc, :], kc == 0, False),
                      reads=[BxT[tb], Bwt], war=[Bbanks[bi]] if kc == 0 else [])
            P.add("pe", f_mm(bank[:, :], Rd[0:1, 512:640], Rd[0:1, 0:512], False, True),
                  reads=[Bones, Bbv], writes=[Bbanks[bi]])
            ti = it % 4
            it += 1
            P.add("act", f_act(tmp[ti], bank[:, :], AF.Gelu_apprx_tanh), reads=[Bbanks[bi]], writes=[Btmp[ti]])
            P.add("dve", lambda e, tb=tb, cb=cb, ti=ti: e.bn_stats(out=stats[:, tb, cb, :], in_=tmp[ti]),
                  reads=[Btmp[ti]], writes=[Bstats[tb]])
            P.add("pool", f_copy(slab[:, 4 * cb:4 * cb + 4, tb, :], tmp[ti].rearrange("p (a b) -> p a b", a=4)),
                  reads=[Btmp[ti]], writes=[Bslab[4 * cb + a][tb // 4] for a in range(4)])

    gate('A1')
    for tb in range(8):
        P.add("dve", lambda e, tb=tb: e.bn_aggr(out=mv[:, tb, :], in_=stats[:, tb, :, :].rearrange("p a b -> p (a b)")),
              reads=[Bstats[tb]], writes=[Bmv])
    gate('a2a')
    P.add("act", f_act(rstd[:, :], mv[:, :, 1], AF.Sqrt, bias=epsc[:], scale=1.0), reads=[Bmv, Beps], writes=[Brstd])
    P.add("dve", lambda e: e.reciprocal(out=rstd[:, :], in_=rstd[:, :]), reads=[Brstd], writes=[Brstd])
    gate('a2b')
    for tb in range(8):
        P.add("dve", f_ts(slab[:, :, tb, :], slab[:, :, tb, :], mv[:, tb, 0:1], rstd[:, tb:tb + 1], ALU.subtract, ALU.mult),
              reads=[Bmv, Brstd] + [Bslab[f][tb // 4] for f in range(32)],
              writes=[Bslab[f][tb // 4] for f in range(32)])

    gate('A2')
    for g in range(8):
        rg = Rg[g % 2]
        bi = brot()
        P.add("pe", f_tr(banks[bi][0:2, 0:128], RB[:, g, :], ident[:, :]), reads=[BRB, Bident], writes=[Bbanks[bi]])
        P.add("dve", f_copy(rg[:, :], banks[bi][0:2, 0:128]), reads=[Bbanks[bi]], writes=[BRg[g % 2]])
        for path in range(2):
            wt_, Bwt = WS.use(8 + 2 * g + path)
            func = AF.Gelu_apprx_tanh if path == 0 else AF.Silu
            boff = 0 if path == 0 else 32
            for fcl in range(4):
                fc = g * 4 + fcl
                s8 = fc % 8
                dstbuf, Bdst = (uT[s8], Bu[s8]) if path == 0 else (sgT[s8], Bsg[s8])
                for th in range(2):
                    bi = brot()
                    bank = banks[bi]
                    for kc in range(16):
                        P.add("pe", f_mm(bank[:, :], wt_[:, kc, fcl * 128:(fcl + 1) * 128],
                                         xT[:, kc, th * 512:(th + 1) * 512], kc == 0, kc == 15),
                              reads=[Bwt] + BxT[th * 4:(th + 1) * 4],
                              war=[Bbanks[bi]] if kc == 0 else [], writes=[Bbanks[bi]] if kc == 15 else [])
                    P.add("act", f_act(dstbuf[:, th * 512:(th + 1) * 512], bank[:, :], func,
                                       bias=bcol[:, boff + fc:boff + fc + 1]),
                          reads=[Bbanks[bi], Bbcol], writes=[Bdst])
        for fcl in range(4):
            fc = g * 4 + fcl
            s8 = fc % 8
            ra = Rab[fc % 4]
            bi = brot()
            P.add("pe", f_tr(banks[bi][0:2, 0:128], CG[:, fc, :], ident[:, :]), reads=[BCG, Bident], writes=[Bbanks[bi]])
            P.add("dve", f_copy(ra[:, :], banks[bi][0:2, 0:128]), reads=[Bbanks[bi]], writes=[BRab[fc % 4]])
            P.add("pool", f_tt(uT[s8], uT[s8], sgT[s8], ALU.mult),
                  reads=[Bu[s8], Bsg[s8]], writes=[Bu[s8]])
            for half in range(2):
                bi = brot()
                bank = banks[bi]
                for tbl in range(4):
                    tb = half * 4 + tbl
                    cols = bank[:, tbl * 128:(tbl + 1) * 128]
                    P.add("pe", f_mm(cols, slab[:, fc, tb, :], WsT[:, g, :], True, False),
                          reads=[Bslab[fc][half], BWsT], war=[Bbanks[bi]] if tbl == 0 else [])
                    P.add("pe", f_mm(cols, ra[:, :], rg[:, :], False, True),
                          reads=[BRab[fc % 4], BRg[g % 2]], writes=[Bbanks[bi]] if tbl == 3 else [])
                sview = slab[:, fc, half * 4:(half + 1) * 4, :].rearrange("p a b -> p (a b)")
                P.add("dve", f_stt(sview, bank[:, :], gcol[:, fc:fc + 1], uT[s8][:, half * 512:(half + 1) * 512],
                                   ALU.mult, ALU.mult),
                      reads=[Bbanks[bi], Bu[s8], Bgcol], writes=[Bslab[fc][half]])

    gate('A3')
    alias(Bz, BxT + Bu + Bsg)
    for cb in range(8):
        wo, Bwo = WS.use(24 + cb)
        for tb in range(8):
            bi = brot()
            bank = banks[bi]
            half = tb // 4
            for kc in range(32):
                P.add("pe", f_mm(bank[:, 0:256], slab[:, kc, tb, :], wo[:, kc, :], kc == 0, kc == 31),
                      reads=[Bslab[kc][half], Bwo],
                      war=[Bbanks[bi]] if kc == 0 else [], writes=[Bbanks[bi]] if kc == 31 else [])
            dst = z[:, tb, cb * 256:(cb + 1) * 256]
            if (cb * 8 + tb) % 2 == 0:
                P.add("act", f_act(dst, bank[:, 0:256], AF.Copy), reads=[Bbanks[bi]], writes=[Bz[tb]])
            else:
                P.add("dve", f_copy(dst, bank[:, 0:256]), reads=[Bbanks[bi]], writes=[Bz[tb]])

    gate('A4')
    allslab = [b for fb in Bslab for b in fb]
    alias([Bln] + Bost + Bx1T, allslab)
    alias([Bxs], Btmp)
    P.add("sp", f_dma(lng, ln_g.partition_broadcast(128)), writes=[Bln], dma="c1")
    P.add("sp", f_dma(lnb, ln_b.partition_broadcast(128)), writes=[Bln], dma="c1")
    outs = []
    for tb in range(8):
        o = ost[tb % 2]
        od = emit_ln_block(C, z[:, tb, :], Bz[tb], xs, Bxs, x, tb, lng, lnb, Bln, o, Bost[tb % 2],
                           lsets[tb % 2], Blsets[tb % 2], epsc, Beps, f"o{tb % 2}", x1_o)
        outs.append(od)
        if "fused" in io:
            io["Bx1d"][tb].writers = [od]
        if tb >= 1:
            emit_transpose_block(C, ost[(tb - 1) % 2], Bost[(tb - 1) % 2], x1T, Bx1T[tb - 1], tb - 1, ident, Bident,
                                 banks, Bbanks, brot)
    emit_transpose_block(C, ost[7 % 2], Bost[7 % 2], x1T, Bx1T[7], 7, ident, Bident, banks, Bbanks, brot)

    gate('A5')
    alias(Bkst, [Bxs])
    ki = 0
    for ct in range(8):
        wt, Bwt = WS.use(32 + ct)
        if ct < 4:
            for hh in range(4):
                h = ct * 4 + hh
                si = ki % 4
                ki += 1
                for th in range(2):
                    bi = brot()
                    bank = banks[bi]
                    for kc in range(16):
                        P.add("pe", f_mm(bank[:, :], wt[:, kc, hh * 128:(hh + 1) * 128], x1T[:, kc, th * 512:(th + 1) * 512],
                                         kc == 0, kc == 15),
                              reads=[Bwt] + Bx1T[th * 4:(th + 1) * 4],
                              war=[Bbanks[bi]] if kc == 0 else [], writes=[Bbanks[bi]] if kc == 15 else [])
                    if th == 0:
                        P.add("act", f_act(kst[si][:, 0:512], bank[:, :], AF.Copy), reads=[Bbanks[bi]], writes=[Bkst[si]])
                    else:
                        P.add("dve", f_copy(kst[si][:, 512:1024], bank[:, :]), reads=[Bbanks[bi]], writes=[Bkst[si]])
                if "fused" in io:
                    kd = P.add("sp", f_dma(io["kloc"][ct][hh * 128:(hh + 1) * 128, :], kst[si]), reads=[Bkst[si]], dma=f"k{si}")
                    io["Bkloc"][ct].writers.append(kd)
                else:
                    outs.append(P.add("sp", f_dma(kT_o[h], kst[si]), reads=[Bkst[si]], dma=f"k{si}"))
        else:
            for tbp in range(4):
                si = ki % 4
                ki += 1
                for t2 in range(2):
                    tb = tbp * 2 + t2
                    bi = brot()
                    bank = banks[bi]
                    for kc in range(16):
                        P.add("pe", f_mm(bank[:, :], x1T[:, kc, tb * 128:(tb + 1) * 128], wt[:, kc, :], kc == 0, kc == 15),
                              reads=[Bwt, Bx1T[tb]],
                              war=[Bbanks[bi]] if kc == 0 else [], writes=[Bbanks[bi]] if kc == 15 else [])
                    if t2 == 0:
                        P.add("act", f_act(kst[si][:, 0:512], bank[:, :], AF.Copy), reads=[Bbanks[bi]], writes=[Bkst[si]])
                    else:
                        P.add("dve", f_copy(kst[si][:, 512:1024], bank[:, :]), reads=[Bbanks[bi]], writes=[Bkst[si]])
                c0 = (ct - 4) * 512
                if "fused" in io:
                    dst = io["vloc"][ct - 4][tbp * 256:(tbp + 1) * 256, :].rearrange("(a p) c -> p a c", p=128)
                    vd = P.add("sp", f_dma(dst, kst[si].rearrange("p (a c) -> p a c", a=2)), reads=[Bkst[si]], dma=f"k{si}")
                    io["Bvloc"][ct - 4].writers.append(vd)
                else:
                    dst = v_o[tbp * 256:(tbp + 1) * 256, c0:c0 + 512].rearrange("(a p) c -> p a c", p=128)
                    outs.append(P.add("sp", f_dma(dst, kst[si].rearrange("p (a c) -> p a c", a=2)), reads=[Bkst[si]], dma=f"k{si}"))
        if "fused" in io and ct in (0, 4):
            io["gather"](ct)
    if "fused" not in io:
        P.add("sp", None, after=outs)
    else:
        io.update(A_Bz=Bz, A_Bx1T=Bx1T, A_x1T=x1T, A_Bln=Bln, A_Bost=Bost, A_Bkst=Bkst, A_Bxs=Bxs,
                  banks=banks, Bbanks=Bbanks, brot=brot, WS=WS)


def blocks_of(r):
    out = []
    for m in range(4):
        out += [8 * m + r, 8 * m + 7 - r]
    return out


def shard_tokens(x):
    res = []
    for c in range(8):
        b, r = divmod(c, 4)
        xb = x[b].reshape(32, 128, -1)
        res.append(np.ascontiguousarray(xb[blocks_of(r)].reshape(1024, -1)))
    return res


def common_A(inputs):
    return {
        "a_w_in": np.ascontiguousarray(inputs["a_w_in"][0]),
        "a_b_in": np.ascontiguousarray(inputs["a_b_in"][0].reshape(1, 12288)),
        "a_vln_g": np.ascontiguousarray(inputs["a_vln_g"][0].reshape(1, 4096)),
        "a_vln_b": np.ascontiguousarray(inputs["a_vln_b"][0].reshape(1, 4096)),
        "a_w_s": np.ascontiguousarray(inputs["a_w_s"][0]),
        "a_b_s": np.ascontiguousarray(inputs["a_b_s"][0].reshape(1, 1024)),
        "a_w_out": np.ascontiguousarray(inputs["a_w_out"][0]),
        "kv_w": np.ascontiguousarray(inputs["kv_w"]),
        "ln_g": np.ascontiguousarray(inputs["ln_g"][0].reshape(1, 2048)),
        "ln_b": np.ascontiguousarray(inputs["ln_b"][0].reshape(1, 2048)),
    }


def run_A(inputs, upto=None):
    nc = build_A(upto)
    xs = shard_tokens(np.asarray(inputs["x"], dtype=np.float32))
    common = common_A(inputs)
    in_maps = [dict(common, x=xs[c]) for c in range(8)]
    res = run_bass_kernel_spmd(nc, in_maps, core_ids=list(range(8)))
    return res.results


def build_B(upto=None):
    C = Ctx()
    try:
        _build_B_body(C, upto)
    except StopBuild:
        pass
    C.P.emit(C.nc, C.st)
    C.st.close()
    return C.nc


def _build_B_body(C, upto):
    def gate(name):
        if upto == name:
            raise StopBuild()

    nc, P = C.nc, C.P
    io = C.io
    fused = "fused" in io
    masks_d = C.din("masks", [128, 32 * 128], BF16)
    if fused:
        w_in, w_out = io["b_w_in"], io["b_w_out"]
    else:
        w_in = C.din("b_w_in", [2048, 4096])
        w_out = C.din("b_w_out", [2048, 2048])
    out_o = C.dout("out", [1024, 2048])
    if fused:
        x1 = io["x1_scratch"]
        ln_g, ln_b = io["ln_g"][1:2, :], io["ln_b"][1:2, :]
    else:
        x1 = C.din("x1", [1024, 2048])
        kTf = C.din("kTf", [16, 128, 4096], BF16)
        vf = C.din("vf", [4096, 2048], BF16)
        ln_g = C.din("ln_g", [1, 2048])
        ln_b = C.din("ln_b", [1, 2048])

    def pos(kb):
        if not fused:
            return kb
        m, o = divmod(kb, 8)
        return (o * 8 + 2 * m) if o < 4 else ((7 - o) * 8 + 2 * m + 1)

    if fused:
        oX, oQ, oZ, oSG, oKV = 64 * K, 32 * K, 32 * K, 0, 96 * K
    else:
        oX, oQ, oZ, oSG, oKV = 0, 32 * K, 0, 64 * K, 144 * K
    x1T = C.view(oX, 32 * K, BF16, "p (k t) -> p k t", k=16)
    ebuf = [C.view(oX + i * 4 * K, 4 * K, F32) for i in range(2)]
    spb = [C.view(oX + 8 * K + i * 2 * K, 2 * K, BF16) for i in range(2)]
    Sb = C.view(oX + 12 * K, 2 * K, BF16)
    wb = [C.view(oX + 14 * K + i * 2 * K, 2 * K, BF16) for i in range(2)]
    qT = C.view(oQ, 32 * K, BF16, "p (h t) -> p h t", h=16)
    z = C.view(oZ, 64 * K, F32, "p (a d) -> p a d", a=8)
    sgT = C.view(oSG, 32 * K, BF16, "p (h t) -> p h t", h=16)
    wsl = [C.view(96 * K + i * 16 * K, 16 * K, BF16) for i in range(3)]
    KTs = [C.view(oKV + i * 8 * K, 8 * K, BF16) for i in range(2)]
    Vs = [C.view(oKV + 16 * K + i * 8 * K, 8 * K, BF16, "p (b d) -> p b d", b=32) for i in range(2)]
    lng = C.view(oKV, 8 * K, F32)
    lnb = C.view(oKV + 8 * K, 8 * K, F32)
    ost = [C.view(oKV + 16 * K + i * 8 * K, 8 * K, F32) for i in range(2)]
    xs = C.view(176 * K, 8 * K, F32)

    Bxs = Buf("xs")
    Bx1T = [Buf(f"x1T{t}") for t in range(8)]
    Bq = [Buf(f"q{h}") for h in range(16)]
    Bsg = [Buf(f"sg{h}") for h in range(16)]
    Bz = [Buf(f"z{t}") for t in range(8)]
    Bw = [Buf(f"w{i}") for i in range(3)]
    BKT = [Buf("KT0"), Buf("KT1")]
    BV = [Buf("V0"), Buf("V1")]
    Be = [Buf("e0"), Buf("e1")]
    Bsp = [Buf("sp0"), Buf("sp1")]
    BS = Buf("S")
    Bwb = [Buf("wb0"), Buf("wb1")]
    Bln = Buf("ln")
    Bost = [Buf("ost0"), Buf("ost1")]

    PT = C.psum4()
    if fused:
        banks, Bbanks, brot = io["banks"], io["Bbanks"], io["brot"]
        Bx1T = io["A_Bx1T"]
        alias(Bsg + Bq, io["A_Bz"])
        alias(BKT + BV, [io["A_Bln"]] + io["A_Bost"])
        alias([Bxs], io["A_Bkst"] + [io["A_Bxs"]])
    else:
        banks = [PT[i // 2][:, (i % 2) * 512:(i % 2 + 1) * 512] for i in range(8)]
        Bbanks = [Buf(f"bank{i}") for i in range(8)]
        brot = Rot(8)

    ident, Bident = emit_consts(C)
    masks = C.sb([128, 32, 128], BF16, "masks_sb")
    negU = C.sb([128, 128], BF16, "negU")
    negO = C.sb([128, 128], BF16, "negO")
    epsc = C.sb([128, 1], F32, "epsc")
    lstats = C.sb([128, 4, 6], F32, "lstats")
    lmv = C.sb([128, 2], F32, "lmv")
    lrstd = C.sb([128, 1], F32, "lrstd")
    lnbias = C.sb([128, 1], F32, "lnbias")
    lsets = [(lstats, lmv, lrstd, lnbias),
             (C.sb([128, 4, 6], F32, "lstats2"), C.sb([128, 2], F32, "lmv2"), C.sb([128, 1], F32, "lrstd2"),
              C.sb([128, 1], F32, "lnbias2"))]
    Blsets = [Buf("lsm0"), Buf("lsm1")]
    Bmasks, BnegU, BnegO, Beps, Blsm = Buf("masks"), Buf("negU"), Buf("negO"), Buf("eps"), Buf("lsm")
    P.add("sp", f_dma(masks[:].rearrange("p a b -> p (a b)"), masks_d), writes=[Bmasks], dma=uch())
    P.add("pool", f_memset(epsc[:], EPS), writes=[Beps])
    P.add("pool", f_memset(negO[:], -1.0), writes=[BnegO])
    P.add("pool", f_memset(negU[:], -1.0), writes=[BnegU])
    P.add("pool", f_asel(negU[:], negU[:], ALU.is_ge, 0.0, [[-1, 128]], 1), reads=[BnegU], writes=[BnegU])

    if fused:
        WS, wbase = io["WS"], 40
    else:
        tiles = ([(w_in[:, ct * 512:(ct + 1) * 512], 16) for ct in range(8)]
                 + [(w_out[:, cb * 512:(cb + 1) * 512], 16) for cb in range(4)])
        WS, wbase = WStream(C, wsl, Bw, tiles), 0
        emit_load_T(C, x1, xs, Bxs, "xs", x1T, Bx1T, ident, Bident, banks, Bbanks, brot)
    gate('B0')

    for ct in range(8):
        wt, Bwt = WS.use(wbase + ct)
        if fused and ct >= 2:
            io["gather"]([0, 4, 1, 5, 2, 6, 3, 7][ct])
        for hh in range(4):
            h = (ct % 4) * 4 + hh
            for th in range(2):
                bi = brot()
                bank = banks[bi]
                for kc in range(16):
                    P.add("pe", f_mm(bank, wt[:, kc, hh * 128:(hh + 1) * 128], x1T[:, kc, th * 512:(th + 1) * 512],
                                     kc == 0, kc == 15),
                          reads=[Bwt] + Bx1T[th * 4:(th + 1) * 4],
                          war=[Bbanks[bi]] if kc == 0 else [], writes=[Bbanks[bi]] if kc == 15 else [])
                if ct < 4:
                    P.add("act", f_act(qT[:, h, th * 512:(th + 1) * 512], bank, AF.Copy, scale=128.0 ** -0.5),
                          reads=[Bbanks[bi]], writes=[Bq[h]])
                else:
                    P.add("act", f_act(sgT[:, h, th * 512:(th + 1) * 512], bank, AF.Silu),
                          reads=[Bbanks[bi]], writes=[Bsg[h]])
    gate('B1')

    Z = PT[0:3]
    BZ = [Buf("Z0"), Buf("Z1"), Buf("Z2")]
    OT = PT[3]
    BOT = Buf("OT")
    alias(BZ + [BOT], Bbanks)
    alias(Be + Bsp + [BS] + Bwb, Bx1T)
    its = [(h, kb) for h in range(16) for kb in range(31, -1, -1)]
    n = len(its)

    def segs(c0, c1):
        out = []
        if c0 < 512:
            out.append((c0, min(c1, 512)))
        if c1 > 512:
            out.append((max(c0, 512), c1))
        return [(a, b) for a, b in out if b > a]

    def load_head(h):
        hs = h % 2
        if fused:
            ct, hh = divmod(h, 4)
            ksrc = io["kall"][ct].rearrange("(r q) t -> q r t", q=512)[hh * 128:(hh + 1) * 128, :, :]
            P.add("sp", f_dma(KTs[hs].rearrange("p (r t) -> p r t", r=4), ksrc), reads=[io["Bkall"][ct]],
                  writes=[BKT[hs]], dma=f"kt{hs}")
            src = io["vall"][ct].rearrange("(b p) c -> p b c", p=128)[:, :, hh * 128:(hh + 1) * 128]
            rd = [io["Bvall"][ct]]
        else:
            P.add("sp", f_dma(KTs[hs], kTf[h]), writes=[BKT[hs]], dma=f"kt{hs}")
            src = vf[:, h * 128:(h + 1) * 128].rearrange("(b p) d -> p b d", p=128)
            rd = []
        for q4 in range(4):
            P.add("sp", f_dma(Vs[hs][:, q4 * 8:(q4 + 1) * 8, :], src[:, q4 * 8:(q4 + 1) * 8, :]), reads=rd,
                  writes=[BV[hs]], dma=f"v{hs}")

    def QK(i):
        h, kb = its[i]
        if kb == 31 and h == 0:
            load_head(0)
            load_head(1)
        c0 = (kb // 4) * 128
        s = i % 3
        sg_ = segs(c0, 1024)
        for k, (a, b) in enumerate(sg_):
            P.add("pe", f_mm(Z[s][:, a:b], KTs[h % 2][:, pos(kb) * 128:(pos(kb) + 1) * 128], qT[:, h, a:b], True, False),
                  reads=[BKT[h % 2], Bq[h]], war=[BZ[s]] if k == 0 else [], writes=[BZ[s]] if k == len(sg_) - 1 else [])

    def E(i):
        h, kb = its[i]
        c0 = (kb // 4) * 128
        P.add("act", f_act(ebuf[i % 2][:, c0:1024], Z[i % 3][:, c0:1024], AF.Exp), reads=[BZ[i % 3]], writes=[Be[i % 2]])

    def L(i):
        h, kb = its[i]
        j0, mi = kb // 4, kb % 4
        c0 = j0 * 128
        P.add("act", f_act(spb[i % 2][:, c0:1024], ebuf[i % 2][:, c0:1024], AF.Ln, bias=1.0), reads=[Be[i % 2]], writes=[Bsp[i % 2]])
        P.add("dve", f_tt(spb[i % 2][:, c0:c0 + 128], spb[i % 2][:, c0:c0 + 128], masks[:, j0 * 4 + mi, :], ALU.mult),
              reads=[Bsp[i % 2], Bmasks], writes=[Bsp[i % 2]])

    def US(i):
        h, kb = its[i]
        j0, mi = kb // 4, kb % 4
        c0 = j0 * 128
        s = i % 3
        ops = [(negU, BnegU, Bsp[i % 2], spb[i % 2], a, b) for a, b in segs(c0, 1024)]
        cS = c0 + 128 if mi == 3 else c0
        ops += [(negO, BnegO, BS, Sb, a, b) for a, b in segs(cS, 1024)]
        for k, (lh, Blh, Brh, rh, a, b) in enumerate(ops):
            P.add("pe", f_mm(Z[s][:, a:b], lh[:, :], rh[:, a:b], False, True),
                  reads=[Blh, Brh], war=[BZ[s]] if k == 0 else [], writes=[BZ[s]] if k == len(ops) - 1 else [])
        if mi == 3:
            P.add("dve", f_copy(Sb[:, c0:c0 + 128], spb[i % 2][:, c0:c0 + 128]), reads=[Bsp[i % 2]], writes=[BS])
        if cS < 1024:
            P.add("dve", f_tt(Sb[:, cS:1024], Sb[:, cS:1024], spb[i % 2][:, cS:1024], ALU.add), reads=[Bsp[i % 2], BS], writes=[BS])

    def W(i):
        h, kb = its[i]
        j0, mi = kb // 4, kb % 4
        c0 = j0 * 128
        P.add("act", f_act(wb[i % 2][:, c0:1024], Z[i % 3][:, c0:1024], AF.Exp), reads=[BZ[i % 3]], writes=[Bwb[i % 2]])
        P.add("dve", f_tt(wb[i % 2][:, c0:c0 + 128], wb[i % 2][:, c0:c0 + 128], masks[:, j0 * 4 + mi, :], ALU.mult),
              reads=[Bwb[i % 2], Bmasks], writes=[Bwb[i % 2]])

    def PV(i):
        h, kb = its[i]
        c0 = (kb // 4) * 128
        if kb == 31:
            if 1 <= h < 15:
                load_head(h + 1)
            P.add("dve", f_memset(OT[:, :], 0.0), writes=[BOT])
        sg_ = segs(c0, 1024)
        for k, (a, b) in enumerate(sg_):
            P.add("pe", f_mm(OT[:, a:b], Vs[h % 2][:, pos(kb), :], wb[i % 2][:, a:b], False, True),
                  reads=[BV[h % 2], Bwb[i % 2]], war=[BOT] if k == 0 else [], writes=[BOT] if k == len(sg_) - 1 else [])
        if kb == 0:
            for hf in range(2):
                P.add("dve", f_tt(sgT[:, h, hf * 512:(hf + 1) * 512], OT[:, hf * 512:(hf + 1) * 512],
                                  sgT[:, h, hf * 512:(hf + 1) * 512], ALU.mult),
                      reads=[BOT, Bsg[h]], writes=[Bsg[h], BOT])

    nit = n
    QK(0)
    E(0)
    L(0)
    for i in range(nit):
        if i + 1 < nit:
            QK(i + 1)
        if i - 1 >= 0:
            W(i - 1)
        if i + 1 < nit:
            E(i + 1)
            L(i + 1)
        US(i)
        if i - 1 >= 0:
            PV(i - 1)
    W(nit - 1)
    PV(nit - 1)
    gate('B2')

    alias(Bz, Bq + Be + Bsp + [BS] + Bwb + Bx1T)
    alias(Bbanks, BZ + [BOT])
    for cb in range(4):
        wo, Bwo = WS.use(wbase + 8 + cb)
        for tb in range(8):
            bi = brot()
            bank = banks[bi]
            for kc in range(16):
                P.add("pe", f_mm(bank, sgT[:, kc, tb * 128:(tb + 1) * 128], wo[:, kc, :], kc == 0, kc == 15),
                      reads=[Bsg[kc], Bwo], war=[Bbanks[bi]] if kc == 0 else [], writes=[Bbanks[bi]] if kc == 15 else [])
            dst = z[:, tb, cb * 512:(cb + 1) * 512]
            if (cb * 8 + tb) % 2 == 0:
                P.add("act", f_act(dst, bank, AF.Copy), reads=[Bbanks[bi]], writes=[Bz[tb]])
            else:
                P.add("dve", f_copy(dst, bank), reads=[Bbanks[bi]], writes=[Bz[tb]])
    gate('B4')

    alias([Bln] + Bost, BKT + BV)
    P.add("sp", f_dma(lng, ln_g.partition_broadcast(128)), writes=[Bln], dma="c1")
    P.add("sp", f_dma(lnb, ln_b.partition_broadcast(128)), writes=[Bln], dma="c1")
    outs = []
    for tb in range(8):
        o = ost[tb % 2]
        outs.append(emit_ln_block(C, z[:, tb, :], Bz[tb], xs, Bxs, x1, tb, lng, lnb, Bln, o, Bost[tb % 2],
                                  lsets[tb % 2], Blsets[tb % 2], epsc, Beps, f"o{tb % 2}", out_o))
    P.add("sp", None, after=outs)


def make_masks(r):
    qb = blocks_of(r)
    m = np.zeros((128, 32, 128), np.float32)
    tri = (np.arange(128)[:, None] < np.arange(128)[None, :]).astype(np.float32)
    for j in range(8):
        for mi in range(4):
            kb = 4 * j + mi
            if kb < qb[j]:
                m[:, j * 4 + mi, :] = 1.0
            elif kb == qb[j]:
                m[:, j * 4 + mi, :] = tri
    return np.ascontiguousarray(m.reshape(128, 32 * 128)).astype(ml_dtypes.bfloat16)


def assemble_kv(resA):
    kTf = [np.zeros((16, 128, 4096), ml_dtypes.bfloat16) for _ in range(2)]
    vf = [np.zeros((4096, 2048), ml_dtypes.bfloat16) for _ in range(2)]
    for c in range(8):
        b, r = divmod(c, 4)
        for j, qb in enumerate(blocks_of(r)):
            kTf[b][:, :, qb * 128:(qb + 1) * 128] = resA[c]["kT"][:, :, j * 128:(j + 1) * 128]
            vf[b][qb * 128:(qb + 1) * 128, :] = resA[c]["v"][j * 128:(j + 1) * 128, :]
    return kTf, vf


def common_B(inputs):
    return {
        "b_w_in": np.ascontiguousarray(inputs["b_w_in"][0]),
        "b_w_out": np.ascontiguousarray(inputs["b_w_out"][0]),
        "ln_g": np.ascontiguousarray(inputs["ln_g"][1].reshape(1, 2048)),
        "ln_b": np.ascontiguousarray(inputs["ln_b"][1].reshape(1, 2048)),
    }


def unshard_tokens(outs):
    res = np.zeros((2, 32, 128, 2048), np.float32)
    for c in range(8):
        b, r = divmod(c, 4)
        res[b, blocks_of(r)] = outs[c].reshape(8, 128, 2048)
    return res.reshape(2, 4096, 2048)


def build_fused():
    C = Ctx()
    nc, P = C.nc, C.P
    io = C.io
    io["fused"] = True
    io["ln_g"] = C.din("ln_g", [2, 2048])
    io["ln_b"] = C.din("ln_b", [2, 2048])
    io["x1_scratch"] = nc.dram_tensor("x1_scratch", [1024, 2048], F32).ap()
    kloc = [nc.dram_tensor(f"kloc{i}", [512, 1024], BF16) for i in range(4)]
    vloc = [nc.dram_tensor(f"vloc{i}", [1024, 512], BF16) for i in range(4)]
    kall = [nc.dram_tensor(f"kall{i}", [2048, 1024], BF16) for i in range(4)]
    vall = [nc.dram_tensor(f"vall{i}", [4096, 512], BF16) for i in range(4)]
    io["kloc"] = [t.ap() for t in kloc]
    io["vloc"] = [t.ap() for t in vloc]
    io["kall"] = [t.ap() for t in kall]
    io["vall"] = [t.ap() for t in vall]
    io["Bkloc"] = [Buf(f"kloc{i}") for i in range(4)]
    io["Bvloc"] = [Buf(f"vloc{i}") for i in range(4)]
    io["Bkall"] = [Buf(f"kall{i}") for i in range(4)]
    io["Bvall"] = [Buf(f"vall{i}") for i in range(4)]
    io["Bx1d"] = [Buf(f"x1d{t}") for t in range(8)]
    groups = [[0, 1, 2, 3], [4, 5, 6, 7]]

    def gather(ct):
        if ct < 4:
            src, dst, Bs, Bd = kloc[ct], kall[ct], io["Bkloc"][ct], io["Bkall"][ct]
        else:
            src, dst, Bs, Bd = vloc[ct - 4], vall[ct - 4], io["Bvloc"][ct - 4], io["Bvall"][ct - 4]
        P.add("pool", lambda e: e.collective_compute("AllGather", ALU.bypass, replica_groups=groups,
                                                     ins=[src.ap().opt()], outs=[dst.ap().opt()]),
              reads=[Bs], writes=[Bd], dma=f"cc{ct}", step=1)

    io["gather"] = gather
    io["b_w_in"] = C.din("b_w_in", [2048, 4096])
    io["b_w_out"] = C.din("b_w_out", [2048, 2048])
    io["extra_tiles"] = ([(io["b_w_in"][:, ct * 512:(ct + 1) * 512], 16) for ct in range(8)]
                         + [(io["b_w_out"][:, cb * 512:(cb + 1) * 512], 16) for cb in range(4)])
    C.prefix = "A_"
    _build_A_body(C, None)
    io["Bx1d_rd"] = {tb: [io["Bx1d"][tb]] for tb in range(8)}
    C.prefix = "B_"
    _build_B_body(C, None)
    P.emit(nc, C.st)
    C.st.close()
    return nc


def kernel(x, a_w_in, a_b_in, a_vln_g, a_vln_b, a_w_s, a_b_s, a_w_out, kv_w, b_w_in, b_w_out, ln_g, ln_b):
    inputs = dict(x=np.asarray(x), a_w_in=np.asarray(a_w_in), a_b_in=np.asarray(a_b_in), a_vln_g=np.asarray(a_vln_g),
                  a_vln_b=np.asarray(a_vln_b), a_w_s=np.asarray(a_w_s), a_b_s=np.asarray(a_b_s),
                  a_w_out=np.asarray(a_w_out), kv_w=np.asarray(kv_w), b_w_in=np.asarray(b_w_in),
                  b_w_out=np.asarray(b_w_out), ln_g=np.asarray(ln_g), ln_b=np.asarray(ln_b))
    nc = build_fused()
    in_maps = fused_in_maps(inputs)
    res = run_bass_kernel_spmd(nc, in_maps, core_ids=list(range(8))).results
    return unshard_tokens([res[c]["out"] for c in range(8)])


def fused_in_maps(inputs):
    xs = shard_tokens(np.asarray(inputs["x"], dtype=np.float32))
    common = common_A(inputs)
    common.update(common_B(inputs))
    common["ln_g"] = np.ascontiguousarray(inputs["ln_g"], dtype=np.float32)
    common["ln_b"] = np.ascontiguousarray(inputs["ln_b"], dtype=np.float32)
    return [dict(common, x=xs[c], masks=make_masks(c % 4)) for c in range(8)]
```

```python
from contextlib import ExitStack

import ml_dtypes
import numpy as np

import concourse.bass as bass
import concourse.mybir as mybir
from concourse.bass_utils import run_bass_kernel_spmd

F32 = mybir.dt.float32
BF16 = mybir.dt.bfloat16
AF = mybir.ActivationFunctionType
ALU = mybir.AluOpType

ALPHA = 4.0 ** 0.25
EPS = 1e-5
ENGS = ("pe", "act", "dve", "pool", "sp")


class Buf:
    __slots__ = ("name", "writers", "readers")

    def __init__(self, name):
        self.name = name
        self.writers = []
        self.readers = []


class Op:
    __slots__ = ("eng", "fn", "deps", "inc", "dma", "dma_val", "val", "step")

    def __init__(self, eng, fn, dma):
        self.eng = eng
        self.fn = fn
        self.deps = []
        self.inc = False
        self.dma = dma
        self.dma_val = 0
        self.val = 0


def alias(new_bufs, old_bufs):
    olds = []
    for b in old_bufs:
        for o in b.readers + b.writers:
            if o not in olds:
                olds.append(o)
    for nb in new_bufs:
        for o in olds:
            if o not in nb.readers:
                nb.readers.append(o)


class Prog:
    def __init__(self):
        self.ops = {e: [] for e in ENGS}
        self.dma_counts = {}
        self.last_dma = {}
        self.bar = []

    def barrier(self, skip_prefix="cc"):
        deps = []
        for e in ENGS:
            for op in reversed(self.ops[e]):
                if op.dma is None and op.fn is not None:
                    deps.append(op)
                    break
        for ch, op in self.last_dma.items():
            if not ch.startswith(skip_prefix):
                deps.append(op)
        self.bar = deps

    def add(self, eng, fn, reads=(), writes=(), war=(), dma=None, after=(), step=16):
        op = Op(eng, fn, dma)
        op.step = step
        deps = {}
        for b in reads:
            for w in b.writers:
                deps[w] = "raw"
        for b in list(writes) + list(war):
            for w in b.writers:
                deps.setdefault(w, "waw")
            for r in b.readers:
                deps.setdefault(r, "war")
        for a in list(after) + self.bar:
            if a is not None:
                deps[a] = "raw"
        for d, kind in deps.items():
            if d is op:
                continue
            if d.dma is not None:
                op.deps.append(d)
            elif d.eng == eng and dma is None and (kind == "war" or (kind == "waw" and eng == "pe")):
                continue
            else:
                d.inc = True
                op.deps.append(d)
        for b in reads:
            if dma is None:
                b.readers = [r for r in b.readers if not (r.eng == eng and r.dma is None)]
            b.readers.append(op)
        for b in writes:
            if b.readers:
                b.writers = [op]
                b.readers = []
            else:
                if dma is None:
                    b.writers = [w for w in b.writers if not (w.eng == eng and w.dma is None)]
                b.writers.append(op)
        if dma is not None:
            c = self.dma_counts.get(dma, 0) + step
            self.dma_counts[dma] = c
            op.dma_val = c
            self.last_dma[dma] = op
        self.ops[eng].append(op)
        return op

    def emit(self, nc, st):
        esem = {e: st.enter_context(nc.semaphore("s_" + e)) for e in ENGS}
        dsem = {c: st.enter_context(nc.semaphore("d_" + c)) for c in self.dma_counts}
        for e in ENGS:
            v = 0
            for op in self.ops[e]:
                if op.inc and op.dma is None:
                    v += 1
                    op.val = v
        block = st.enter_context(nc.Block())

        def run(e, eng):
            waited = {}
            for op in self.ops[e]:
                need = {}
                for d in op.deps:
                    if d.dma is not None:
                        key, val, sem = ("d", d.dma), d.dma_val, dsem[d.dma]
                    else:
                        key, val, sem = ("e", d.eng), d.val, esem[d.eng]
                    if val > need.get(key, (0, None))[0]:
                        need[key] = (val, sem)
                for key, (val, sem) in need.items():
                    if waited.get(key, 0) >= val:
                        continue
                    eng.wait_ge(sem, val)
                    waited[key] = val
                if op.fn is None:
                    continue
                ins = op.fn(eng)
                if op.dma is not None:
                    ins.then_inc(dsem[op.dma], op.step)
                elif op.inc:
                    ins.then_inc(esem[e], 1)

        @block.tensor
        def _(eng):
            run("pe", eng)

        @block.scalar
        def _(eng):
            run("act", eng)

        @block.vector
        def _(eng):
            run("dve", eng)

        @block.gpsimd
        def _(eng):
            run("pool", eng)

        @block.sync
        def _(eng):
            run("sp", eng)


class Rot:
    def __init__(self, n):
        self.n = n
        self.i = -1

    def __call__(self):
        self.i = (self.i + 1) % self.n
        return self.i


def f_mm(out, lhsT, rhs, start, stop):
    return lambda e: e.matmul(out, lhsT=lhsT, rhs=rhs, start=start, stop=stop, skip_group_check=True)


def f_tr(out, in_, ident):
    return lambda e: e.transpose(out, in_, ident)


def f_act(out, in_, func, bias=None, scale=1.0):
    if bias is None:
        return lambda e: e.activation(out=out, in_=in_, func=func, scale=scale)
    return lambda e: e.activation(out=out, in_=in_, func=func, bias=bias, scale=scale)


def f_copy(out, in_):
    return lambda e: e.tensor_copy(out=out, in_=in_)


def f_dma(out, in_):
    return lambda e: e.dma_start(out=out, in_=in_)


def f_tt(out, in0, in1, op):
    return lambda e: e.tensor_tensor(out=out, in0=in0, in1=in1, op=op)


def f_ts(out, in0, s1, s2, op0, op1):
    return lambda e: e.tensor_scalar(out=out, in0=in0, scalar1=s1, scalar2=s2, op0=op0, op1=op1)


def f_stt(out, in0, scalar, in1, op0, op1):
    return lambda e: e.scalar_tensor_tensor(out=out, in0=in0, scalar=scalar, in1=in1, op0=op0, op1=op1)


def f_memset(ap, v):
    return lambda e: e.memset(ap, v)


def f_asel(out, in_, cmp, fill, pattern, cm, base=0):
    return lambda e: e.affine_select(out=out, in_=in_, compare_op=cmp, fill=fill, base=base,
                                     pattern=pattern, channel_multiplier=cm)


class Ctx:
    def __init__(self, arena_bytes=184 * 1024):
        self.nc = bass.Bass("TRN2", target_bir_lowering=False)
        self.P = Prog()
        self.st = ExitStack()
        self.arena = self.st.enter_context(self.nc.sbuf_tensor("arena", [128, arena_bytes // 4], F32))
        self.nsmall = 0
        self.prefix = ""
        self.PT = None
        self.io = {}

    def view(self, off, nbytes, dt, pat=None, **kw):
        a = self.arena[:, off // 4:(off + nbytes) // 4]
        if dt is BF16:
            a = a.bitcast(BF16)
        if pat is not None:
            a = a.rearrange(pat, **kw)
        return a

    def sb(self, shape, dt, name=None):
        self.nsmall += 1
        return self.st.enter_context(self.nc.sbuf_tensor(self.prefix + (name or f"sm{self.nsmall}"), shape, dt))

    def psum4(self):
        if self.PT is None:
            self.PT = [self.ps([128, 1024], f"pt{i}") for i in range(4)]
        return self.PT

    def ps(self, shape, name):
        return self.st.enter_context(self.nc.psum_tensor(name, shape, F32))

    def din(self, name, shape, dt=F32):
        return self.nc.dram_tensor(name, shape, dt, kind="ExternalInput").ap()

    def dout(self, name, shape, dt=F32):
        return self.nc.dram_tensor(name, shape, dt, kind="ExternalOutput").ap()


K = 1024


def emit_consts(C):
    P = C.P
    if "ident" in C.io:
        return C.io["ident"]
    ident = C.sb([128, 128], F32, "ident")
    Bident = Buf("ident")
    P.add("pool", f_memset(ident[:], 1.0), writes=[Bident])
    P.add("pool", f_asel(ident[:], ident[:], ALU.is_equal, 0.0, [[-1, 128]], 1), reads=[Bident], writes=[Bident])
    C.io["ident"] = (ident, Bident)
    return ident, Bident


def emit_load_T(C, src, xs, Bxs, xsch, dstT, BdstT, ident, Bident, banks, Bbanks, rot):
    P = C.P
    for tb in range(8):
        P.add("sp", f_dma(xs[:], src[tb * 128:(tb + 1) * 128, :]), writes=[Bxs], dma=xsch)
        emit_transpose_block(C, xs, Bxs, dstT, BdstT[tb], tb, ident, Bident, banks, Bbanks, rot)


def emit_transpose_block(C, xs, Bxs, dstT, Bdst, tb, ident, Bident, banks, Bbanks, rot):
    P = C.P
    for q in range(4):
        bi = rot()
        bank = banks[bi]
        for jj in range(4):
            kc = q * 4 + jj
            P.add("pe", f_tr(bank[:, jj * 128:(jj + 1) * 128], xs[:, kc * 128:(kc + 1) * 128], ident[:]),
                  reads=[Bxs, Bident], writes=[Bbanks[bi]] if jj == 3 else [], war=[Bbanks[bi]] if jj == 0 else [])
        dst = dstT[:, q * 4:(q + 1) * 4, tb * 128:(tb + 1) * 128]
        srcv = bank[:, :].rearrange("p (a b) -> p a b", a=4)
        if q % 2 == 0:
            P.add("act", f_act(dst, srcv, AF.Copy), reads=[Bbanks[bi]], writes=[Bdst])
        else:
            P.add("dve", f_copy(dst, srcv), reads=[Bbanks[bi]], writes=[Bdst])


def emit_ln_load(C, xs, Bxs, x_src, tb):
    C.P.add("sp", f_dma(xs[:], x_src[tb * 128:(tb + 1) * 128, :]), reads=C.io.get("Bx1d_rd", {}).get(tb, []),
            writes=[Bxs], dma="xs")


def emit_ln_block(C, zt, Bz, xs, Bxs, x_src, tb, lng, lnb, Bln, ost, Bost, small, Bsm, epsc, Beps, och, out_dst):
    P = C.P
    stats, mv, rstd, nb = small
    if tb == 0:
        emit_ln_load(C, xs, Bxs, x_src, 0)
    P.add("dve", f_stt(zt, xs[:], ALPHA, zt, ALU.mult, ALU.add), reads=[Bxs, Bz], writes=[Bz])
    if tb + 1 < 8:
        emit_ln_load(C, xs, Bxs, x_src, tb + 1)
    for c in range(4):
        P.add("dve", lambda e, c=c: e.bn_stats(out=stats[:, c, :], in_=zt[:, c * 512:(c + 1) * 512]),
              reads=[Bz], writes=[Bsm])
    P.add("dve", lambda e: e.bn_aggr(out=mv[:], in_=stats[:].rearrange("p a b -> p (a b)")), reads=[Bsm], writes=[Bsm])
    P.add("act", f_act(rstd[:], mv[:, 1:2], AF.Sqrt, bias=epsc[:], scale=1.0), reads=[Bsm, Beps], writes=[Bsm])
    P.add("dve", lambda e: e.reciprocal(out=rstd[:], in_=rstd[:]), reads=[Bsm], writes=[Bsm])
    P.add("dve", f_stt(nb[:], mv[:, 0:1], -1.0, rstd[:], ALU.mult, ALU.mult), reads=[Bsm], writes=[Bsm])
    P.add("act", f_act(zt, zt, AF.Identity, bias=nb[:], scale=rstd[:]), reads=[Bz, Bsm], writes=[Bz])
    P.add("dve", f_tt(ost, zt, lng, ALU.mult), reads=[Bz, Bln], writes=[Bost])
    P.add("pool", f_tt(ost, ost, lnb, ALU.add), reads=[Bost, Bln], writes=[Bost])
    return P.add("sp", f_dma(out_dst[tb * 128:(tb + 1) * 128, :], ost), reads=[Bost], dma=och)


def load_wtile(C, dst, Bdst, ch, src2d, nk, split=4):
    P = C.P
    step = nk // split
    for s in range(split):
        P.add("pool", f_dma(dst[:, s * step:(s + 1) * step, :],
                            src2d[s * step * 128:(s + 1) * step * 128, :].rearrange("(k p) c -> p k c", p=128)),
              writes=[Bdst], dma=ch)


class WStream:
    def __init__(self, C, wsl, Bw, tiles):
        self.C, self.wsl, self.Bw, self.tiles = C, wsl, Bw, tiles
        self.issued = 0

    def use(self, i):
        while self.issued < min(i + 3, len(self.tiles)):
            j = self.issued
            src2d, nk = self.tiles[j]
            sl = j % 3
            dst = self.wsl[sl].rearrange("p (k c) -> p k c", k=nk)
            load_wtile(self.C, dst, self.Bw[sl], f"w{sl}", src2d, nk)
            self.issued += 1
        sl = i % 3
        nk = self.tiles[i][1]
        return self.wsl[sl].rearrange("p (k c) -> p k c", k=nk), self.Bw[sl]


_uid = [0]


def uch(prefix="c"):
    _uid[0] += 1
    return f"{prefix}{_uid[0]}"


class StopBuild(Exception):
    pass


def build_A(upto=None):
    C = Ctx()
    try:
        _build_A_body(C, upto)
    except StopBuild:
        pass
    C.P.emit(C.nc, C.st)
    C.st.close()
    return C.nc


def _build_A_body(C, upto):
    def gate(name):
        if upto == name:
            raise StopBuild()

    nc, P = C.nc, C.P
    io = C.io
    x = C.din("x", [1024, 2048])
    w_in = C.din("a_w_in", [2048, 12288])
    b_in = C.din("a_b_in", [1, 12288])
    vln_g = C.din("a_vln_g", [1, 4096])
    vln_b = C.din("a_vln_b", [1, 4096])
    w_s = C.din("a_w_s", [8, 128, 128])
    b_s = C.din("a_b_s", [1, 1024])
    w_out = C.din("a_w_out", [4096, 2048])
    kv_w = C.din("kv_w", [2048, 4096])
    if "fused" in io:
        ln_g, ln_b = io["ln_g"][0:1, :], io["ln_b"][0:1, :]
        x1_o = io["x1_scratch"]
    else:
        ln_g = C.din("ln_g", [1, 2048])
        ln_b = C.din("ln_b", [1, 2048])
        x1_o = C.dout("x1", [1024, 2048])
        kT_o = C.dout("kT", [16, 128, 1024], BF16)
        v_o = C.dout("v", [1024, 2048], BF16)

    xT = C.view(0, 32 * K, BF16, "p (k t) -> p k t", k=16)
    uT = [C.view(32 * K + i * 2 * K, 2 * K, BF16) for i in range(8)]
    sgT = [C.view(48 * K + i * 2 * K, 2 * K, BF16) for i in range(8)]
    z = C.view(0, 64 * K, F32, "p (a d) -> p a d", a=8)
    slab = C.view(64 * K, 64 * K, BF16, "p (f t c) -> p f t c", f=32, t=8)
    x1T = C.view(64 * K, 32 * K, BF16, "p (k t) -> p k t", k=16)
    lng = C.view(96 * K, 8 * K, F32)
    lnb = C.view(104 * K, 8 * K, F32)
    ost = [C.view(112 * K + i * 8 * K, 8 * K, F32) for i in range(2)]
    wsl = [C.view(128 * K + i * 16 * K, 16 * K, BF16) for i in range(3)]
    xs = C.view(176 * K, 8 * K, F32)
    tmp = [C.view(176 * K + i * 2 * K, 2 * K, F32) for i in range(4)]
    kst = [C.view(176 * K + i * 2 * K, 2 * K, BF16) for i in range(4)]

    Bxs = Buf("xs")
    Btmp = [Buf(f"tmp{i}") for i in range(4)]
    Bkst = [Buf(f"kst{i}") for i in range(4)]
    BxT = [Buf(f"xT{t}") for t in range(8)]
    Bu = [Buf(f"u{i}") for i in range(8)]
    Bsg = [Buf(f"sg{i}") for i in range(8)]
    Bz = [Buf(f"z{t}") for t in range(8)]
    Bslab = [[Buf(f"slab{f}_{h}") for h in range(2)] for f in range(32)]
    Bx1T = [Buf(f"x1T{t}") for t in range(8)]
    Bln = Buf("ln")
    Bost = [Buf("ost0"), Buf("ost1")]
    Bw = [Buf(f"w{i}") for i in range(3)]
    tiles = ([(w_in[:, 4096 + cb * 512:4096 + (cb + 1) * 512], 16) for cb in range(8)]
             + [t for g in range(8) for t in ((w_in[:, g * 512:(g + 1) * 512], 16),
                                              (w_in[:, 8192 + g * 512:8192 + (g + 1) * 512], 16))]
             + [(w_out[:, cb * 256:(cb + 1) * 256], 32) for cb in range(8)]
             + [(kv_w[:, ct * 512:(ct + 1) * 512], 16) for ct in range(8)])
    tiles = tiles + io.get("extra_tiles", [])
    WS = WStream(C, wsl, Bw, tiles)

    PT = C.psum4()
    banks = [PT[i // 2][:, (i % 2) * 512:(i % 2 + 1) * 512] for i in range(8)]
    Bbanks = [Buf(f"bank{i}") for i in range(8)]
    brot = Rot(8)

    ident, Bident = emit_consts(C)

    brow_u = C.sb([32, 128], F32, "brow_u")
    brow_g = C.sb([32, 128], F32, "brow_g")
    grow = C.sb([32, 128], F32, "grow")
    vbrow = C.sb([32, 128], F32, "vbrow")
    bs8 = C.sb([8, 128], F32, "bs8")
    bcol = C.sb([128, 64], F32, "bcol")
    gcol = C.sb([128, 32], F32, "gcol")
    CG = C.sb([128, 32, 2], F32, "CG")
    RB = C.sb([128, 8, 2], F32, "RB")
    Rab = [C.sb([2, 128], F32, f"Rab{i}") for i in range(4)]
    Rg = [C.sb([2, 128], F32, f"Rg{i}") for i in range(2)]
    Rd = C.sb([1, 640], F32, "Rd")
    Wsf = C.view(160 * K, 4 * K, F32, "p (g s) -> p g s", g=8)
    WsTf = C.view(164 * K, 4 * K, F32, "p (g s) -> p g s", g=8)
    WsT = C.sb([128, 8, 128], BF16, "WsT")
    epsc = C.sb([128, 1], F32, "epsc")
    stats = C.sb([128, 8, 8, 6], F32, "stats")
    mv = C.sb([128, 8, 2], F32, "mv")
    rstd = C.sb([128, 8], F32, "rstd")
    lstats = C.sb([128, 4, 6], F32, "lstats")
    lmv = C.sb([128, 2], F32, "lmv")
    lrstd = C.sb([128, 1], F32, "lrstd")
    lnbias = C.sb([128, 1], F32, "lnbias")
    lsets = [(lstats, lmv, lrstd, lnbias),
             (C.sb([128, 4, 6], F32, "lstats2"), C.sb([128, 2], F32, "lmv2"), C.sb([128, 1], F32, "lrstd2"),
              C.sb([128, 1], F32, "lnbias2"))]
    Blsets = [Buf("lsm0"), Buf("lsm1")]
    Bbrow, Bgrow, Bbcol, Bgcol = Buf("brow"), Buf("grow"), Buf("bcol"), Buf("gcol")
    BCG, BRB, BRab, BRg = Buf("CG"), Buf("RB"), [Buf(f"Rab{i}") for i in range(4)], [Buf("Rg0"), Buf("Rg1")]
    Bones, Bbv = Buf("ones"), Buf("bv")
    BWsf, BWsTf, BWsT, Beps = Bw[2], Bw[2], Buf("WsT"), Buf("eps")
    Bstats = [Buf(f"stats{t}") for t in range(8)]
    Bmv, Brstd, Blsm = Buf("mv"), Buf("rstd"), Buf("lsm")

    P.add("pool", f_memset(epsc[:], EPS), writes=[Beps])
    P.add("pool", f_memset(Rd[0:1, 512:640], 1.0), writes=[Bones])
    b_in96 = b_in.rearrange("o (f p) -> (o f) p", p=128)
    P.add("sp", f_dma(brow_u[:], b_in96[0:32, :]), writes=[Bbrow], dma=uch())
    P.add("sp", f_dma(brow_g[:], b_in96[64:96, :]), writes=[Bbrow], dma=uch())
    P.add("sp", f_dma(grow[:], vln_g.rearrange("o (f p) -> (o f) p", p=128)), writes=[Bgrow], dma=uch())
    P.add("sp", f_dma(vbrow[:], vln_b.rearrange("o (f p) -> (o f) p", p=128)), writes=[Bgrow], dma=uch())
    P.add("sp", f_dma(bs8[:], b_s.rearrange("o (g t) -> (o g) t", t=128)), writes=[Bgrow], dma=uch())
    P.add("sp", f_dma(Wsf[:], w_s.rearrange("g t s -> t g s")), writes=[BWsf], dma=uch())
    gate('c1')
    for (src_, nrow, dst_, Bsrc, Bd) in ((brow_u, 32, bcol[:, 0:32], Bbrow, Bbcol), (brow_g, 32, bcol[:, 32:64], Bbrow, Bbcol),
                                       (grow, 32, gcol[:, :], Bgrow, Bgcol), (vbrow, 32, CG[:, :, 0], Bgrow, BCG),
                                       (bs8, 8, RB[:, :, 1], Bgrow, BRB)):
        bi = brot()
        P.add("pe", f_tr(banks[bi][:, 0:nrow], src_[:, :], ident[0:nrow, 0:nrow]), reads=[Bsrc, Bident], writes=[Bbanks[bi]])
        P.add("dve", f_copy(dst_, banks[bi][:, 0:nrow]), reads=[Bbanks[bi]], writes=[Bd])
    P.add("dve", lambda e: e.reciprocal(out=CG[:, :, 1], in_=gcol[:, :]), reads=[Bgcol], writes=[BCG])
    P.add("dve", f_tt(CG[:, :, 0], CG[:, :, 0], CG[:, :, 1], ALU.mult), reads=[BCG], writes=[BCG])
    gate('c2')
    for g in range(8):
        P.add("pool", f_asel(Wsf[:, g, :], Wsf[:, g, :], ALU.is_ge, 0.0, [[-1, 128]], 1), reads=[BWsf], writes=[BWsf])
    P.add("dve", lambda e: e.tensor_reduce(out=RB[:, :, 0], in_=Wsf[:, :, :], axis=mybir.AxisListType.X, op=ALU.add),
          reads=[BWsf], writes=[BRB])
    gate('c3')
    for g in range(8):
        bi = brot()
        P.add("pe", f_tr(banks[bi][:, 0:128], Wsf[:, g, :], ident[:]), reads=[BWsf, Bident], writes=[Bbanks[bi]])
        P.add("dve", f_copy(WsT[:, g, :], banks[bi][:, 0:128]), reads=[Bbanks[bi]], writes=[BWsT])
    gate('c4')

    gate('consts')
    xs_b = C.view(32 * K, 8 * K, F32)
    Bxs_b = Buf("xs_b")
    for tb in range(8):
        xb, Bxb, ch = (xs, Bxs, "xs") if tb % 2 == 0 else (xs_b, Bxs_b, "xsb")
        P.add("sp", f_dma(xb[:], x[tb * 128:(tb + 1) * 128, :]), writes=[Bxb], dma=ch)
        emit_transpose_block(C, xb, Bxb, xT, BxT[tb], tb, ident, Bident, banks, Bbanks, brot)
    alias(Bu[0:4], [Bxs_b])

    gate('A0')
    alias(Btmp, [Bxs])
    it = 0
    for cb in range(8):
        wt, Bwt = WS.use(cb)
        P.add("sp", f_dma(Rd[0:1, 0:512], b_in[:, 4096 + cb * 512:4096 + (cb + 1) * 512]), writes=[Bbv], dma="bv")
        for tb in range(8):
            bi = brot()
            bank = banks[bi]
            for kc in range(16):
                P.add("pe", f_mm(bank[:, :], xT[:, kc, tb * 128:(tb + 1) * 128], wt[:, kc, :], kc == 0, False),
                      reads=[BxT[tb], Bwt], war=[Bbanks[bi]] if kc == 0 else [])
            P.add("pe", f_mm(bank[:, :], Rd[0:1, 512:640], Rd[0:1, 0:512], False, True),
                  reads=[Bones, Bbv], writes=[Bbanks[bi]])
            ti = it % 4
            it += 1
            P.add("act", f_act(tmp[ti], bank[:, :], AF.Gelu_apprx_tanh), reads=[Bbanks[bi]], writes=[Btmp[ti]])
            P.add("dve", lambda e, tb=tb, cb=cb, ti=ti: e.bn_stats(out=stats[:, tb, cb, :], in_=tmp[ti]),
                  reads=[Btmp[ti]], writes=[Bstats[tb]])
            P.add("pool", f_copy(slab[:, 4 * cb:4 * cb + 4, tb, :], tmp[ti].rearrange("p (a b) -> p a b", a=4)),
                  reads=[Btmp[ti]], writes=[Bslab[4 * cb + a][tb // 4] for a in range(4)])

    gate('A1')
    for tb in range(8):
        P.add("dve", lambda e, tb=tb: e.bn_aggr(out=mv[:, tb, :], in_=stats[:, tb, :, :].rearrange("p a b -> p (a b)")),
              reads=[Bstats[tb]], writes=[Bmv])
    gate('a2a')
    P.add("act", f_act(rstd[:, :], mv[:, :, 1], AF.Sqrt, bias=epsc[:], scale=1.0), reads=[Bmv, Beps], writes=[Brstd])
    P.add("dve", lambda e: e.reciprocal(out=rstd[:, :], in_=rstd[:, :]), reads=[Brstd], writes=[Brstd])
    gate('a2b')
    for tb in range(8):
        P.add("dve", f_ts(slab[:, :, tb, :], slab[:, :, tb, :], mv[:, tb, 0:1], rstd[:, tb:tb + 1], ALU.subtract, ALU.mult),
              reads=[Bmv, Brstd] + [Bslab[f][tb // 4] for f in range(32)],
              writes=[Bslab[f][tb // 4] for f in range(32)])

    gate('A2')
    for g in range(8):
        rg = Rg[g % 2]
        bi = brot()
        P.add("pe", f_tr(banks[bi][0:2, 0:128], RB[:, g, :], ident[:, :]), reads=[BRB, Bident], writes=[Bbanks[bi]])
        P.add("dve", f_copy(rg[:, :], banks[bi][0:2, 0:128]), reads=[Bbanks[bi]], writes=[BRg[g % 2]])
        for path in range(2):
            wt_, Bwt = WS.use(8 + 2 * g + path)
            func = AF.Gelu_apprx_tanh if path == 0 else AF.Silu
            boff = 0 if path == 0 else 32
            for fcl in range(4):
                fc = g * 4 + fcl
                s8 = fc % 8
                dstbuf, Bdst = (uT[s8], Bu[s8]) if path == 0 else (sgT[s8], Bsg[s8])
                for th in range(2):
                    bi = brot()
                    bank = banks[bi]
                    for kc in range(16):
                        P.add("pe", f_mm(bank[:, :], wt_[:, kc, fcl * 128:(fcl + 1) * 128],
                                         xT[:, kc, th * 512:(th + 1) * 512], kc == 0, kc == 15),
                              reads=[Bwt] + BxT[th * 4:(th + 1) * 4],
                              war=[Bbanks[bi]] if kc == 0 else [], writes=[Bbanks[bi]] if kc == 15 else [])
                    P.add("act", f_act(dstbuf[:, th * 512:(th + 1) * 512], bank[:, :], func,
                                       bias=bcol[:, boff + fc:boff + fc + 1]),
                          reads=[Bbanks[bi], Bbcol], writes=[Bdst])
        for fcl in range(4):
            fc = g * 4 + fcl
            s8 = fc % 8
            ra = Rab[fc % 4]
            bi = brot()
            P.add("pe", f_tr(banks[bi][0:2, 0:128], CG[:, fc, :], ident[:, :]), reads=[BCG, Bident], writes=[Bbanks[bi]])
            P.add("dve", f_copy(ra[:, :], banks[bi][0:2, 0:128]), reads=[Bbanks[bi]], writes=[BRab[fc % 4]])
            P.add("pool", f_tt(uT[s8], uT[s8], sgT[s8], ALU.mult),
                  reads=[Bu[s8], Bsg[s8]], writes=[Bu[s8]])
            for half in range(2):
                bi = brot()
                bank = banks[bi]
                for tbl in range(4):
                    tb = half * 4 + tbl
                    cols = bank[:, tbl * 128:(tbl + 1) * 128]
                    P.add("pe", f_mm(cols, slab[:, fc, tb, :], WsT[:, g, :], True, False),
                          reads=[Bslab[fc][half], BWsT], war=[Bbanks[bi]] if tbl == 0 else [])
                    P.add("pe", f_mm(cols, ra[:, :], rg[:, :], False, True),
                          reads=[BRab[fc % 4], BRg[g % 2]], writes=[Bbanks[bi]] if tbl == 3 else [])
                sview = slab[:, fc, half * 4:(half + 1) * 4, :].rearrange("p a b -> p (a b)")
                P.add("dve", f_stt(sview, bank[:, :], gcol[:, fc:fc + 1], uT[s8][:, half * 512:(half + 1) * 512],
                                   ALU.mult, ALU.mult),
                      reads=[Bbanks[bi], Bu[s8], Bgcol], writes=[Bslab[fc][half]])

    gate('A3')
    alias(Bz, BxT + Bu + Bsg)
    for cb in range(8):
        wo, Bwo = WS.use(24 + cb)
        for tb in range(8):
            bi = brot()
            bank = banks[bi]
            half = tb // 4
            for kc in range(32):
                P.add("pe", f_mm(bank[:, 0:256], slab[:, kc, tb, :], wo[:, kc, :], kc == 0, kc == 31),
                      reads=[Bslab[kc][half], Bwo],
                      war=[Bbanks[bi]] if kc == 0 else [], writes=[Bbanks[bi]] if kc == 31 else [])
            dst = z[:, tb, cb * 256:(cb + 1) * 256]
            if (cb * 8 + tb) % 2 == 0:
                P.add("act", f_act(dst, bank[:, 0:256], AF.Copy), reads=[Bbanks[bi]], writes=[Bz[tb]])
            else:
                P.add("dve", f_copy(dst, bank[:, 0:256]), reads=[Bbanks[bi]], writes=[Bz[tb]])

    gate('A4')
    allslab = [b for fb in Bslab for b in fb]
    alias([Bln] + Bost + Bx1T, allslab)
    alias([Bxs], Btmp)
    P.add("sp", f_dma(lng, ln_g.partition_broadcast(128)), writes=[Bln], dma="c1")
    P.add("sp", f_dma(lnb, ln_b.partition_broadcast(128)), writes=[Bln], dma="c1")
    outs = []
    for tb in range(8):
        o = ost[tb % 2]
        od = emit_ln_block(C, z[:, tb, :], Bz[tb], xs, Bxs, x, tb, lng, lnb, Bln, o, Bost[tb % 2],
                           lsets[tb % 2], Blsets[tb % 2], epsc, Beps, f"o{tb % 2}", x1_o)
        outs.append(od)
        if "fused" in io:
            io["Bx1d"][tb].writers = [od]
        if tb >= 1:
            emit_transpose_block(C, ost[(tb - 1) % 2], Bost[(tb - 1) % 2], x1T, Bx1T[tb - 1], tb - 1, ident, Bident,
                                 banks, Bbanks, brot)
    emit_transpose_block(C, ost[7 % 2], Bost[7 % 2], x1T, Bx1T[7], 7, ident, Bident, banks, Bbanks, brot)

    gate('A5')
    alias(Bkst, [Bxs])
    ki = 0
    for ct in range(8):
        wt, Bwt = WS.use(32 + ct)
        if ct < 4:
            for hh in range(4):
                h = ct * 4 + hh
                si = ki % 4
                ki += 1
                for th in range(2):
                    bi = brot()
                    bank = banks[bi]
                    for kc in range(16):
                        P.add("pe", f_mm(bank[:, :], wt[:, kc, hh * 128:(hh + 1) * 128], x1T[:, kc, th * 512:(th + 1) * 512],
                                         kc == 0, kc == 15),
                              reads=[Bwt] + Bx1T[th * 4:(th + 1) * 4],
                              war=[Bbanks[bi]] if kc == 0 else [], writes=[Bbanks[bi]] if kc == 15 else [])
                    if th == 0:
                        P.add("act", f_act(kst[si][:, 0:512], bank[:, :], AF.Copy), reads=[Bbanks[bi]], writes=[Bkst[si]])
                    else:
                        P.add("dve", f_copy(kst[si][:, 512:1024], bank[:, :]), reads=[Bbanks[bi]], writes=[Bkst[si]])
                if "fused" in io:
                    kd = P.add("sp", f_dma(io["kloc"][ct][hh * 128:(hh + 1) * 128, :], kst[si]), reads=[Bkst[si]], dma=f"k{si}")
                    io["Bkloc"][ct].writers.append(kd)
                else:
                    outs.append(P.add("sp", f_dma(kT_o[h], kst[si]), reads=[Bkst[si]], dma=f"k{si}"))
        else:
            for tbp in range(4):
                si = ki % 4
                ki += 1
                for t2 in range(2):
                    tb = tbp * 2 + t2
                    bi = brot()
                    bank = banks[bi]
                    for kc in range(16):
                        P.add("pe", f_mm(bank[:, :], x1T[:, kc, tb * 128:(tb + 1) * 128], wt[:, kc, :], kc == 0, kc == 15),
                              reads=[Bwt, Bx1T[tb]],
                              war=[Bbanks[bi]] if kc == 0 else [], writes=[Bbanks[bi]] if kc == 15 else [])
                    if t2 == 0:
                        P.add("act", f_act(kst[si][:, 0:512], bank[:, :], AF.Copy), reads=[Bbanks[bi]], writes=[Bkst[si]])
                    else:
                        P.add("dve", f_copy(kst[si][:, 512:1024], bank[:, :]), reads=[Bbanks[bi]], writes=[Bkst[si]])
                c0 = (ct - 4) * 512
                if "fused" in io:
                    dst = io["vloc"][ct - 4][tbp * 256:(tbp + 1) * 256, :].rearrange("(a p) c -> p a c", p=128)
                    vd = P.add("sp", f_dma(dst, kst[si].rearrange("p (a c) -> p a c", a=2)), reads=[Bkst[si]], dma=f"k{si}")
                    io["Bvloc"][ct - 4].writers.append(vd)
                else:
                    dst = v_o[tbp * 256:(tbp + 1) * 256, c0:c0 + 512].rearrange("(a p) c -> p a c", p=128)
                    outs.append(P.add("sp", f_dma(dst, kst[si].rearrange("p (a c) -> p a c", a=2)), reads=[Bkst[si]], dma=f"k{si}"))
        if "fused" in io and ct in (0, 4):
            io["gather"](ct)
    if "fused" not in io:
        P.add("sp", None, after=outs)
    else:
        io.update(A_Bz=Bz, A_Bx1T=Bx1T, A_x1T=x1T, A_Bln=Bln, A_Bost=Bost, A_Bkst=Bkst, A_Bxs=Bxs,
                  banks=banks, Bbanks=Bbanks, brot=brot, WS=WS)


def blocks_of(r):
    out = []
    for m in range(4):
        out += [8 * m + r, 8 * m + 7 - r]
    return out


def shard_tokens(x):
    res = []
    for c in range(8):
        b, r = divmod(c, 4)
        xb = x[b].reshape(32, 128, -1)
        res.append(np.ascontiguousarray(xb[blocks_of(r)].reshape(1024, -1)))
    return res


def common_A(inputs):
    return {
        "a_w_in": np.ascontiguousarray(inputs["a_w_in"][0]),
        "a_b_in": np.ascontiguousarray(inputs["a_b_in"][0].reshape(1, 12288)),
        "a_vln_g": np.ascontiguousarray(inputs["a_vln_g"][0].reshape(1, 4096)),
        "a_vln_b": np.ascontiguousarray(inputs["a_vln_b"][0].reshape(1, 4096)),
        "a_w_s": np.ascontiguousarray(inputs["a_w_s"][0]),
        "a_b_s": np.ascontiguousarray(inputs["a_b_s"][0].reshape(1, 1024)),
        "a_w_out": np.ascontiguousarray(inputs["a_w_out"][0]),
        "kv_w": np.ascontiguousarray(inputs["kv_w"]),
        "ln_g": np.ascontiguousarray(inputs["ln_g"][0].reshape(1, 2048)),
        "ln_b": np.ascontiguousarray(inputs["ln_b"][0].reshape(1, 2048)),
    }


def run_A(inputs, upto=None):
    nc = build_A(upto)
    xs = shard_tokens(np.asarray(inputs["x"], dtype=np.float32))
    common = common_A(inputs)
    in_maps = [dict(common, x=xs[c]) for c in range(8)]
    res = run_bass_kernel_spmd(nc, in_maps, core_ids=list(range(8)))
    return res.results


def build_B(upto=None):
    C = Ctx()
    try:
        _build_B_body(C, upto)
    except StopBuild:
        pass
    C.P.emit(C.nc, C.st)
    C.st.close()
    return C.nc


def _build_B_body(C, upto):
    def gate(name):
        if upto == name:
            raise StopBuild()

    nc, P = C.nc, C.P
    io = C.io
    fused = "fused" in io
    masks_d = C.din("masks", [128, 32 * 128], BF16)
    if fused:
        w_in, w_out = io["b_w_in"], io["b_w_out"]
    else:
        w_in = C.din("b_w_in", [2048, 4096])
        w_out = C.din("b_w_out", [2048, 2048])
    out_o = C.dout("out", [1024, 2048])
    if fused:
        x1 = io["x1_scratch"]
        ln_g, ln_b = io["ln_g"][1:2, :], io["ln_b"][1:2, :]
    else:
        x1 = C.din("x1", [1024, 2048])
        kTf = C.din("kTf", [16, 128, 4096], BF16)
        vf = C.din("vf", [4096, 2048], BF16)
        ln_g = C.din("ln_g", [1, 2048])
        ln_b = C.din("ln_b", [1, 2048])

    def pos(kb):
        if not fused:
            return kb
        m, o = divmod(kb, 8)
        return (o * 8 + 2 * m) if o < 4 else ((7 - o) * 8 + 2 * m + 1)

    if fused:
        oX, oQ, oZ, oSG, oKV = 64 * K, 32 * K, 32 * K, 0, 96 * K
    else:
        oX, oQ, oZ, oSG, oKV = 0, 32 * K, 0, 64 * K, 144 * K
    x1T = C.view(oX, 32 * K, BF16, "p (k t) -> p k t", k=16)
    ebuf = [C.view(oX + i * 4 * K, 4 * K, F32) for i in range(2)]
    spb = [C.view(oX + 8 * K + i * 2 * K, 2 * K, BF16) for i in range(2)]
    Sb = C.view(oX + 12 * K, 2 * K, BF16)
    wb = [C.view(oX + 14 * K + i * 2 * K, 2 * K, BF16) for i in range(2)]
    qT = C.view(oQ, 32 * K, BF16, "p (h t) -> p h t", h=16)
    z = C.view(oZ, 64 * K, F32, "p (a d) -> p a d", a=8)
    sgT = C.view(oSG, 32 * K, BF16, "p (h t) -> p h t", h=16)
    wsl = [C.view(96 * K + i * 16 * K, 16 * K, BF16) for i in range(3)]
    KTs = [C.view(oKV + i * 8 * K, 8 * K, BF16) for i in range(2)]
    Vs = [C.view(oKV + 16 * K + i * 8 * K, 8 * K, BF16, "p (b d) -> p b d", b=32) for i in range(2)]
    lng = C.view(oKV, 8 * K, F32)
    lnb = C.view(oKV + 8 * K, 8 * K, F32)
    ost = [C.view(oKV + 16 * K + i * 8 * K, 8 * K, F32) for i in range(2)]
    xs = C.view(176 * K, 8 * K, F32)

    Bxs = Buf("xs")
    Bx1T = [Buf(f"x1T{t}") for t in range(8)]
    Bq = [Buf(f"q{h}") for h in range(16)]
    Bsg = [Buf(f"sg{h}") for h in range(16)]
    Bz = [Buf(f"z{t}") for t in range(8)]
    Bw = [Buf(f"w{i}") for i in range(3)]
    BKT = [Buf("KT0"), Buf("KT1")]
    BV = [Buf("V0"), Buf("V1")]
    Be = [Buf("e0"), Buf("e1")]
    Bsp = [Buf("sp0"), Buf("sp1")]
    BS = Buf("S")
    Bwb = [Buf("wb0"), Buf("wb1")]
    Bln = Buf("ln")
    Bost = [Buf("ost0"), Buf("ost1")]

    PT = C.psum4()
    if fused:
        banks, Bbanks, brot = io["banks"], io["Bbanks"], io["brot"]
        Bx1T = io["A_Bx1T"]
        alias(Bsg + Bq, io["A_Bz"])
        alias(BKT + BV, [io["A_Bln"]] + io["A_Bost"])
        alias([Bxs], io["A_Bkst"] + [io["A_Bxs"]])
    else:
        banks = [PT[i // 2][:, (i % 2) * 512:(i % 2 + 1) * 512] for i in range(8)]
        Bbanks = [Buf(f"bank{i}") for i in range(8)]
        brot = Rot(8)

    ident, Bident = emit_consts(C)
    masks = C.sb([128, 32, 128], BF16, "masks_sb")
    negU = C.sb([128, 128], BF16, "negU")
    negO = C.sb([128, 128], BF16, "negO")
    epsc = C.sb([128, 1], F32, "epsc")
    lstats = C.sb([128, 4, 6], F32, "lstats")
    lmv = C.sb([128, 2], F32, "lmv")
    lrstd = C.sb([128, 1], F32, "lrstd")
    lnbias = C.sb([128, 1], F32, "lnbias")
    lsets = [(lstats, lmv, lrstd, lnbias),
             (C.sb([128, 4, 6], F32, "lstats2"), C.sb([128, 2], F32, "lmv2"), C.sb([128, 1], F32, "lrstd2"),
              C.sb([128, 1], F32, "lnbias2"))]
    Blsets = [Buf("lsm0"), Buf("lsm1")]
    Bmasks, BnegU, BnegO, Beps, Blsm = Buf("masks"), Buf("negU"), Buf("negO"), Buf("eps"), Buf("lsm")
    P.add("sp", f_dma(masks[:].rearrange("p a b -> p (a b)"), masks_d), writes=[Bmasks], dma=uch())
    P.add("pool", f_memset(epsc[:], EPS), writes=[Beps])
    P.add("pool", f_memset(negO[:], -1.0), writes=[BnegO])
    P.add("pool", f_memset(negU[:], -1.0), writes=[BnegU])
    P.add("pool", f_asel(negU[:], negU[:], ALU.is_ge, 0.0, [[-1, 128]], 1), reads=[BnegU], writes=[BnegU])

    if fused:
        WS, wbase = io["WS"], 40
    else:
        tiles = ([(w_in[:, ct * 512:(ct + 1) * 512], 16) for ct in range(8)]
                 + [(w_out[:, cb * 512:(cb + 1) * 512], 16) for cb in range(4)])
        WS, wbase = WStream(C, wsl, Bw, tiles), 0
        emit_load_T(C, x1, xs, Bxs, "xs", x1T, Bx1T, ident, Bident, banks, Bbanks, brot)
    gate('B0')

    for ct in range(8):
        wt, Bwt = WS.use(wbase + ct)
        if fused and ct >= 2:
            io["gather"]([0, 4, 1, 5, 2, 6, 3, 7][ct])
        for hh in range(4):
            h = (ct % 4) * 4 + hh
            for th in range(2):
                bi = brot()
                bank = banks[bi]
                for kc in range(16):
                    P.add("pe", f_mm(bank, wt[:, kc, hh * 128:(hh + 1) * 128], x1T[:, kc, th * 512:(th + 1) * 512],
                                     kc == 0, kc == 15),
                          reads=[Bwt] + Bx1T[th * 4:(th + 1) * 4],
                          war=[Bbanks[bi]] if kc == 0 else [], writes=[Bbanks[bi]] if kc == 15 else [])
                if ct < 4:
                    P.add("act", f_act(qT[:, h, th * 512:(th + 1) * 512], bank, AF.Copy, scale=128.0 ** -0.5),
                          reads=[Bbanks[bi]], writes=[Bq[h]])
                else:
                    P.add("act", f_act(sgT[:, h, th * 512:(th + 1) * 512], bank, AF.Silu),
                          reads=[Bbanks[bi]], writes=[Bsg[h]])
    gate('B1')

    Z = PT[0:3]
    BZ = [Buf("Z0"), Buf("Z1"), Buf("Z2")]
    OT = PT[3]
    BOT = Buf("OT")
    alias(BZ + [BOT], Bbanks)
    alias(Be + Bsp + [BS] + Bwb, Bx1T)
    its = [(h, kb) for h in range(16) for kb in range(31, -1, -1)]
    n = len(its)

    def segs(c0, c1):
        out = []
        if c0 < 512:
            out.append((c0, min(c1, 512)))
        if c1 > 512:
            out.append((max(c0, 512), c1))
        return [(a, b) for a, b in out if b > a]

    def load_head(h):
        hs = h % 2
        if fused:
            ct, hh = divmod(h, 4)
            ksrc = io["kall"][ct].rearrange("(r q) t -> q r t", q=512)[hh * 128:(hh + 1) * 128, :, :]
            P.add("sp", f_dma(KTs[hs].rearrange("p (r t) -> p r t", r=4), ksrc), reads=[io["Bkall"][ct]],
                  writes=[BKT[hs]], dma=f"kt{hs}")
            src = io["vall"][ct].rearrange("(b p) c -> p b c", p=128)[:, :, hh * 128:(hh + 1) * 128]
            rd = [io["Bvall"][ct]]
        else:
            P.add("sp", f_dma(KTs[hs], kTf[h]), writes=[BKT[hs]], dma=f"kt{hs}")
            src = vf[:, h * 128:(h + 1) * 128].rearrange("(b p) d -> p b d", p=128)
            rd = []
        for q4 in range(4):
            P.add("sp", f_dma(Vs[hs][:, q4 * 8:(q4 + 1) * 8, :], src[:, q4 * 8:(q4 + 1) * 8, :]), reads=rd,
                  writes=[BV[hs]], dma=f"v{hs}")

    def QK(i):
        h, kb = its[i]
        if kb == 31 and h == 0:
            load_head(0)
            load_head(1)
        c0 = (kb // 4) * 128
        s = i % 3
        sg_ = segs(c0, 1024)
        for k, (a, b) in enumerate(sg_):
            P.add("pe", f_mm(Z[s][:, a:b], KTs[h % 2][:, pos(kb) * 128:(pos(kb) + 1) * 128], qT[:, h, a:b], True, False),
                  reads=[BKT[h % 2], Bq[h]], war=[BZ[s]] if k == 0 else [], writes=[BZ[s]] if k == len(sg_) - 1 else [])

    def E(i):
        h, kb = its[i]
        c0 = (kb // 4) * 128
        P.add("act", f_act(ebuf[i % 2][:, c0:1024], Z[i % 3][:, c0:1024], AF.Exp), reads=[BZ[i % 3]], writes=[Be[i % 2]])

    def L(i):
        h, kb = its[i]
        j0, mi = kb // 4, kb % 4
        c0 = j0 * 128
        P.add("act", f_act(spb[i % 2][:, c0:1024], ebuf[i % 2][:, c0:1024], AF.Ln, bias=1.0), reads=[Be[i % 2]], writes=[Bsp[i % 2]])
        P.add("dve", f_tt(spb[i % 2][:, c0:c0 + 128], spb[i % 2][:, c0:c0 + 128], masks[:, j0 * 4 + mi, :], ALU.mult),
              reads=[Bsp[i % 2], Bmasks], writes=[Bsp[i % 2]])

    def US(i):
        h, kb = its[i]
        j0, mi = kb // 4, kb % 4
        c0 = j0 * 128
        s = i % 3
        ops = [(negU, BnegU, Bsp[i % 2], spb[i % 2], a, b) for a, b in segs(c0, 1024)]
        cS = c0 + 128 if mi == 3 else c0
        ops += [(negO, BnegO, BS, Sb, a, b) for a, b in segs(cS, 1024)]
        for k, (lh, Blh, Brh, rh, a, b) in enumerate(ops):
            P.add("pe", f_mm(Z[s][:, a:b], lh[:, :], rh[:, a:b], False, True),
                  reads=[Blh, Brh], war=[BZ[s]] if k == 0 else [], writes=[BZ[s]] if k == len(ops) - 1 else [])
        if mi == 3:
            P.add("dve", f_copy(Sb[:, c0:c0 + 128], spb[i % 2][:, c0:c0 + 128]), reads=[Bsp[i % 2]], writes=[BS])
        if cS < 1024:
            P.add("dve", f_tt(Sb[:, cS:1024], Sb[:, cS:1024], spb[i % 2][:, cS:1024], ALU.add), reads=[Bsp[i % 2], BS], writes=[BS])

    def W(i):
        h, kb = its[i]
        j0, mi = kb // 4, kb % 4
        c0 = j0 * 128
        P.add("act", f_act(wb[i % 2][:, c0:1024], Z[i % 3][:, c0:1024], AF.Exp), reads=[BZ[i % 3]], writes=[Bwb[i % 2]])
        P.add("dve", f_tt(wb[i % 2][:, c0:c0 + 128], wb[i % 2][:, c0:c0 + 128], masks[:, j0 * 4 + mi, :], ALU.mult),
              reads=[Bwb[i % 2], Bmasks], writes=[Bwb[i % 2]])

    def PV(i):
        h, kb = its[i]
        c0 = (kb // 4) * 128
        if kb == 31:
            if 1 <= h < 15:
                load_head(h + 1)
            P.add("dve", f_memset(OT[:, :], 0.0), writes=[BOT])
        sg_ = segs(c0, 1024)
        for k, (a, b) in enumerate(sg_):
            P.add("pe", f_mm(OT[:, a:b], Vs[h % 2][:, pos(kb), :], wb[i % 2][:, a:b], False, True),
                  reads=[BV[h % 2], Bwb[i % 2]], war=[BOT] if k == 0 else [], writes=[BOT] if k == len(sg_) - 1 else [])
        if kb == 0:
            for hf in range(2):
                P.add("dve", f_tt(sgT[:, h, hf * 512:(hf + 1) * 512], OT[:, hf * 512:(hf + 1) * 512],
                                  sgT[:, h, hf * 512:(hf + 1) * 512], ALU.mult),
                      reads=[BOT, Bsg[h]], writes=[Bsg[h], BOT])

    nit = n
    QK(0)
    E(0)
    L(0)
    for i in range(nit):
        if i + 1 < nit:
            QK(i + 1)
        if i - 1 >= 0:
            W(i - 1)
        if i + 1 < nit:
            E(i + 1)
            L(i + 1)
        US(i)
        if i - 1 >= 0:
            PV(i - 1)
    W(nit - 1)
    PV(nit - 1)
    gate('B2')

    alias(Bz, Bq + Be + Bsp + [BS] + Bwb + Bx1T)
    alias(Bbanks, BZ + [BOT])
    for cb in range(4):
        wo, Bwo = WS.use(wbase + 8 + cb)
        for tb in range(8):
            bi = brot()
            bank = banks[bi]
            for kc in range(16):
                P.add("pe", f_mm(bank, sgT[:, kc, tb * 128:(tb + 1) * 128], wo[:, kc, :], kc == 0, kc == 15),
                      reads=[Bsg[kc], Bwo], war=[Bbanks[bi]] if kc == 0 else [], writes=[Bbanks[bi]] if kc == 15 else [])
            dst = z[:, tb, cb * 512:(cb + 1) * 512]
            if (cb * 8 + tb) % 2 == 0:
                P.add("act", f_act(dst, bank, AF.Copy), reads=[Bbanks[bi]], writes=[Bz[tb]])
            else:
                P.add("dve", f_copy(dst, bank), reads=[Bbanks[bi]], writes=[Bz[tb]])
    gate('B4')

    alias([Bln] + Bost, BKT + BV)
    P.add("sp", f_dma(lng, ln_g.partition_broadcast(128)), writes=[Bln], dma="c1")
    P.add("sp", f_dma(lnb, ln_b.partition_broadcast(128)), writes=[Bln], dma="c1")
    outs = []
    for tb in range(8):
        o = ost[tb % 2]
        outs.append(emit_ln_block(C, z[:, tb, :], Bz[tb], xs, Bxs, x1, tb, lng, lnb, Bln, o, Bost[tb % 2],
                                  lsets[tb % 2], Blsets[tb % 2], epsc, Beps, f"o{tb % 2}", out_o))
    P.add("sp", None, after=outs)


def make_masks(r):
    qb = blocks_of(r)
    m = np.zeros((128, 32, 128), np.float32)
    tri = (np.arange(128)[:, None] < np.arange(128)[None, :]).astype(np.float32)
    for j in range(8):
        for mi in range(4):
            kb = 4 * j + mi
            if kb < qb[j]:
                m[:, j * 4 + mi, :] = 1.0
            elif kb == qb[j]:
                m[:, j * 4 + mi, :] = tri
    return np.ascontiguousarray(m.reshape(128, 32 * 128)).astype(ml_dtypes.bfloat16)


def assemble_kv(resA):
    kTf = [np.zeros((16, 128, 4096), ml_dtypes.bfloat16) for _ in range(2)]
    vf = [np.zeros((4096, 2048), ml_dtypes.bfloat16) for _ in range(2)]
    for c in range(8):
        b, r = divmod(c, 4)
        for j, qb in enumerate(blocks_of(r)):
            kTf[b][:, :, qb * 128:(qb + 1) * 128] = resA[c]["kT"][:, :, j * 128:(j + 1) * 128]
            vf[b][qb * 128:(qb + 1) * 128, :] = resA[c]["v"][j * 128:(j + 1) * 128, :]
    return kTf, vf


def common_B(inputs):
    return {
        "b_w_in": np.ascontiguousarray(inputs["b_w_in"][0]),
        "b_w_out": np.ascontiguousarray(inputs["b_w_out"][0]),
        "ln_g": np.ascontiguousarray(inputs["ln_g"][1].reshape(1, 2048)),
        "ln_b": np.ascontiguousarray(inputs["ln_b"][1].reshape(1, 2048)),
    }


def unshard_tokens(outs):
    res = np.zeros((2, 32, 128, 2048), np.float32)
    for c in range(8):
        b, r = divmod(c, 4)
        res[b, blocks_of(r)] = outs[c].reshape(8, 128, 2048)
    return res.reshape(2, 4096, 2048)


def build_fused():
    C = Ctx()
    nc, P = C.nc, C.P
    io = C.io
    io["fused"] = True
    io["ln_g"] = C.din("ln_g", [2, 2048])
    io["ln_b"] = C.din("ln_b", [2, 2048])
    io["x1_scratch"] = nc.dram_tensor("x1_scratch", [1024, 2048], F32).ap()
    kloc = [nc.dram_tensor(f"kloc{i}", [512, 1024], BF16) for i in range(4)]
    vloc = [nc.dram_tensor(f"vloc{i}", [1024, 512], BF16) for i in range(4)]
    kall = [nc.dram_tensor(f"kall{i}", [2048, 1024], BF16) for i in range(4)]
    vall = [nc.dram_tensor(f"vall{i}", [4096, 512], BF16) for i in range(4)]
    io["kloc"] = [t.ap() for t in kloc]
    io["vloc"] = [t.ap() for t in vloc]
    io["kall"] = [t.ap() for t in kall]
    io["vall"] = [t.ap() for t in vall]
    io["Bkloc"] = [Buf(f"kloc{i}") for i in range(4)]
    io["Bvloc"] = [Buf(f"vloc{i}") for i in range(4)]
    io["Bkall"] = [Buf(f"kall{i}") for i in range(4)]
    io["Bvall"] = [Buf(f"vall{i}") for i in range(4)]
    io["Bx1d"] = [Buf(f"x1d{t}") for t in range(8)]
    groups = [[0, 1, 2, 3], [4, 5, 6, 7]]

    def gather(ct):
        if ct < 4:
            src, dst, Bs, Bd = kloc[ct], kall[ct], io["Bkloc"][ct], io["Bkall"][ct]
        else:
            src, dst, Bs, Bd = vloc[ct - 4], vall[ct - 4], io["Bvloc"][ct - 4], io["Bvall"][ct - 4]
        P.add("pool", lambda e: e.collective_compute("AllGather", ALU.bypass, replica_groups=groups,
                                                     ins=[src.ap().opt()], outs=[dst.ap().opt()]),
              reads=[Bs], writes=[Bd], dma=f"cc{ct}", step=1)

    io["gather"] = gather
    io["b_w_in"] = C.din("b_w_in", [2048, 4096])
    io["b_w_out"] = C.din("b_w_out", [2048, 2048])
    io["extra_tiles"] = ([(io["b_w_in"][:, ct * 512:(ct + 1) * 512], 16) for ct in range(8)]
                         + [(io["b_w_out"][:, cb * 512:(cb + 1) * 512], 16) for cb in range(4)])
    C.prefix = "A_"
    _build_A_body(C, None)
    io["Bx1d_rd"] = {tb: [io["Bx1d"][tb]] for tb in range(8)}
    C.prefix = "B_"
    _build_B_body(C, None)
    P.emit(nc, C.st)
    C.st.close()
    return nc


def kernel(x, a_w_in, a_b_in, a_vln_g, a_vln_b, a_w_s, a_b_s, a_w_out, kv_w, b_w_in, b_w_out, ln_g, ln_b):
    inputs = dict(x=np.asarray(x), a_w_in=np.asarray(a_w_in), a_b_in=np.asarray(a_b_in), a_vln_g=np.asarray(a_vln_g),
                  a_vln_b=np.asarray(a_vln_b), a_w_s=np.asarray(a_w_s), a_b_s=np.asarray(a_b_s),
                  a_w_out=np.asarray(a_w_out), kv_w=np.asarray(kv_w), b_w_in=np.asarray(b_w_in),
                  b_w_out=np.asarray(b_w_out), ln_g=np.asarray(ln_g), ln_b=np.asarray(ln_b))
    nc = build_fused()
    in_maps = fused_in_maps(inputs)
    res = run_bass_kernel_spmd(nc, in_maps, core_ids=list(range(8))).results
    return unshard_tokens([res[c]["out"] for c in range(8)])


def fused_in_maps(inputs):
    xs = shard_tokens(np.asarray(inputs["x"], dtype=np.float32))
    common = common_A(inputs)
    common.update(common_B(inputs))
    common["ln_g"] = np.ascontiguousarray(inputs["ln_g"], dtype=np.float32)
    common["ln_b"] = np.ascontiguousarray(inputs["ln_b"], dtype=np.float32)
    return [dict(common, x=xs[c], masks=make_masks(c % 4)) for c in range(8)]
```
